# Optimizing a Trainium2 kernel written in Bass

```python
import math
import jax, jax.numpy as jnp
from jax import lax
import numpy as np

D_MODEL = 1024
BATCH = 16
SEQ = 4096
DEPTH = 4

CHUNK = 64
N_MIXERS = 4
Q_BLOCK = 128
EPS = 1e-6
GMLP_CHUNK = 128
GMLP_WIDTH = 1024
GMLP_GROUPS = 8
GMLP_GROUP_DIM = GMLP_WIDTH // GMLP_GROUPS
DIFF_HEADS = 8
DIFF_HEAD_DIM = 64
DIFF_V_DIM = 2 * DIFF_HEAD_DIM
FOX_HEADS = 16
FOX_HEAD_DIM = 64
RNN_WIDTH = 1280
RNN_BLOCKS = 16
RNN_BLOCK_DIM = RNN_WIDTH // RNN_BLOCKS
RNN_CONV = 4
RGLRU_C = 8.0
D_FF = 2816
FFN_CONV = 3
REL_BUCKETS = 32
REL_MAX_DIST = 128
N_A = (DEPTH - 0 + N_MIXERS - 1) // N_MIXERS
N_B = (DEPTH - 1 + N_MIXERS - 1) // N_MIXERS
N_C = (DEPTH - 2 + N_MIXERS - 1) // N_MIXERS
N_D = (DEPTH - 3 + N_MIXERS - 1) // N_MIXERS

kernel_name = "hybrid_chunk_causal_interleaved_trunk"


def rmsnorm(x, g):
    xf = x.astype(jnp.float32)
    y = xf * lax.rsqrt(jnp.mean(xf * xf, axis=-1, keepdims=True) + EPS)
    return (y * g.astype(jnp.float32)).astype(x.dtype)


def layernorm(x, g, b):
    xf = x.astype(jnp.float32)
    mu = jnp.mean(xf, axis=-1, keepdims=True)
    var = jnp.mean(jnp.square(xf - mu), axis=-1, keepdims=True)
    y = (xf - mu) * lax.rsqrt(var + EPS)
    return (y * g.astype(jnp.float32) + b.astype(jnp.float32)).astype(x.dtype)


def causal_dwconv(x, w, b):
    K, C = w.shape
    y = lax.conv_general_dilated(x, w[:, None, :].astype(x.dtype), window_strides=(1,),
                                 padding=[(K - 1, 0)], dimension_numbers=('NWC', 'WIO', 'NWC'),
                                 feature_group_count=C)
    return y + b.astype(x.dtype)


def chunk_mask(qpos, kpos):
    return (kpos[None, :] // CHUNK) <= (qpos[:, None] // CHUNK)


def t5_bucket(rel):
    half = REL_BUCKETS // 2
    max_exact = half // 2
    n = jnp.abs(rel)
    ret = jnp.where(rel > 0, half, 0)
    nf = jnp.maximum(n, 1).astype(jnp.float32)
    large = max_exact + (jnp.log(nf / max_exact) / math.log(REL_MAX_DIST / max_exact)
                         * (half - max_exact)).astype(jnp.int32)
    large = jnp.minimum(large, half - 1)
    return ret + jnp.where(n < max_exact, n, large)


def gmlp_mixer(x, w_in, ln_g, ln_b, w_s, b_s, w_out):
    B, S, _ = x.shape
    z = jax.nn.gelu(x @ w_in)
    u, v = z[..., :GMLP_WIDTH], z[..., GMLP_WIDTH:]
    v = layernorm(v, ln_g, ln_b)
    v = v.reshape(B, S // GMLP_CHUNK, GMLP_CHUNK, GMLP_GROUPS, GMLP_GROUP_DIM)
    p = jnp.arange(GMLP_CHUNK)
    mask = (p[None, :] // CHUNK) <= (p[:, None] // CHUNK)
    ws = jnp.where(mask[None], w_s, jnp.zeros((), w_s.dtype))
    v = jnp.einsum('gpq,bnqgc->bnpgc', ws, v) + b_s.T[None, None, :, :, None]
    return (u * v.reshape(B, S, GMLP_WIDTH)) @ w_out


def diff_attention(x, w_in, lam, sub_g, w_out, rel_bias, lambda_init):
    B, S, _ = x.shape
    H, d = DIFF_HEADS, DIFF_HEAD_DIM
    hq = H * 2 * d
    qkv = x @ w_in
    q = qkv[..., :hq].reshape(B, S, H, 2, d).transpose(0, 2, 1, 3, 4)
    k = qkv[..., hq:2 * hq].reshape(B, S, H, 2, d).transpose(0, 2, 1, 3, 4)
    v = qkv[..., 2 * hq:].reshape(B, S, H, DIFF_V_DIM).transpose(0, 2, 1, 3)
    lamf = lam.astype(jnp.float32)
    lam_full = jnp.exp(jnp.sum(lamf[0] * lamf[1])) - jnp.exp(jnp.sum(lamf[2] * lamf[3])) + lambda_init
    scale = d ** -0.5
    outs = []
    for q0 in range(0, S, Q_BLOCK):
        end = q0 + Q_BLOCK
        qpos = q0 + jnp.arange(Q_BLOCK)
        kpos = jnp.arange(end)
        bias = rel_bias[t5_bucket(kpos[None, :] - qpos[:, None])].astype(jnp.float32).transpose(2, 0, 1)
        s = jnp.einsum('bhqid,bhkid->ibhqk', q[:, :, q0:end], k[:, :, :end]).astype(jnp.float32) * scale + bias
        s = jnp.where(chunk_mask(qpos, kpos), s, -jnp.inf)
        p = jax.nn.softmax(s, axis=-1)
        attn = (p[0] - lam_full * p[1]).astype(v.dtype)
        outs.append(jnp.einsum('bhqk,bhkd->bhqd', attn, v[:, :, :end]))
    o = jnp.concatenate(outs, axis=2)
    o = rmsnorm(o, sub_g) * (1.0 - lambda_init)
    return o.transpose(0, 2, 1, 3).reshape(B, S, H * DIFF_V_DIM) @ w_out


def forgetting_attention(x, w_in, b_f, w_out):
    B, S, _ = x.shape
    H, d = FOX_HEADS, FOX_HEAD_DIM
    hd = H * d
    proj = x @ w_in
    q = proj[..., :hd].reshape(B, S, H, d).transpose(0, 2, 1, 3)
    k = proj[..., hd:2 * hd].reshape(B, S, H, d).transpose(0, 2, 1, 3)
    v = proj[..., 2 * hd:3 * hd].reshape(B, S, H, d).transpose(0, 2, 1, 3)
    f_logit = proj[..., 3 * hd:].astype(jnp.float32) + b_f.astype(jnp.float32)
    cum = jnp.cumsum(jax.nn.log_sigmoid(f_logit), axis=1).transpose(0, 2, 1)
    scale = d ** -0.5
    outs = []
    for q0 in range(0, S, Q_BLOCK):
        end = q0 + Q_BLOCK
        qpos = q0 + jnp.arange(Q_BLOCK)
        kpos = jnp.arange(end)
        decay = cum[:, :, q0:end, None] - cum[:, :, None, :end]
        s = jnp.einsum('bhqd,bhkd->bhqk', q[:, :, q0:end], k[:, :, :end]).astype(jnp.float32) * scale + decay
        s = jnp.where(kpos[None, :] <= qpos[:, None], s, -jnp.inf)
        p = jax.nn.softmax(s, axis=-1).astype(v.dtype)
        outs.append(jnp.einsum('bhqk,bhkd->bhqd', p, v[:, :, :end]))
    o = jnp.concatenate(outs, axis=2)
    return o.transpose(0, 2, 1, 3).reshape(B, S, hd) @ w_out


def _linear_recurrence_combine(left, right):
    a1, b1 = left
    a2, b2 = right
    return (a1 * a2, a2 * b1 + b2)


def rglru_block(x, w_in, conv_w, conv_b, w_r, b_r, w_i, b_i, lam, w_out):
    B, S, _ = x.shape
    z = x @ w_in
    gate, xr = z[..., :RNN_WIDTH], z[..., RNN_WIDTH:]
    xr = causal_dwconv(xr, conv_w, conv_b).astype(jnp.float32)
    xb = xr.reshape(B, S, RNN_BLOCKS, RNN_BLOCK_DIM)
    r = jax.nn.sigmoid(jnp.einsum('bsnc,ncd->bsnd', xb, w_r.astype(jnp.float32)).reshape(B, S, RNN_WIDTH)
                       + b_r.astype(jnp.float32))
    i = jax.nn.sigmoid(jnp.einsum('bsnc,ncd->bsnd', xb, w_i.astype(jnp.float32)).reshape(B, S, RNN_WIDTH)
                       + b_i.astype(jnp.float32))
    log_a = -RGLRU_C * r * jax.nn.softplus(-lam.astype(jnp.float32))
    a = jnp.exp(log_a)
    u = jnp.sqrt(-jnp.expm1(2.0 * log_a)) * (i * xr)
    _, h = lax.associative_scan(_linear_recurrence_combine, (a, u), axis=1)
    y = h.astype(x.dtype) * jax.nn.gelu(gate)
    return y @ w_out


def conv_ffn(x, w_up, conv_w, conv_b, w_down):
    h = causal_dwconv(x @ w_up, conv_w, conv_b)
    g, u = h[..., :D_FF], h[..., D_FF:]
    return (jax.nn.gelu(g) * u) @ w_down


def setup_inputs(seed: int = 0) -> dict:
    key = jax.random.key(seed)
    ks = jax.random.split(key, 32)
    f32 = jnp.float32

    def nrm(i, shape, scale):
        return scale * jax.random.normal(ks[i], shape, f32)

    D = D_MODEL
    a8 = jax.random.uniform(ks[28], (N_D, RNN_WIDTH), f32, 0.9, 0.999)
    a_base = a8 ** (1.0 / RGLRU_C)
    return {
        "x": nrm(0, (BATCH, SEQ, D), 1.0),
        "norm_g": 1.0 + nrm(1, (DEPTH, 4, D), 0.1),
        "ffn_w_up": nrm(2, (DEPTH, D, 2 * D_FF), D ** -0.5),
        "ffn_conv_w": nrm(3, (DEPTH, FFN_CONV, 2 * D_FF), FFN_CONV ** -0.5),
        "ffn_conv_b": nrm(4, (DEPTH, 2 * D_FF), 0.01),
        "ffn_w_down": nrm(5, (DEPTH, D_FF, D), D_FF ** -0.5),
        "rel_bias": nrm(6, (REL_BUCKETS, DIFF_HEADS), 0.5),
        "a_w_in": nrm(7, (N_A, D, 2 * GMLP_WIDTH), D ** -0.5),
        "a_ln_g": 1.0 + nrm(8, (N_A, GMLP_WIDTH), 0.1),
        "a_ln_b": nrm(9, (N_A, GMLP_WIDTH), 0.05),
        "a_w_s": nrm(10, (N_A, GMLP_GROUPS, GMLP_CHUNK, GMLP_CHUNK), GMLP_CHUNK ** -0.5),
        "a_b_s": 1.0 + nrm(11, (N_A, GMLP_GROUPS, GMLP_CHUNK), 0.1),
        "a_w_out": nrm(12, (N_A, GMLP_WIDTH, D), GMLP_WIDTH ** -0.5),
        "b_w_in": nrm(13, (N_B, D, 3 * DIFF_HEADS * DIFF_V_DIM), D ** -0.5),
        "b_lam": nrm(14, (N_B, 4, DIFF_HEAD_DIM), 0.1),
        "b_sub_g": 1.0 + nrm(15, (N_B, DIFF_V_DIM), 0.1),
        "b_w_out": nrm(16, (N_B, DIFF_HEADS * DIFF_V_DIM, D), (DIFF_HEADS * DIFF_V_DIM) ** -0.5),
        "c_w_in": nrm(17, (N_C, D, 3 * FOX_HEADS * FOX_HEAD_DIM + FOX_HEADS), D ** -0.5),
        "c_b_f": jax.random.uniform(ks[18], (N_C, FOX_HEADS), f32, 3.0, 6.0),
        "c_w_out": nrm(19, (N_C, FOX_HEADS * FOX_HEAD_DIM, D), (FOX_HEADS * FOX_HEAD_DIM) ** -0.5),
        "d_w_in": nrm(20, (N_D, D, 2 * RNN_WIDTH), D ** -0.5),
        "d_conv_w": nrm(21, (N_D, RNN_CONV, RNN_WIDTH), RNN_CONV ** -0.5),
        "d_conv_b": nrm(22, (N_D, RNN_WIDTH), 0.01),
        "d_w_r": nrm(23, (N_D, RNN_BLOCKS, RNN_BLOCK_DIM, RNN_BLOCK_DIM), RNN_BLOCK_DIM ** -0.5),
        "d_b_r": nrm(24, (N_D, RNN_WIDTH), 0.05),
        "d_w_i": nrm(25, (N_D, RNN_BLOCKS, RNN_BLOCK_DIM, RNN_BLOCK_DIM), RNN_BLOCK_DIM ** -0.5),
        "d_b_i": nrm(26, (N_D, RNN_WIDTH), 0.05),
        "d_lam": jnp.log(a_base) - jnp.log1p(-a_base),
        "d_w_out": nrm(27, (N_D, RNN_WIDTH, D), RNN_WIDTH ** -0.5),
    }


def reference(x, norm_g, ffn_w_up, ffn_conv_w, ffn_conv_b, ffn_w_down, rel_bias,
              a_w_in, a_ln_g, a_ln_b, a_w_s, a_b_s, a_w_out,
              b_w_in, b_lam, b_sub_g, b_w_out,
              c_w_in, c_b_f, c_w_out,
              d_w_in, d_conv_w, d_conv_b, d_w_r, d_b_r, d_w_i, d_b_i, d_lam, d_w_out):
    h = x
    for layer in range(DEPTH):
        m = layer % N_MIXERS
        j = layer // N_MIXERS
        y = rmsnorm(h, norm_g[layer, 0])
        if m == 0:
            y = gmlp_mixer(y, a_w_in[j], a_ln_g[j], a_ln_b[j], a_w_s[j], a_b_s[j], a_w_out[j])
        elif m == 1:
            lambda_init = 0.8 - 0.6 * math.exp(-0.3 * layer)
            y = diff_attention(y, b_w_in[j], b_lam[j], b_sub_g[j], b_w_out[j], rel_bias, lambda_init)
        elif m == 2:
            y = forgetting_attention(y, c_w_in[j], c_b_f[j], c_w_out[j])
        else:
            y = rglru_block(y, d_w_in[j], d_conv_w[j], d_conv_b[j], d_w_r[j], d_b_r[j],
                            d_w_i[j], d_b_i[j], d_lam[j], d_w_out[j])
        h = h + rmsnorm(y, norm_g[layer, 1])
        y = rmsnorm(h, norm_g[layer, 2])
        y = conv_ffn(y, ffn_w_up[layer], ffn_conv_w[layer], ffn_conv_b[layer], ffn_w_down[layer])
        h = h + rmsnorm(y, norm_g[layer, 3])
    return h
```

```python
import contextlib
import math
import os
KCUT = int(os.environ.get('KCUT', '99'))
KDUMP = int(os.environ.get('KDUMP', '0'))
import numpy as np
import concourse.bass as bass
import concourse.mybir as mybir
from concourse.bass_utils import run_bass_kernel_spmd

F32 = mybir.dt.float32
BF16 = mybir.dt.bfloat16
AF = mybir.ActivationFunctionType
ALU = mybir.AluOpType

NCORES = 8
D = 1024
S = 4096
NSEQ = 2
TC = S * NSEQ
EPS = 1e-6
DFF = 2816
RW = 1280
SEM_CAP = 6000
LAMBDA_INIT = 0.8 - 0.6 * math.exp(-0.3 * 1)
TZC = 384
GVN = 1152


class T:
    __slots__ = ("w", "r", "prev", "const", "slot")

    def __init__(self, const=False):
        self.slot = 0
        self.w = []
        self.r = []
        self.prev = []
        self.const = const


class Prog:
    ENGS = ("pe", "act", "dve", "pool", "sp")

    def __init__(self, nc):
        self.nc = nc
        self.ins = {e: [] for e in self.ENGS}
        self.dma_cnt = {}
        self.last_real = {}
        self.stack = contextlib.ExitStack()

    def newgen(self, t):
        t.prev = t.w + t.r
        t.w = []
        t.r = []

    def op(self, eng, fn, reads=(), writes=(), pw=(), dma=None):
        me_idx = len(self.ins[eng])
        if dma:
            prod = ("dma:" + dma, self.dma_cnt.get(dma, 0))
            self.dma_cnt[dma] = prod[1] + 1
        else:
            prod = (eng, me_idx)
        deps = set()
        raw = set()
        for t in reads:
            deps.update(t.w)
            raw.update(t.w)
        for t in writes:
            deps.update(t.w)
            deps.update(t.r)
            deps.update(t.prev)
        for t in pw:
            deps.update(t.prev)
        pruned = []
        for d in deps:
            if d[0] == eng and not dma:
                if eng == "pe" or d not in raw:
                    continue
            pruned.append(d)
        self.ins[eng].append([fn, pruned, False, dma])
        if not dma:
            self.last_real[eng] = me_idx
        for t in writes:
            t.w = [prod]
            t.r = []
            t.prev = []
        for t in pw:
            t.w.append(prod)
        for t in reads:
            if not t.const and prod not in t.w:
                t.r.append(prod)
        return prod

    def fence_dma(self, key):
        c = self.dma_cnt.get(key, 0)
        if c:
            for e in self.ENGS:
                self.ins[e].append([None, [("dma:" + key, c - 1)], False, None])

    def barrier(self):
        lasts = [(e, i) for e, i in self.last_real.items()]
        dmas = [("dma:" + k, c - 1) for k, c in self.dma_cnt.items() if c > 0]
        for e in self.ENGS:
            deps = [d for d in lasts if d[0] != e] + dmas
            self.ins[e].append([None, deps, False, None])

    def emit(self, final_waits=()):
        nc = self.nc
        for e in self.ENGS:
            for ins in self.ins[e]:
                for d in ins[1]:
                    if not d[0].startswith("dma:"):
                        self.ins[d[0]][d[1]][2] = True
        signum = {}
        for e in self.ENGS:
            c = 0
            for i, ins in enumerate(self.ins[e]):
                if ins[2]:
                    signum[(e, i)] = c
                    c += 1
        sems = {}

        def getsem(key):
            if key not in sems:
                sems[key] = self.stack.enter_context(nc.semaphore(key.replace(":", "_")))
            return sems[key]

        dcap = SEM_CAP // 16

        def wait_target(d):
            if d[0].startswith("dma:"):
                j = d[1]
                return (getsem(f"{d[0]}_{j // dcap}"), 16 * (j % dcap + 1))
            j = signum[d]
            return (getsem(f"e_{d[0]}_{j // SEM_CAP}"), j % SEM_CAP + 1)

        for e in self.ENGS:
            dcount = {}
            for i, ins in enumerate(self.ins[e]):
                for d in ins[1]:
                    wait_target(d)
                if ins[2]:
                    wait_target((e, i))
                if ins[3]:
                    k = ins[3]
                    wait_target(("dma:" + k, dcount.get(k, 0)))
                    dcount[k] = dcount.get(k, 0) + 1
        for d in final_waits:
            wait_target(d)
        prog = self

        def run_engine(ename, eng, extra_waits=()):
            seen = {}
            dcount = {}
            for i, (fn, deps, sig, dma) in enumerate(prog.ins[ename]):
                wl = {}
                for d in deps:
                    s, v = wait_target(d)
                    if seen.get(s.name, 0) >= v:
                        continue
                    if wl.get(s.name, (None, 0))[1] < v:
                        wl[s.name] = (s, v)
                for s, v in wl.values():
                    eng.wait_ge(s, v)
                    seen[s.name] = v
                if fn is None:
                    continue
                instr = fn(eng)
                if dma:
                    j = dcount.get(dma, 0)
                    dcount[dma] = j + 1
                    s, v = wait_target(("dma:" + dma, j))
                    instr.then_inc(s, 16)
                elif sig:
                    s, v = wait_target((ename, i))
                    instr.then_inc(s, 1)
            for d in extra_waits:
                s, v = wait_target(d)
                eng.wait_ge(s, v)

        with nc.Block() as block:
            @block.tensor
            def _(eng):
                run_engine("pe", eng)

            @block.scalar
            def _(eng):
                run_engine("act", eng)

            @block.vector
            def _(eng):
                run_engine("dve", eng)

            @block.gpsimd
            def _(eng):
                run_engine("pool", eng, extra_waits=final_waits)

            @block.sync
            def _(eng):
                run_engine("sp", eng)
        self.stack.close()


class Ring:
    def __init__(self, items):
        self.items = items
        self.i = 0
        for k, it in enumerate(items):
            if isinstance(it, tuple) and isinstance(it[1], T):
                it[1].slot = k

    def next(self):
        it = self.items[self.i % len(self.items)]
        self.i += 1
        return it


def seq_tiles(n_full, width):
    out = []
    t = 0
    while t < S:
        n = min(width, S - t)
        out.append((t, n))
        t += n
    return out


def build(nlayers=4, nsub=None):
    nc = bass.Bass("TRN2", target_bir_lowering=False)
    P = Prog(nc)
    ins = {}

    def din(name, shape, dt=F32):
        ins[name] = nc.dram_tensor(name, list(shape), dt, kind="ExternalInput")
        return ins[name]

    xT = din("xT", [D, TC])
    ng_d = din("ng", [128, 128])
    NLD = max(nlayers, 1)
    wup_d = [din(f"wup{l}", [128, 8, 2 * DFF]) for l in range(NLD)]
    wdn_d = [din(f"wdn{l}", [128, 22, D]) for l in range(NLD)]
    fcw_d = [din(f"fcw{l}", [128, 44, 4]) for l in range(NLD)]
    a_win_d = din("a_win", [128, 8, 2048]); a_lng_d = din("a_lng", [128, 1024]); a_lnb_d = din("a_lnb", [128, 1024])
    a_wsT_d = din("a_wsT", [128, 8, 128]); a_mask_d = din("a_mask", [128, 128]); a_bs_d = din("a_bs", [128, 8, 512])
    a_wout_d = din("a_wout", [128, 8, D])
    if nlayers > 1:
      b_win_d = din("b_win", [128, 8, 3072]); b_lam_d = din("b_lam", [1, 256]); b_subg_d = din("b_subg", [128, 1])
      b_wout_d = din("b_wout", [128, 8, D]); relb_d = din("relb", [32, 8]); ohrev_d = din("ohrev", [32, GVN]); jflip_d = din("jflip", [128, 128])
    if nlayers > 2:
      c_win_d = din("c_win", [128, 8, 3088]); c_bf_d = din("c_bf", [16, 1]); c_wout_d = din("c_wout", [128, 8, D]); tri_d = din("tri", [128, 128])
    if nlayers > 3:
      d_win_d = din("d_win", [128, 8, 2 * RW]); d_cw_d = din("d_cw", [128, 10, 5]); d_wr_d = din("d_wr", [128, 10, RW]); d_wi_d = din("d_wi", [128, 10, RW])
      d_br_d = din("d_br", [128, 10]); d_bi_d = din("d_bi", [128, 10]); d_lam_d = din("d_lam", [128, 10]); d_wout_d = din("d_wout", [128, 10, D])
    outT = nc.dram_tensor("outT", [D, TC], F32, kind="ExternalOutput")

    def dscr(name, shape, dt):
        return nc.dram_tensor(name, list(shape), dt)

    wup_b = [dscr(f"wupb{l}", [128, 8, 2 * DFF], BF16) for l in range(4)]
    wdn_b = [dscr(f"wdnb{l}", [128, 22, D], BF16) for l in range(4)]
    a_win_b = dscr("a_winb", [128, 8, 2048], BF16); a_wout_b = dscr("a_woutb", [128, 8, D], BF16)
    b_win_b = dscr("b_winb", [128, 8, 3072], BF16); b_wout_b = dscr("b_woutb", [128, 8, D], BF16)
    c_win_b = dscr("c_winb", [128, 8, 3088], BF16); c_wout_b = dscr("c_woutb", [128, 8, D], BF16)
    d_win_b = dscr("d_winb", [128, 8, 2 * RW], BF16); d_wout_b = dscr("d_woutb", [128, 10, D], BF16)
    d_wr_b = dscr("d_wrb", [128, 10, RW], BF16); d_wi_b = dscr("d_wib", [128, 10, RW], BF16)
    hbuf = [dscr(f"h{i}", [D, TC], F32) for i in range(7)]
    qk_s = {l: dscr(f"qk{l}", [2048, TC], BF16) for l in (1, 2)}
    v_s = {l: dscr(f"v{l}", [TC, 1024], BF16) for l in (1, 2)}
    ao_s = {l: dscr(f"ao{l}", [D, TC], BF16) for l in (1, 2)}
    cq_s = dscr("cq", [16, 6, TC], BF16); ck_s = dscr("ck", [16, 6, TC], BF16)
    gv_s = dscr("gv", [8, GVN], F32)

    sb = lambda name, shape, dt=F32: P.stack.enter_context(nc.sbuf_tensor(name, list(shape), dt))
    ARENA_N = 52400
    arena = sb("arena", [128, ARENA_N])
    ones_bf = sb("ones_bf", [128, 128], BF16); t_ones = T(const=True)
    ng = sb("ngs", [128, 128]); t_ng = T(const=True)
    PSALL = P.stack.enter_context(nc.psum_tensor("psall", [128, 4096], F32))
    PSB = [PSALL[:, i * 512:(i + 1) * 512] for i in range(8)]
    TPS = [T() for _ in range(8)]

    class Arena:
        def __init__(self):
            self.off = 0

        def reset(self):
            self.off = 0

        def alloc(self, n, dt=F32):
            n32 = n if dt == F32 else (n + 1) // 2
            a = arena[:, self.off:self.off + n32]
            self.off += n32
            assert self.off <= ARENA_N, self.off
            return a if dt == F32 else a.bitcast(BF16)[:, :n]

        def tile(self, shape, dt=F32):
            n = int(np.prod(shape))
            a = self.alloc(n, dt)
            if len(shape) == 2:
                a = a.rearrange("p (a b) -> p a b", a=shape[0])
            elif len(shape) == 3:
                a = a.rearrange("p (a b c) -> p a b c", a=shape[0], b=shape[1])
            return a, T()

    A = Arena()

    def act(out, in_, func, reads, writes=(), pw=(), **kw):
        return P.op("act", lambda e: e.activation(out, in_, func, **kw), reads, writes, pw)

    def mm(out, lhsT, rhs, start, stop, reads, writes):
        return P.op("pe", lambda e: e.matmul(out, lhsT, rhs, start=start, stop=stop), reads, writes)

    def tt(eng, out, in0, in1, op, reads, writes=(), pw=()):
        return P.op(eng, lambda e: e.tensor_tensor(out, in0, in1, op), reads, writes, pw)

    def stt(out, in0, scalar, in1, op0, op1, reads, writes=(), pw=()):
        return P.op("dve", lambda e: e.scalar_tensor_tensor(out, in0, scalar, in1, op0, op1), reads, writes, pw)

    def ts(eng, out, in0, s1, s2, op0, op1, reads, writes=(), pw=()):
        if s2 is None:
            return P.op(eng, lambda e: e.tensor_scalar(out, in0, s1, None, op0), reads, writes, pw)
        return P.op(eng, lambda e: e.tensor_scalar(out, in0, s1, s2, op0, op1), reads, writes, pw)

    def cp(eng, out, in_, reads, writes=(), pw=()):
        return P.op(eng, lambda e: e.tensor_copy(out, in_), reads, writes, pw)

    def recip(out, in_, reads, writes=(), pw=()):
        return P.op("dve", lambda e: e.reciprocal(out, in_), reads, writes, pw)

    def memset(eng, ap, val, writes=(), pw=()):
        return P.op(eng, lambda e: e.memset(ap, val), (), writes, pw)

    def dma(q, out, in_, key, reads=(), writes=(), pw=()):
        return P.op(q, lambda e: e.dma_start(out=out, in_=in_), reads, writes, pw, dma=key)

    dumps = []

    def dump(name, ap, reads):
        if not KDUMP:
            return
        d = nc.dram_tensor("dbg_" + name, list(ap.shape), ap.dtype, kind="ExternalOutput")
        dumps.append(dma("pool", d.ap(), ap, "dbg", reads=reads))

    ps_i = [0]
    ps_pool = [list(range(8))]

    def psnext(excl=None):
        pool = ps_pool[0]
        while True:
            i = pool[ps_i[0] % len(pool)]
            ps_i[0] += 1
            if excl is None or PSB[i] is not excl:
                return PSB[i], TPS[i]

    memset("pool", ones_bf[:], 1.0, writes=[t_ones])
    dma("sp", ng[:], ng_d.ap(), "c", writes=[t_ng])

    t_w = {}

    def cast_weight(src, dst, kc, n, name):
        t = T(const=True)
        t_w[name] = t
        total = kc * n
        sflat = src.ap().rearrange("p a b -> p (a b)")
        dflat = dst.ap().rearrange("p a b -> p (a b)")
        PIECE = 2048
        for o in range(0, total, PIECE):
            m = min(PIECE, total - o)
            (s32, ts32) = stg32.next()
            (s16, ts16) = stg16.next()
            dma("sp", s32[:, :m], sflat[:, o:o + m], f"cw{ts32.slot}", writes=[ts32])
            eng = cast_eng.next()
            if eng == "act":
                act(s16[:, :m], s32[:, :m], AF.Copy, reads=[ts32], writes=[ts16])
            else:
                cp(eng, s16[:, :m], s32[:, :m], reads=[ts32], writes=[ts16])
            dma("pool", dflat[:, o:o + m], s16[:, :m], f"cs{ts16.slot}", reads=[ts16], pw=[t])

    A.reset()
    stg32 = Ring([(A.alloc(2048), T()) for _ in range(4)])
    stg16 = Ring([(A.alloc(2048, BF16), T()) for _ in range(4)])
    cast_eng = Ring(["dve", "act", "pool"])
    cast_list = [(a_win_d, a_win_b, 8, 2048, "a_win"), (a_wout_d, a_wout_b, 8, D, "a_wout"), (wup_d[0], wup_b[0], 8, 2 * DFF, "wup0"), (wdn_d[0], wdn_b[0], 22, D, "wdn0")]
    if nlayers > 1:
        cast_list += [(b_win_d, b_win_b, 8, 3072, "b_win"), (b_wout_d, b_wout_b, 8, D, "b_wout"), (wup_d[1], wup_b[1], 8, 2 * DFF, "wup1"), (wdn_d[1], wdn_b[1], 22, D, "wdn1")]
    if nlayers > 2:
        cast_list += [(c_win_d, c_win_b, 8, 3088, "c_win"), (c_wout_d, c_wout_b, 8, D, "c_wout"), (wup_d[2], wup_b[2], 8, 2 * DFF, "wup2"), (wdn_d[2], wdn_b[2], 22, D, "wdn2")]
    if nlayers > 3:
        cast_list += [(d_win_d, d_win_b, 8, 2 * RW, "d_win"), (d_wout_d, d_wout_b, 10, D, "d_wout"), (d_wr_d, d_wr_b, 10, RW, "d_wr"), (d_wi_d, d_wi_b, 10, RW, "d_wi"),
                      (wup_d[3], wup_b[3], 8, 2 * DFF, "wup3"), (wdn_d[3], wdn_b[3], 22, D, "wdn3")]
    for c in cast_list:
        cast_weight(*c)

    def hview(buf, a, b):
        return buf.ap().rearrange("(c p) t -> p c t", p=128)[:, :, a:b]

    class Ctx:
        pass

    def common_alloc(wmax, kc_back, nh=2, ny=2, nyo=2, full=True):
        c = Ctx()
        if full:
            c.wring = Ring([A.tile([4096], BF16) for _ in range(4)])
            c.hring = Ring([A.tile([8, wmax]) for _ in range(nh)])
            c.yring = Ring([A.tile([8, wmax], BF16) for _ in range(ny)])
            c.yoring = Ring([A.tile([8, wmax]) for _ in range(nyo)])
        c.sqring = Ring([A.tile([wmax], BF16) for _ in range(3)])
        c.rsring = Ring([A.tile([wmax]) for _ in range(2)])
        return c

    def wload(c, src_ap, shape, tw):
        (wt, twt) = c.wring.next()
        a, b = shape
        view = wt[:, :a * b].rearrange("p (a b) -> p a b", a=a)
        dma("sp", view, src_ap, f"w{twt.slot}", reads=[tw], writes=[twt])
        return view, twt

    def load_h(c, src, base, t0, n, halo):
        (hb, th) = c.hring.next()
        w = n + halo
        if halo and t0 == 0:
            P.newgen(th)
            memset("pool", hb[:, :, 0:halo], 0.0, pw=[th])
            dma("sp", hb[:, :, halo:w], hview(src, base, base + n), f"h{th.slot}", reads=[t_h[src.name]], pw=[th])
        else:
            dma("sp", hb[:, :, 0:w], hview(src, base + t0 - halo, base + t0 + n), f"h{th.slot}", reads=[t_h[src.name]], writes=[th])
        return hb, th

    def rstd_from_ps(c, pb, tpb, w, scale):
        (rs, trs) = c.rsring.next()
        if getattr(c, "rstd_exp", False):
            act(rs[:, :w], pb[:, :w], AF.Ln, reads=[tpb], writes=[trs], scale=scale, bias=EPS)
            act(rs[:, :w], rs[:, :w], AF.Exp, reads=[trs], writes=[trs], scale=-0.5)
        else:
            act(rs[:, :w], pb[:, :w], AF.Sqrt, reads=[tpb], writes=[trs], scale=scale, bias=EPS)
            recip(rs[:, :w], rs[:, :w], reads=[trs], writes=[trs])
        return rs, trs

    def norm_front(c, hb, th, w, gcol):
        (y, ty) = c.yring.next()
        P.newgen(ty)
        pb, tpb = psnext()
        for ch in range(8):
            (sq, tsq) = c.sqring.next()
            act(sq[:, :w], hb[:, ch, :w], AF.Square, reads=[th], writes=[tsq])
            mm(pb[:, :w], ones_bf[:], sq[:, :w], ch == 0, ch == 7, reads=[tsq, t_ones], writes=[tpb])
        rs, trs = rstd_from_ps(c, pb, tpb, w, 1.0 / D)
        for ch in range(8):
            stt(y[:, ch, :w], hb[:, ch, :w], ng[:, gcol + ch:gcol + ch + 1], rs[:, :w], ALU.mult, ALU.mult, reads=[th, trs, t_ng], pw=[ty])
        return y, ty

    dbg_ob = [False]

    def out_back(c, src, tsrc, kc, n, wsrc, tw, gcol, hb, th, halo, dst, dbase, defer=False):
        (yo, tyo) = c.yoring.next()
        P.newgen(tyo)
        pn, tpn = psnext()
        prev_sq = None
        for oc in range(8):
            wv, twv = wload(c, wsrc.ap()[:, :, oc * 128:(oc + 1) * 128], (kc, 128), tw)
            pb, tpb = psnext(excl=pn)
            for k in range(kc):
                tk = tsrc[k] if isinstance(tsrc, list) else tsrc
                mm(pb[:, :n], wv[:, k, :], src[:, k, :n], k == 0, k == kc - 1, reads=[twv, tk], writes=[tpb])
            if prev_sq is not None:
                mm(pn[:, :n], ones_bf[:], prev_sq[0][:, :n], prev_sq[2] == 0, False, reads=[prev_sq[1], t_ones], writes=[tpn])
            act(yo[:, oc, :n], pb[:, :n], AF.Copy, reads=[tpb], pw=[tyo])
            (sq, tsq) = c.sqring.next()
            act(sq[:, :n], pb[:, :n], AF.Square, reads=[tpb], writes=[tsq])
            prev_sq = (sq, tsq, oc)
        mm(pn[:, :n], ones_bf[:], prev_sq[0][:, :n], False, True, reads=[prev_sq[1], t_ones], writes=[tpn])
        rs, trs = rstd_from_ps(c, pn, tpn, n, 1.0 / D)
        tyo2 = T()
        tyo3 = T()

        def piece(oc):
            tt("pool" if defer else "dve", yo[:, oc, :n], yo[:, oc, :n], rs[:, :n], ALU.mult, reads=[tyo, trs], pw=[tyo2])
            stt(yo[:, oc, :n], yo[:, oc, :n], ng[:, gcol + oc:gcol + oc + 1], hb[:, oc, halo:halo + n], ALU.mult, ALU.add, reads=[tyo2, th, t_ng], pw=[tyo3])

        def store():
            st = dma("pool", hview(dst, dbase, dbase + n), yo[:, :, :n], f"hs{tyo.slot}", reads=[tyo3], pw=[t_h[dst.name]])
            tyo.r.append(st)
            tyo.w.extend(tyo2.w + tyo3.w)
            final.append(st)
            return st

        if defer:
            return [(lambda oc=oc: piece(oc)) for oc in range(8)] + [store]
        for oc in range(8):
            piece(oc)
        return store()

    t_h = {b.name: T(const=True) for b in hbuf}
    t_h[xT.name] = T(const=True)
    t_h[outT.name] = T(const=True)
    final = []

    def ffn_layer(l, hin, hout):
        P.barrier()
        A.reset()
        c = common_alloc(458, 22, nh=3)
        fcw, tfcw = A.tile([44, 4])
        dma("sp", fcw, fcw_d[l].ap(), "c", writes=[tfcw])
        tfcw.const = True
        P.fence_dma("c")
        gu, tgu = A.tile([22, 456], BF16)
        aring = Ring([A.tile([456]) for _ in range(6)])
        glring = Ring([A.tile([456]) for _ in range(3)])
        tiles = [(s, t0, n) for s in range(NSEQ) for (t0, n) in seq_tiles(None, 456)]
        twu, twd = t_w[f"wup{l}"], t_w[f"wdn{l}"]
        nxt = load_h(c, hin, tiles[0][0] * S, tiles[0][1], tiles[0][2], 2)
        y_nxt = norm_front(c, nxt[0], nxt[1], tiles[0][2] + 2, (l * 4 + 2) * 8)
        tgu_l = [T() for _ in range(22)]
        tail = []
        for ti, (s, t0, n) in enumerate(tiles):
            hb, th = nxt
            y, ty = y_nxt
            if ti + 1 < len(tiles):
                s2, t02, n2 = tiles[ti + 1]
                nxt = load_h(c, hin, s2 * S, t02, n2, 2)
            w = n + 2
            for j in range(22):
                if tail and j >= 1:
                    tail.pop(0)()
                if j == 10 and ti + 1 < len(tiles):
                    assert not tail
                    y_nxt = norm_front(c, nxt[0], nxt[1], tiles[ti + 1][2] + 2, (l * 4 + 2) * 8)
                wv, twv = wload(c, wup_b[l].ap()[:, :, j * 256:(j + 1) * 256], (8, 256), twu)
                res = []
                for half in range(2):
                    cc = j + 22 * half
                    pb, tpb = psnext()
                    for k in range(8):
                        mm(pb[:, :w], wv[:, k, half * 128:(half + 1) * 128], y[:, k, :w], k == 0, k == 7, reads=[twv, ty], writes=[tpb])
                    (a, ta) = aring.next()
                    act(a[:, :n], pb[:, 2:w], AF.Identity, reads=[tpb, tfcw], writes=[ta], scale=fcw[:, cc, 2:3], bias=fcw[:, cc, 3:4])
                    stt(a[:, :n], pb[:, 1:w - 1], fcw[:, cc, 1:2], a[:, :n], ALU.mult, ALU.add, reads=[tpb, ta, tfcw], writes=[ta])
                    stt(a[:, :n], pb[:, 0:w - 2], fcw[:, cc, 0:1], a[:, :n], ALU.mult, ALU.add, reads=[tpb, ta, tfcw], writes=[ta])
                    res.append((a, ta))
                (gl, tgl) = glring.next()
                act(gl[:, :n], res[0][0][:, :n], AF.Gelu_apprx_tanh, reads=[res[0][1]], writes=[tgl])
                tt("pool", gu[:, j, :n], gl[:, :n], res[1][0][:, :n], ALU.mult, reads=[tgl, res[1][1]], writes=[tgu_l[j]])
            tail = out_back(c, gu, tgu_l, 22, n, wdn_b[l], twd, (l * 4 + 3) * 8, hb, th, 2, hout, s * S + t0, defer=True)
        while tail:
            tail.pop(0)()

    def gmlp_layer(hin, hout):
        P.barrier()
        A.reset()
        c = common_alloc(512, 8)
        lng, tlng = A.tile([1024]); lnb, tlnb = A.tile([1024])
        bs, tbs = A.tile([8, 512])
        wsT32, tws32 = A.tile([8, 128]); msk, tmsk = A.tile([128])
        wsT, tws = A.tile([8, 128], BF16)
        dma("sp", lng, a_lng_d.ap(), "c", writes=[tlng]); dma("sp", lnb, a_lnb_d.ap(), "c", writes=[tlnb])
        dma("sp", bs, a_bs_d.ap(), "c", writes=[tbs]); dma("sp", wsT32, a_wsT_d.ap(), "c", writes=[tws32]); dma("sp", msk, a_mask_d.ap(), "c", writes=[tmsk])
        for g in range(8):
            tt("dve", wsT[:, g, :], wsT32[:, g, :], msk, ALU.mult, reads=[tws32, tmsk], pw=[tws])
        for t in (tlng, tlnb, tbs, tws):
            t.const = True
        P.fence_dma("c")
        u_sb, tu = A.tile([8, 512])
        v_sb, tv = A.tile([4, 1024])
        tv_l = [T() for _ in range(4)]
        tguv_l = [T() for _ in range(8)]
        vn, tvn = A.tile([4, 1024], BF16)
        guv, tguv = A.tile([8, 512], BF16)
        st6, tst6 = A.tile([4, 12]); mv, tmv = A.tile([4, 4])
        tiles = [(s, t0) for s in range(NSEQ) for t0 in range(0, S, 512)]
        twin, twout = t_w["a_win"], t_w["a_wout"]
        nxt = load_h(c, hin, 0, 0, 512, 0)
        y_nxt = norm_front(c, nxt[0], nxt[1], 512, (0 * 4 + 0) * 8)
        for ti, (s, t0) in enumerate(tiles):
            hb, th = nxt
            if ti + 1 < len(tiles):
                s2, t02 = tiles[ti + 1]
                nxt = load_h(c, hin, s2 * S, t02, 512, 0)
            def cut(extra):
                st = dma("pool", hview(hout, s * S + t0, s * S + t0 + 512), hb[:, :, :512], "hs", reads=[th] + extra, pw=[t_h[hout.name]])
                final.append(st)
            if KCUT == 0:
                cut([]); continue
            y, ty = y_nxt
            if ti == 0:
                dump("y", y, [ty])
            if KCUT == 1:
                cut([ty]); continue
            P.newgen(tu)
            for oc in range(8):
                wv, twv = wload(c, a_win_b.ap()[:, :, oc * 128:(oc + 1) * 128], (8, 128), twin)
                pb, tpb = psnext()
                for k in range(8):
                    mm(pb[:, :], wv[:, k, :], y[:, k, :], k == 0, k == 7, reads=[twv, ty], writes=[tpb])
                act(u_sb[:, oc, :], pb[:, :], AF.Gelu_apprx_tanh, reads=[tpb], pw=[tu])
            if ti + 1 < len(tiles):
                y_nxt = norm_front(c, nxt[0], nxt[1], 512, (0 * 4 + 0) * 8)
            if KCUT == 2:
                cut([tu]); continue
            wvs = [wload(c, a_win_b.ap()[:, :, 1024 + half * 512:1024 + (half + 1) * 512], (8, 512), twin) for half in range(2)]
            P.newgen(tvn)
            for tcn in range(4):
                tvt = tv_l[tcn]
                P.newgen(tvt)
                for half in range(2):
                    wv, twv = wvs[half]
                    pb, tpb = psnext()
                    for k in range(8):
                        mm(pb[:, :], y[:, k, tcn * 128:(tcn + 1) * 128], wv[:, k, :], k == 0, k == 7, reads=[twv, ty], writes=[tpb])
                    act(v_sb[:, tcn, half * 512:(half + 1) * 512], pb[:, :], AF.Gelu_apprx_tanh, reads=[tpb], pw=[tvt])
                tl = T()
                P.op("dve", lambda e, tcn=tcn: e.bn_stats(st6[:, tcn, 0:6], v_sb[:, tcn, 0:512]), reads=[tvt], writes=[tl])
                P.op("dve", lambda e, tcn=tcn: e.bn_stats(st6[:, tcn, 6:12], v_sb[:, tcn, 512:1024]), reads=[tvt], pw=[tl])
                tm = T()
                P.op("dve", lambda e, tcn=tcn: e.bn_aggr(mv[:, tcn, 0:2], st6[:, tcn, :]), reads=[tl], writes=[tm])
                act(mv[:, tcn, 2:3], mv[:, tcn, 1:2], AF.Sqrt, reads=[tm], writes=[tm], scale=1.0, bias=EPS)
                recip(mv[:, tcn, 3:4], mv[:, tcn, 2:3], reads=[tm], writes=[tm])
                tvv = T()
                ts("dve", v_sb[:, tcn, :], v_sb[:, tcn, :], mv[:, tcn, 0:1], mv[:, tcn, 3:4], ALU.subtract, ALU.mult, reads=[tvt, tm], writes=[tvv])
                tt("dve", v_sb[:, tcn, :], v_sb[:, tcn, :], lng, ALU.mult, reads=[tvv, tlng], writes=[tvv])
                tt("dve", vn[:, tcn, :], v_sb[:, tcn, :], lnb, ALU.add, reads=[tvv, tlnb], pw=[tvn])
                tvt.r.extend(tvv.w + tvv.r)
            if ti == 0:
                dump("vn", vn, [tvn])
            if KCUT == 4:
                cut([tu, tvn]); continue
            P.newgen(tguv)
            for g in range(8):
                pb, tpb = psnext()
                for tcn in range(4):
                    mm(pb[:, tcn * 128:(tcn + 1) * 128], vn[:, tcn, g * 128:(g + 1) * 128], wsT[:, g, :], True, True, reads=[tvn, tws], writes=[tpb])
                (sq, tsq) = c.rsring.next()
                tt("dve", sq[:, :512], pb[:, :], bs[:, g, :], ALU.add, reads=[tpb, tbs], writes=[tsq])
                tt("dve", guv[:, g, :], sq[:, :512], u_sb[:, g, :], ALU.mult, reads=[tsq, tu], writes=[tguv_l[g]])
            st = out_back(c, guv, tguv_l, 8, 512, a_wout_b, twout, (0 * 4 + 1) * 8, hb, th, 0, hout, s * S + t0)

    def attn_layer(l, hin, hout):
        fox = (l == 2)
        P.barrier()
        A.reset()
        c = common_alloc(512, 8)
        win_b = c_win_b if fox else b_win_b
        twin = t_w["c_win" if fox else "b_win"]
        wout_b = c_wout_b if fox else b_wout_b
        twout = t_w["c_wout" if fox else "b_wout"]
        qk, v, ao = qk_s[l], v_s[l], ao_s[l]
        tqk = [T(const=True) for _ in range(NSEQ)]; tvs = [T(const=True) for _ in range(NSEQ)]; tao = [T(const=True) for _ in range(NSEQ)]
        tcqk = [T(const=True) for _ in range(NSEQ)]
        mark = A.off
        qko, tqko = A.tile([16, 512], BF16)
        vt, tvt = A.tile([4, 1024], BF16)
        if fox:
            nbf, tnbf = A.tile([1]); dma("sp", nbf[0:16, :], c_bf_d.ap(), "c", writes=[tnbf])
            ts("dve", nbf[0:16, :], nbf[0:16, :], -1.0, None, ALU.mult, ALU.bypass, reads=[tnbf], writes=[tnbf]); tnbf.const = True
            on16, ton16 = A.tile([512]); memset("pool", on16[0:16, :], 1.0, writes=[ton16]); ton16.const = True
            spb = Ring([A.tile([512]) for _ in range(2)])
            csum = Ring([A.tile([512]) for _ in range(2)])
            r1, tr1 = A.tile([512]); hf, thf = A.tile([512])
            cqt = Ring([A.tile([6, 512], BF16) for _ in range(2)]); ckt = Ring([A.tile([6, 512], BF16) for _ in range(2)])
            prev_cs = None
        tiles = [(s, t0) for s in range(NSEQ) for t0 in range(0, S, 512)]
        P.fence_dma("c")
        nxt = load_h(c, hin, 0, 0, 512, 0)
        y_nxt = norm_front(c, nxt[0], nxt[1], 512, (l * 4 + 0) * 8)
        for ti, (s, t0) in enumerate(tiles):
            hb, th = nxt
            if ti + 1 < len(tiles):
                s2, t02 = tiles[ti + 1]
                nxt = load_h(c, hin, s2 * S, t02, 512, 0)
            g0 = s * S + t0
            y, ty = y_nxt
            P.newgen(tqko)
            for oc in range(16):
                wv, twv = wload(c, win_b.ap()[:, :, oc * 128:(oc + 1) * 128], (8, 128), twin)
                pb, tpb = psnext()
                for k in range(8):
                    mm(pb[:, :], wv[:, k, :], y[:, k, :], k == 0, k == 7, reads=[twv, ty], writes=[tpb])
                if oc % 2 == 0:
                    act(qko[:, oc, :], pb[:, :], AF.Copy, reads=[tpb], pw=[tqko])
                else:
                    cp("dve", qko[:, oc, :], pb[:, :], reads=[tpb], pw=[tqko])
            dma("pool", qk.ap().rearrange("(c p) t -> p c t", p=128)[:, :, g0:g0 + 512], qko, "qsq", reads=[tqko], pw=[tqk[s]])
            y_cur_keep = (y, ty)
            if ti + 1 < len(tiles):
                y_nxt = norm_front(c, nxt[0], nxt[1], 512, (l * 4 + 0) * 8)
            P.newgen(tvt)
            for half in range(2):
                wv, twv = wload(c, win_b.ap()[:, :, 2048 + half * 512:2048 + (half + 1) * 512], (8, 512), twin)
                for tcn in range(4):
                    pb, tpb = psnext()
                    for k in range(8):
                        mm(pb[:, :], y[:, k, tcn * 128:(tcn + 1) * 128], wv[:, k, :], k == 0, k == 7, reads=[twv, ty], writes=[tpb])
                    if tcn % 2 == 0:
                        act(vt[:, tcn, half * 512:(half + 1) * 512], pb[:, :], AF.Copy, reads=[tpb], pw=[tvt])
                    else:
                        cp("dve", vt[:, tcn, half * 512:(half + 1) * 512], pb[:, :], reads=[tpb], pw=[tvt])
            dma("pool", v.ap()[g0:g0 + 512, :].rearrange("(n p) c -> p n c", p=128), vt, "qsv", reads=[tvt], pw=[tvs[s]])
            if fox:
                wv, twv = wload(c, win_b.ap()[:, :, 3072:3088], (8, 16), twin)
                pb, tpb = psnext()
                for k in range(8):
                    mm(pb[0:16, :], wv[:, k, :], y[:, k, :], k == 0, k == 7, reads=[twv, ty], writes=[tpb])
                (sp_, tsp) = spb.next()
                act(sp_[0:16, :], pb[0:16, :], AF.Exp, reads=[tpb, tnbf], writes=[tsp], scale=-1.0, bias=nbf[0:16, 0:1])
                act(sp_[0:16, :], sp_[0:16, :], AF.Ln, reads=[tsp], writes=[tsp], scale=1.0, bias=1.0)
                ts("dve", sp_[0:16, :], sp_[0:16, :], 8.0, None, ALU.mult, ALU.bypass, reads=[tsp], writes=[tsp])
                (cs, tcs) = csum.next()
                init = 0.0 if t0 == 0 else prev_cs[0][0:16, 511:512]
                rd = [tsp, ton16] + ([] if t0 == 0 else [prev_cs[1]])
                P.op("dve", lambda e, cs=cs, sp_=sp_, init=init: e.tensor_tensor_scan(cs[0:16, :], on16[0:16, :], sp_[0:16, :], init, ALU.mult, ALU.add), reads=rd, writes=[tcs])
                prev_cs = (cs, tcs)
                (cq, tcq) = cqt.next(); (ck, tck) = ckt.next()
                P.newgen(tcq); P.newgen(tck)
                cur, tcur = cs, tcs
                for part in range(3):
                    cp("dve", ck[0:16, 3 + part, :], cur[0:16, :], reads=[tcur], pw=[tck])
                    ts("dve", cq[0:16, part, :], ck[0:16, 3 + part, :], -1.0, None, ALU.mult, ALU.bypass, reads=[tck], pw=[tcq])
                    if part < 2:
                        cp("dve", hf[0:16, :], ck[0:16, 3 + part, :], reads=[tck], writes=[thf])
                        tt("dve", r1[0:16, :], cur[0:16, :], hf[0:16, :], ALU.subtract, reads=[tcur, thf], writes=[tr1])
                        cur, tcur = r1, tr1
                    memset("pool", ck[0:16, part, :], 1.0, pw=[tck])
                    memset("pool", cq[0:16, 3 + part, :], 1.0, pw=[tcq])
                dma("pool", cq_s.ap()[:, :, g0:g0 + 512], cq[0:16, :, :], f"qcq{tcq.slot}", reads=[tcq], pw=[tcqk[s]])
                dma("pool", ck_s.ap()[:, :, g0:g0 + 512], ck[0:16, :, :], f"qck{tck.slot}", reads=[tck], pw=[tcqk[s]])
        P.barrier()
        A.reset()
        c = common_alloc(512, 8, full=False)
        ps_pool[0] = [2, 3, 4, 5, 6, 7] if fox else [4, 5, 6, 7]
        acc_i = [0]
        NH = 16 if fox else 8
        qt_r = Ring([A.tile([S], BF16) for _ in range(2)])
        kt_r = Ring([A.tile([S], BF16) for _ in range(2)])
        va_r = Ring([A.tile([32, 128], BF16) for _ in range(2)])
        e_r = Ring([A.tile([512], BF16) for _ in range(6)])
        sb_r = Ring([A.tile([512]) for _ in range(3)])
        o_r = Ring([A.tile([512]) for _ in range(4)])
        aot_r = Ring([A.tile([512], BF16) for _ in range(3)])
        if fox:
            tri, ttri = A.tile([128], BF16)
            tri32, ttri32 = A.tile([128])
            dma("sp", tri32, tri_d.ap(), "c", writes=[ttri32])
            cp("dve", tri, tri32, reads=[ttri32], writes=[ttri]); ttri.const = True
            for (va, tva) in va_r.items:
                memset("pool", va[:, :, 64:128], 1.0, writes=[tva])
        else:
            lam, tlam = A.tile([256]); dma("sp", lam[0:1, :], b_lam_d.ap(), "c", writes=[tlam])
            sc, tsc = A.tile([8])
            tt("dve", lam[0:1, 0:64], lam[0:1, 0:64], lam[0:1, 64:128], ALU.mult, reads=[tlam], writes=[tlam])
            tt("dve", lam[0:1, 128:192], lam[0:1, 128:192], lam[0:1, 192:256], ALU.mult, reads=[tlam], writes=[tlam])
            P.op("dve", lambda e: e.reduce_sum(sc[0:1, 0:1], lam[0:1, 0:64], mybir.AxisListType.X), reads=[tlam], writes=[tsc])
            P.op("dve", lambda e: e.reduce_sum(sc[0:1, 1:2], lam[0:1, 128:192], mybir.AxisListType.X), reads=[tlam, tsc], writes=[tsc])
            act(sc[0:1, 0:2], sc[0:1, 0:2], AF.Exp, reads=[tsc], writes=[tsc])
            tt("dve", sc[0:1, 2:3], sc[0:1, 1:2], sc[0:1, 0:1], ALU.subtract, reads=[tsc], writes=[tsc])
            ts("dve", sc[0:1, 3:4], sc[0:1, 2:3], -LAMBDA_INIT, None, ALU.add, ALU.bypass, reads=[tsc], writes=[tsc])
            on32, ton32 = A.tile([128]); memset("pool", on32[0:1, :], 1.0, writes=[ton32])
            pb, tpb = psnext()
            mm(pb[:, 0:1], on32[0:1, :], sc[0:1, 3:4], True, True, reads=[ton32, tsc], writes=[tpb])
            nlam, tnl = A.tile([1]); cp("dve", nlam, pb[:, 0:1], reads=[tpb], writes=[tnl]); tnl.const = True
            subg, tsg = A.tile([1]); dma("sp", subg, b_subg_d.ap(), "c", writes=[tsg])
            ts("dve", subg, subg, 1.0 - LAMBDA_INIT, None, ALU.mult, ALU.bypass, reads=[tsg], writes=[tsg]); tsg.const = True
            relb, trb = A.tile([8]); dma("sp", relb[0:32, :], relb_d.ap(), "c", writes=[trb])
            oh, toh = A.tile([GVN]); dma("sp", oh[0:32, :], ohrev_d.ap(), "c", writes=[toh])
            gvt, tgvt = A.tile([GVN])
            for o in range(0, GVN, 384):
                pb, tpb = psnext()
                mm(pb[0:8, 0:384], relb[0:32, :], oh[0:32, o:o + 384], True, True, reads=[trb, toh], writes=[tpb])
                cp("dve", gvt[0:8, o:o + 384], pb[0:8, 0:384], reads=[tpb], pw=[tgvt])
            tgv = T(const=True)
            dma("pool", gv_s.ap(), gvt[0:8, :], "gv", reads=[tgvt], pw=[tgv])
            jf, tjf = A.tile([128]); dma("sp", jf, jflip_d.ap(), "c", writes=[tjf])
            P.fence_dma("c"); P.fence_dma("gv")
            tz, ttz = A.tile([8, 1024]); P.newgen(ttz)
            cb, tcb = A.tile([8]); P.newgen(tcb)
            hk, thk = A.tile([1024])
            for h in range(8):
                dma("sp", hk, bass.AP(gv_s, h * GVN, [[1, 128], [1, 1024]]), "c", reads=[tgv], writes=[thk])
                for o in range(2):
                    pb, tpb = psnext()
                    mm(pb[:, :], jf, hk[:, o * 512:(o + 1) * 512], True, True, reads=[tjf, thk], writes=[tpb])
                    cp("dve", tz[:, h, o * 512:(o + 1) * 512], pb[:, :], reads=[tpb], pw=[ttz])
                cp("dve", cb[:, h:h + 1], tz[:, h, 1023:1024], reads=[ttz], pw=[tcb])
            ttz.const = True; tcb.const = True
        P.fence_dma("c")
        AOV = ao.ap()

        P.barrier()
        NM = 8
        steps = []
        for s in range(NSEQ):
            for h in range(NH):
                for m in range(NM):
                    nkt = 4 * m + 4
                    for kk in range(nkt):
                        steps.append((s, h, m, kk, nkt))
        NSTEP = len(steps)
        heads = [(s, h) for s in range(NSEQ) for h in range(NH)]
        hbufs = {}

        def load_head(idx):
            if idx >= len(heads) or idx in hbufs:
                return
            s, h = heads[idx]
            (qt, tqt) = qt_r.next(); (kt, tkt) = kt_r.next(); (va, tva) = va_r.next()
            base = s * S
            if fox:
                P.newgen(tqt); P.newgen(tkt)
                dma("sp", qt[0:64, :], qk.ap()[h * 64:(h + 1) * 64, base:base + S], f"hq{tqt.slot}", reads=[tqk[s]], pw=[tqt])
                dma("sp", kt[0:64, :], qk.ap()[1024 + h * 64:1024 + (h + 1) * 64, base:base + S], f"hk{tkt.slot}", reads=[tqk[s]], pw=[tkt])
                dma("sp", qt[64:70, :], cq_s.ap()[h, :, base:base + S], f"hq{tqt.slot}", reads=[tcqk[s]], pw=[tqt])
                dma("sp", kt[64:70, :], ck_s.ap()[h, :, base:base + S], f"hk{tkt.slot}", reads=[tcqk[s]], pw=[tkt])
                P.newgen(tva)
                dma("sp", va[:, :, 0:64], v.ap()[base:base + S, h * 64:(h + 1) * 64].rearrange("(n p) c -> p n c", p=128), f"hv{tva.slot}", reads=[tvs[s]], pw=[tva])
            else:
                dma("sp", qt, qk.ap()[h * 128:(h + 1) * 128, base:base + S], f"hq{tqt.slot}", reads=[tqk[s]], writes=[tqt])
                dma("sp", kt, qk.ap()[1024 + h * 128:1024 + (h + 1) * 128, base:base + S], f"hk{tkt.slot}", reads=[tqk[s]], writes=[tkt])
                dma("sp", va, v.ap()[base:base + S, h * 128:(h + 1) * 128].rearrange("(n p) c -> p n c", p=128), f"hv{tva.slot}", reads=[tvs[s]], writes=[tva])
            hbufs[idx] = (qt, tqt, kt, tkt, va, tva)

        KR = 70 if fox else 64
        if fox:
            s_ring = Ring([(PSB[i], TPS[i]) for i in (2, 3, 4, 5, 6, 7)])
            acc_ring = Ring([(PSB[0], TPS[0]), (PSB[1], TPS[1])])
        else:
            pairs = [(PSALL[:, (4 + 2 * i) * 512:(6 + 2 * i) * 512].rearrange("p (a b) -> p a b", a=2), T()) for i in range(2)]
            po_pair = (PSALL[:, 0:1024].rearrange("p (a b) -> p a b", a=2), T())
            pl_pair = (PSALL[:, 1024:2048].rearrange("p (a b) -> p a b", a=2), T())
            s_ring = Ring(pairs)
            ones32, ton32b = A.tile([128]); memset("pool", ones32, 1.0, writes=[ton32b]); ton32b.const = True
            e2_r = Ring([A.tile([2, 512], BF16) for _ in range(4)])
            sb2_r = Ring([A.tile([2, 512]) for _ in range(2)])
            osb_r = Ring([A.tile([2, 512]) for _ in range(2)])
            r12_r = Ring([A.tile([2, 512]) for _ in range(2)])
        st_S = {}
        st_E = {}
        cur = {}
        pending = []

        def stage_A(g):
            s, h, m, kk, nkt = steps[g]
            hi = heads.index((s, h))
            if m == 0 and kk == 0:
                load_head(hi)
            qt, tqt, kt, tkt, va, tva = hbufs[hi]
            c0 = max(0, kk - 4 * m) * 128
            ks = slice(kk * 128, (kk + 1) * 128)
            qs = slice(m * 512 + c0, m * 512 + 512)
            (sp_, tsp_) = s_ring.next()
            if fox:
                mm(sp_[:, c0:512], kt[0:KR, ks], qt[0:KR, qs], True, True, reads=[tkt, tqt], writes=[tsp_])
            else:
                for mp in range(2):
                    rr = slice(mp * 64, (mp + 1) * 64)
                    mm(sp_[:, mp, c0:512], kt[rr, ks], qt[rr, qs], True, True, reads=[tkt, tqt], writes=[tsp_])
            st_S[g] = (sp_, tsp_)

        def stage_B(g):
            s, h, m, kk, nkt = steps[g]
            c0 = max(0, kk - 4 * m) * 128
            diag = kk >= 4 * m
            sp_, tsp_ = st_S.pop(g)
            if fox:
                (e1, te1) = e_r.next()
                act(e1[:, c0:512], sp_[:, c0:512], AF.Exp, reads=[tsp_], writes=[te1], scale=0.125)
                if diag:
                    tt("dve", e1[:, c0:c0 + 128], e1[:, c0:c0 + 128], tri, ALU.mult, reads=[te1, ttri], writes=[te1])
                st_E[g] = (e1, te1)
            else:
                near = kk >= 4 * m - 1
                (e2, te2) = e2_r.next()
                if near:
                    dd = 128 * (kk - 4 * m)
                    j0 = TZC - dd + c0
                    (sbt, tsb) = sb2_r.next()
                    P.newgen(tsb)
                    for mp in range(2):
                        stt(sbt[:, mp, c0:512], sp_[:, mp, c0:512], 0.125, tz[:, h, j0:j0 + 512 - c0], ALU.mult, ALU.add, reads=[tsp_, ttz], pw=[tsb])
                    if c0 == 0:
                        act(e2[:, :, :], sbt[:, :, :], AF.Exp, reads=[tsb], writes=[te2])
                    else:
                        P.newgen(te2)
                        for mp in range(2):
                            act(e2[:, mp, c0:512], sbt[:, mp, c0:512], AF.Exp, reads=[tsb], pw=[te2])
                else:
                    act(e2[:, :, c0:512], sp_[:, :, c0:512], AF.Exp, reads=[tsp_, tcb], writes=[te2], scale=0.125, bias=cb[:, h:h + 1])
                if diag:
                    tm_ = T()
                    tm_.w = list(te2.w)
                    for mp in range(2):
                        memset("pool", e2[64:128, mp, c0:c0 + 64], 0.0, writes=[tm_])
                        te2.w.extend(tm_.w)
                st_E[g] = (e2, te2)

        def stage_C(g):
            s, h, m, kk, nkt = steps[g]
            hi = heads.index((s, h))
            qt, tqt, kt, tkt, va, tva = hbufs[hi]
            c0 = max(0, kk - 4 * m) * 128
            e_, te_ = st_E.pop(g)
            if m == 0 and kk == 0:
                load_head(hi + 1)
            base = s * S
            g0 = base + m * 512
            if fox:
                if kk == 0:
                    cur["po"] = acc_ring.next()
                po, tpo = cur["po"]
                mm(po[:, c0:512], va[:, kk, :], e_[:, c0:512], kk == 0, kk == nkt - 1, reads=[tva, te_], writes=[tpo])
                if kk == nkt - 1:
                    (rl, trl) = o_r.next()
                    recip(rl[64:128, :], po[64:128, :], reads=[tpo], writes=[trl])
                    (aot, taot) = aot_r.next()
                    tt("dve", aot[0:64, :], po[0:64, :], rl[64:128, :], ALU.mult, reads=[tpo, trl], writes=[taot])
                    dma("pool", AOV[h * 64:(h + 1) * 64, g0:g0 + 512], aot[0:64, :], f"ao{taot.slot}", reads=[taot], pw=[tao[s]])
            else:
                po, tpo = po_pair
                pl, tpl = pl_pair
                for mp in range(2):
                    mm(po[:, mp, c0:512], va[:, kk, :], e_[:, mp, c0:512], kk == 0, kk == nkt - 1, reads=[tva, te_], writes=[tpo])
                    mm(pl[:, mp, c0:512], ones_bf[:], e_[:, mp, c0:512], kk == 0, kk == nkt - 1, reads=[t_ones, te_], writes=[tpl])
                if kk == nkt - 1:
                    (osb, tosb) = osb_r.next()
                    cp("dve", osb[:, :, :], po[:, :, :], reads=[tpo], writes=[tosb])
                    (r12, tr12) = r12_r.next()
                    P.op("dve", lambda e, r12=r12, pl=pl: e.reciprocal(r12[:, :, :], pl[:, :, :]), reads=[tpl], writes=[tr12])

                    (sq, tsq) = c.sqring.next()

                    def epi1(osb=osb, tosb=tosb, r12=r12, tr12=tr12, sq=sq, tsq=tsq):
                        tt("dve", r12[:, :, :], osb[:, :, :], r12[:, :, :], ALU.mult, reads=[tosb, tr12], writes=[tr12])
                        stt(r12[:, 0, :], r12[:, 1, :], nlam[:, 0:1], r12[:, 0, :], ALU.mult, ALU.add, reads=[tr12, tnl], writes=[tr12])
                        tt("dve", sq[:, :512], r12[:, 0, :], r12[:, 0, :], ALU.mult, reads=[tr12], writes=[tsq])

                    def epi2(h=h, s=s, g0=g0, r12=r12, tr12=tr12, sq=sq, tsq=tsq):
                        (pn2, tpn2) = s_ring.items[s_ring.i % len(s_ring.items)]
                        mm(pn2[:, 0, :], ones_bf[:], sq[:, :512], True, True, reads=[tsq, t_ones], writes=[tpn2])
                        (rs, trs) = c.rsring.next()
                        act(rs[:, :512], pn2[:, 0, :], AF.Ln, reads=[tpn2], writes=[trs], scale=1.0 / 128, bias=EPS)
                        act(rs[:, :512], rs[:, :512], AF.Exp, reads=[trs], writes=[trs], scale=-0.5)
                        (aot, taot) = aot_r.next()
                        stt(aot, r12[:, 0, :], subg[:, 0:1], rs[:, :512], ALU.mult, ALU.mult, reads=[tr12, trs, tsg], writes=[taot])
                        dma("pool", AOV[h * 128:(h + 1) * 128, g0:g0 + 512], aot, f"ao{taot.slot}", reads=[taot], pw=[tao[s]])
                    pending.append((g + 1, epi1))
                    pending.append((g + 3, epi2))

        DEPTH = 2
        for g in range(min(DEPTH, NSTEP)):
            stage_A(g)
        for g in range(NSTEP):
            stage_B(g)
            while pending and pending[0][0] <= g:
                pending.pop(0)[1]()
            if g + DEPTH < NSTEP:
                stage_A(g + DEPTH)
            stage_C(g)
        while pending:
            pending.pop(0)[1]()

        P.barrier()
        A.reset()
        ps_pool[0] = list(range(8))
        c = common_alloc(512, 8)
        aor = Ring([A.tile([8, 512], BF16) for _ in range(2)])
        for ti, (s, t0) in enumerate(tiles):
            g0 = s * S + t0
            hb, th = load_h(c, hin, s * S, t0, 512, 0)
            (at, tat) = aor.next()
            dma("sp", at, AOV.rearrange("(c p) t -> p c t", p=128)[:, :, g0:g0 + 512], f"al{tat.slot}", reads=[tao[s]], writes=[tat])
            st = out_back(c, at, tat, 8, 512, wout_b, twout, (l * 4 + 1) * 8, hb, th, 0, hout, g0)

    def rglru_layer(hin, hout):
        l = 3
        P.barrier()
        A.reset()
        c = common_alloc(459, 10, ny=1)
        kcs = []
        for oc in range(10):
            lo = (oc * 128) // 80 * 80
            hi = -(-((oc + 1) * 128) // 80) * 80
            kcs.append(list(range(lo // 128, min(10, -(-hi // 128)))))
        wr, twr = A.tile([10, 3, 128], BF16); wi, twi = A.tile([10, 3, 128], BF16)
        P.newgen(twr); P.newgen(twi)
        for oc in range(10):
            for i, k in enumerate(kcs[oc]):
                dma("sp", wr[:, oc, i, :], d_wr_b.ap()[:, k, oc * 128:(oc + 1) * 128], "c", reads=[t_w["d_wr"]], pw=[twr])
                dma("sp", wi[:, oc, i, :], d_wi_b.ap()[:, k, oc * 128:(oc + 1) * 128], "c", reads=[t_w["d_wi"]], pw=[twi])
        twr.const = True; twi.const = True
        cw, tcw = A.tile([10, 5]); dma("sp", cw, d_cw_d.ap(), "c", writes=[tcw]); tcw.const = True
        br, tbr = A.tile([10]); dma("sp", br, d_br_d.ap(), "c", writes=[tbr])
        bi, tbi = A.tile([10]); dma("sp", bi, d_bi_d.ap(), "c", writes=[tbi])
        ts("dve", br, br, -1.0, None, ALU.mult, ALU.bypass, reads=[tbr], writes=[tbr]); tbr.const = True
        ts("dve", bi, bi, -1.0, None, ALU.mult, ALU.bypass, reads=[tbi], writes=[tbi]); tbi.const = True
        cs1, tcs1 = A.tile([10]); dma("sp", cs1, d_lam_d.ap(), "c", writes=[tcs1])
        cs2, tcs2 = A.tile([10])
        act(cs1, cs1, AF.Exp, reads=[tcs1], writes=[tcs1], scale=-1.0)
        act(cs1, cs1, AF.Ln, reads=[tcs1], writes=[tcs1], scale=1.0, bias=1.0)
        ts("dve", cs1, cs1, -8.0, None, ALU.mult, ALU.bypass, reads=[tcs1], writes=[tcs1])
        ts("dve", cs2, cs1, 2.0, None, ALU.mult, ALU.bypass, reads=[tcs1], writes=[tcs2])
        tcs1.const = True; tcs2.const = True
        P.fence_dma("c")
        xc, txc = A.tile([10, 456]); xcb, txcb = A.tile([10, 456], BF16)
        gg, tgg = A.tile([10, 456], BF16)
        hh, thh = A.tile([10, 456])
        hst, thst = A.tile([10])
        yv, tyv = A.tile([10, 456], BF16)
        tyv_l = [T() for _ in range(10)]
        tmp_r = Ring([A.tile([456]) for _ in range(8)])
        tiles = [(s, t0, n) for s in range(NSEQ) for (t0, n) in seq_tiles(None, 456)]
        twin, twout = t_w["d_win"], t_w["d_wout"]
        nxt = load_h(c, hin, 0, 0, tiles[0][2], 3)
        prev = None
        for ti, (s, t0, n) in enumerate(tiles):
            hb, th = nxt
            if ti + 1 < len(tiles):
                s2, t02, n2 = tiles[ti + 1]
                nxt = load_h(c, hin, s2 * S, t02, n2, 3)
            w = n + 3
            c.rstd_exp = False
            y, ty = norm_front(c, hb, th, w, (l * 4 + 0) * 8)
            c.rstd_exp = True
            P.newgen(txc); P.newgen(txcb); P.newgen(tgg)
            for ch in range(10):
                wv, twv = wload(c, d_win_b.ap()[:, :, RW + ch * 128:RW + (ch + 1) * 128], (8, 128), twin)
                pb, tpb = psnext()
                for k in range(8):
                    mm(pb[:, :w], wv[:, k, :], y[:, k, :w], k == 0, k == 7, reads=[twv, ty], writes=[tpb])
                ta = T()
                act(xc[:, ch, :n], pb[:, 3:w], AF.Identity, reads=[tpb, tcw], writes=[ta], pw=[txc], scale=cw[:, ch, 3:4], bias=cw[:, ch, 4:5])
                for tap in range(3):
                    stt(xc[:, ch, :n], pb[:, tap:tap + n], cw[:, ch, tap:tap + 1], xc[:, ch, :n], ALU.mult, ALU.add, reads=[tpb, ta, tcw], writes=[ta])
                txc.w.extend(ta.w)
                act(xcb[:, ch, :n], xc[:, ch, :n], AF.Copy, reads=[ta], pw=[txcb])
                txc.r.extend(ta.r)
                wv, twv = wload(c, d_win_b.ap()[:, :, ch * 128:(ch + 1) * 128], (8, 128), twin)
                pb, tpb = psnext()
                for k in range(8):
                    mm(pb[:, :w], wv[:, k, :], y[:, k, :w], k == 0, k == 7, reads=[twv, ty], writes=[tpb])
                act(gg[:, ch, :n], pb[:, 3:w], AF.Gelu_apprx_tanh, reads=[tpb], pw=[tgg])
            P.newgen(thh); P.newgen(tyv)
            for oc in range(10):
                pr, tpr = psnext()
                for i, k in enumerate(kcs[oc]):
                    mm(pr[:, :n], wr[:, oc, i, :], xcb[:, k, :n], i == 0, i == len(kcs[oc]) - 1, reads=[twr, txcb], writes=[tpr])
                pi, tpi = psnext()
                for i, k in enumerate(kcs[oc]):
                    mm(pi[:, :n], wi[:, oc, i, :], xcb[:, k, :n], i == 0, i == len(kcs[oc]) - 1, reads=[twi, txcb], writes=[tpi])
                (r_, tr_) = tmp_r.next(); (i_, ti_) = tmp_r.next(); (a_, ta_) = tmp_r.next(); (m_, tm_) = tmp_r.next()
                act(r_[:, :n], pr[:, :n], AF.Exp, reads=[tpr, tbr], writes=[tr_], scale=-1.0, bias=br[:, oc:oc + 1])
                act(i_[:, :n], pi[:, :n], AF.Exp, reads=[tpi, tbi], writes=[ti_], scale=-1.0, bias=bi[:, oc:oc + 1])
                act(r_[:, :n], r_[:, :n], AF.Ln, reads=[tr_], writes=[tr_], scale=1.0, bias=1.0)
                act(i_[:, :n], i_[:, :n], AF.Ln, reads=[ti_], writes=[ti_], scale=1.0, bias=1.0)
                act(r_[:, :n], r_[:, :n], AF.Exp, reads=[tr_], writes=[tr_], scale=-1.0)
                act(i_[:, :n], i_[:, :n], AF.Exp, reads=[ti_], writes=[ti_], scale=-1.0)
                act(a_[:, :n], r_[:, :n], AF.Exp, reads=[tr_, tcs1], writes=[ta_], scale=cs1[:, oc:oc + 1])
                act(m_[:, :n], r_[:, :n], AF.Exp, reads=[tr_, tcs2], writes=[tm_], scale=cs2[:, oc:oc + 1])
                act(m_[:, :n], m_[:, :n], AF.Ln, reads=[tm_], writes=[tm_], scale=-1.0, bias=1.0)
                act(m_[:, :n], m_[:, :n], AF.Exp, reads=[tm_], writes=[tm_], scale=0.5)
                tt("dve", i_[:, :n], i_[:, :n], xc[:, oc, :n], ALU.mult, reads=[ti_, txc], writes=[ti_])
                tt("dve", i_[:, :n], i_[:, :n], m_[:, :n], ALU.mult, reads=[ti_, tm_], writes=[ti_])
                if t0 == 0:
                    init, rd = 0.0, []
                else:
                    init, rd = hst[:, oc:oc + 1], [thst]
                tsc_ = T()
                P.op("dve", lambda e, oc=oc, a_=a_, i_=i_, init=init, n=n: e.tensor_tensor_scan(hh[:, oc, :n], a_[:, :n], i_[:, :n], init, ALU.mult, ALU.add),
                     reads=[ta_, ti_] + rd, writes=[tsc_])
                thh.w.extend(tsc_.w)
                cp("dve", hst[:, oc:oc + 1], hh[:, oc, n - 1:n], reads=[tsc_], pw=[thst])
                tt("dve", yv[:, oc, :n], hh[:, oc, :n], gg[:, oc, :n], ALU.mult, reads=[tsc_, tgg], writes=[tyv_l[oc]])
            st = out_back(c, yv, tyv_l, 10, n, d_wout_b, twout, (l * 4 + 1) * 8, hb, th, 3, hout, s * S + t0)

    seqn = []
    for l in range(nlayers):
        seqn += [("mix", l), ("ffn", l)]
    if nsub is not None:
        seqn = seqn[:nsub]
    bufs = [xT] + [hbuf[i] for i in range(len(seqn) - 1)] + [outT]
    for i, (kind, l) in enumerate(seqn):
        hin, hout = bufs[i], bufs[i + 1]
        final.clear()
        if kind == "ffn":
            ffn_layer(l, hin, hout)
        elif l == 0:
            gmlp_layer(hin, hout)
        elif l in (1, 2):
            attn_layer(l, hin, hout)
        else:
            rglru_layer(hin, hout)
    P.emit(final_waits=list(final) + dumps)
    nc._in_names = list(ins.keys())
    return nc


def _t5_bucket(rel):
    half, max_exact = 16, 8
    n = np.abs(rel)
    ret = np.where(rel > 0, half, 0)
    nf = np.maximum(n, 1).astype(np.float32)
    large = max_exact + (np.log(nf / np.float32(max_exact)) / np.float32(math.log(128 / max_exact)) * (half - max_exact)).astype(np.int32)
    large = np.minimum(large, half - 1)
    return ret + np.where(n < max_exact, n, large)


def _kc(w):
    K, N = w.shape
    return np.ascontiguousarray(w.reshape(K // 128, 128, N).transpose(1, 0, 2))


def _col(v, nch):
    return np.ascontiguousarray(v.reshape(nch, 128).T)


def prep_shared(inp):
    f = np.float32
    m = {}
    ngx = inp["norm_g"]
    m["ng"] = np.ascontiguousarray(ngx.reshape(4, 4, 8, 128).transpose(3, 0, 1, 2).reshape(128, 128)).astype(f)
    perm = np.concatenate([np.concatenate([np.arange(j * 128, (j + 1) * 128), DFF + np.arange(j * 128, (j + 1) * 128)]) for j in range(22)])
    for l in range(4):
        m[f"wup{l}"] = _kc(inp["ffn_w_up"][l][:, perm])
        m[f"wdn{l}"] = _kc(inp["ffn_w_down"][l])
        cwv = inp["ffn_conv_w"][l]
        cb = inp["ffn_conv_b"][l]
        arr = np.concatenate([cwv, cb[None]], 0)
        m[f"fcw{l}"] = np.ascontiguousarray(arr.reshape(4, 44, 128).transpose(2, 1, 0)).astype(f)
    m["a_win"] = _kc(inp["a_w_in"][0])
    m["a_lng"] = np.ascontiguousarray(np.broadcast_to(inp["a_ln_g"][0][None], (128, 1024))).astype(f)
    m["a_lnb"] = np.ascontiguousarray(np.broadcast_to(inp["a_ln_b"][0][None], (128, 1024))).astype(f)
    m["a_wsT"] = np.ascontiguousarray(inp["a_w_s"][0].transpose(2, 0, 1)).astype(f)
    p = np.arange(128)
    m["a_mask"] = ((p[:, None] // 64) <= (p[None, :] // 64)).astype(f)
    m["a_bs"] = np.ascontiguousarray(np.broadcast_to(np.tile(inp["a_b_s"][0], (1, 4))[None], (128, 8, 512))).astype(f)
    m["a_wout"] = _kc(inp["a_w_out"][0])
    m["b_win"] = _kc(inp["b_w_in"][0])
    m["b_lam"] = np.ascontiguousarray(inp["b_lam"][0].reshape(1, 256)).astype(f)
    m["b_subg"] = np.ascontiguousarray(inp["b_sub_g"][0].reshape(128, 1)).astype(f)
    m["b_wout"] = _kc(inp["b_w_out"][0])
    m["relb"] = np.ascontiguousarray(inp["rel_bias"]).astype(f)
    mmv = np.arange(GVN)
    bk = _t5_bucket(127 + TZC - mmv)
    oh = np.zeros((32, GVN), f)
    oh[bk, mmv] = 1.0
    m["ohrev"] = oh
    m["jflip"] = np.ascontiguousarray(np.eye(128, dtype=f)[::-1])
    m["c_win"] = _kc(inp["c_w_in"][0])
    m["c_bf"] = np.ascontiguousarray(inp["c_b_f"][0].reshape(16, 1)).astype(f)
    m["c_wout"] = _kc(inp["c_w_out"][0])
    m["tri"] = (p[:, None] <= p[None, :]).astype(f)
    m["d_win"] = _kc(inp["d_w_in"][0])
    cwd = np.concatenate([inp["d_conv_w"][0], inp["d_conv_b"][0][None]], 0)
    m["d_cw"] = np.ascontiguousarray(cwd.reshape(5, 10, 128).transpose(2, 1, 0)).astype(f)
    for nm, key in (("d_wr", "d_w_r"), ("d_wi", "d_w_i")):
        dense = np.zeros((RW, RW), f)
        for n in range(16):
            dense[n * 80:(n + 1) * 80, n * 80:(n + 1) * 80] = inp[key][0][n]
        m[nm] = _kc(dense)
    m["d_br"] = _col(inp["d_b_r"][0], 10).astype(f)
    m["d_bi"] = _col(inp["d_b_i"][0], 10).astype(f)
    m["d_lam"] = _col(inp["d_lam"][0], 10).astype(f)
    m["d_wout"] = _kc(inp["d_w_out"][0])
    return m


_NC_CACHE = {}


def run(inputs, nlayers=4, trace=False, nsub=None):
    inp = {k: np.asarray(v) for k, v in inputs.items()}
    if (nlayers, nsub) not in _NC_CACHE:
        _NC_CACHE[(nlayers, nsub)] = build(nlayers, nsub)
    nc = _NC_CACHE[(nlayers, nsub)]
    shared = prep_shared(inp)
    shared = {k: v for k, v in shared.items() if k in nc._in_names}
    x = inp["x"]
    in_maps = []
    for cidx in range(NCORES):
        xs = x[cidx * NSEQ:(cidx + 1) * NSEQ].reshape(TC, D)
        mcore = dict(shared)
        mcore["xT"] = np.ascontiguousarray(xs.T)
        in_maps.append(mcore)
    res = run_bass_kernel_spmd(nc, in_maps, core_ids=list(range(NCORES)), **({"trace": True} if trace else {}))
    out = np.empty((16, S, D), np.float32)
    for cidx in range(NCORES):
        out[cidx * NSEQ:(cidx + 1) * NSEQ] = res.results[cidx]["outT"].T.reshape(NSEQ, S, D)
    return out, res


def kernel(**inputs):
    out, _ = run(inputs, 4)
    return out
```

```python
import contextlib
import math
import os
KCUT = int(os.environ.get('KCUT', '99'))
KDUMP = int(os.environ.get('KDUMP', '0'))
import numpy as np
import concourse.bass as bass
import concourse.mybir as mybir
from concourse.bass_utils import run_bass_kernel_spmd

F32 = mybir.dt.float32
BF16 = mybir.dt.bfloat16
AF = mybir.ActivationFunctionType
ALU = mybir.AluOpType

NCORES = 8
D = 1024
S = 4096
NSEQ = 2
TC = S * NSEQ
EPS = 1e-6
DFF = 2816
RW = 1280
SEM_CAP = 6000
LAMBDA_INIT = 0.8 - 0.6 * math.exp(-0.3 * 1)
TZC = 384
GVN = 1152


class T:
    __slots__ = ("w", "r", "prev", "const", "slot")

    def __init__(self, const=False):
        self.slot = 0
        self.w = []
        self.r = []
        self.prev = []
        self.const = const


class Prog:
    ENGS = ("pe", "act", "dve", "pool", "sp")

    def __init__(self, nc):
        self.nc = nc
        self.ins = {e: [] for e in self.ENGS}
        self.dma_cnt = {}
        self.last_real = {}
        self.stack = contextlib.ExitStack()

    def newgen(self, t):
        t.prev = t.w + t.r
        t.w = []
        t.r = []

    def op(self, eng, fn, reads=(), writes=(), pw=(), dma=None):
        me_idx = len(self.ins[eng])
        if dma:
            prod = ("dma:" + dma, self.dma_cnt.get(dma, 0))
            self.dma_cnt[dma] = prod[1] + 1
        else:
            prod = (eng, me_idx)
        deps = set()
        raw = set()
        for t in reads:
            deps.update(t.w)
            raw.update(t.w)
        for t in writes:
            deps.update(t.w)
            deps.update(t.r)
            deps.update(t.prev)
        for t in pw:
            deps.update(t.prev)
        pruned = []
        for d in deps:
            if d[0] == eng and not dma:
                if eng == "pe" or d not in raw:
                    continue
            pruned.append(d)
        self.ins[eng].append([fn, pruned, False, dma])
        if not dma:
            self.last_real[eng] = me_idx
        for t in writes:
            t.w = [prod]
            t.r = []
            t.prev = []
        for t in pw:
            t.w.append(prod)
        for t in reads:
            if not t.const and prod not in t.w:
                t.r.append(prod)
        return prod

    def fence_dma(self, key):
        c = self.dma_cnt.get(key, 0)
        if c:
            for e in self.ENGS:
                self.ins[e].append([None, [("dma:" + key, c - 1)], False, None])

    def barrier(self):
        lasts = [(e, i) for e, i in self.last_real.items()]
        dmas = [("dma:" + k, c - 1) for k, c in self.dma_cnt.items() if c > 0]
        for e in self.ENGS:
            deps = [d for d in lasts if d[0] != e] + dmas
            self.ins[e].append([None, deps, False, None])

    def emit(self, final_waits=()):
        nc = self.nc
        for e in self.ENGS:
            for ins in self.ins[e]:
                for d in ins[1]:
                    if not d[0].startswith("dma:"):
                        self.ins[d[0]][d[1]][2] = True
        signum = {}
        for e in self.ENGS:
            c = 0
            for i, ins in enumerate(self.ins[e]):
                if ins[2]:
                    signum[(e, i)] = c
                    c += 1
        sems = {}

        def getsem(key):
            if key not in sems:
                sems[key] = self.stack.enter_context(nc.semaphore(key.replace(":", "_")))
            return sems[key]

        dcap = SEM_CAP // 16

        def wait_target(d):
            if d[0].startswith("dma:"):
                j = d[1]
                return (getsem(f"{d[0]}_{j // dcap}"), 16 * (j % dcap + 1))
            j = signum[d]
            return (getsem(f"e_{d[0]}_{j // SEM_CAP}"), j % SEM_CAP + 1)

        for e in self.ENGS:
            dcount = {}
            for i, ins in enumerate(self.ins[e]):
                for d in ins[1]:
                    wait_target(d)
                if ins[2]:
                    wait_target((e, i))
                if ins[3]:
                    k = ins[3]
                    wait_target(("dma:" + k, dcount.get(k, 0)))
                    dcount[k] = dcount.get(k, 0) + 1
        for d in final_waits:
            wait_target(d)
        prog = self

        def run_engine(ename, eng, extra_waits=()):
            seen = {}
            dcount = {}
            for i, (fn, deps, sig, dma) in enumerate(prog.ins[ename]):
                wl = {}
                for d in deps:
                    s, v = wait_target(d)
                    if seen.get(s.name, 0) >= v:
                        continue
                    if wl.get(s.name, (None, 0))[1] < v:
                        wl[s.name] = (s, v)
                for s, v in wl.values():
                    eng.wait_ge(s, v)
                    seen[s.name] = v
                if fn is None:
                    continue
                instr = fn(eng)
                if dma:
                    j = dcount.get(dma, 0)
                    dcount[dma] = j + 1
                    s, v = wait_target(("dma:" + dma, j))
                    instr.then_inc(s, 16)
                elif sig:
                    s, v = wait_target((ename, i))
                    instr.then_inc(s, 1)
            for d in extra_waits:
                s, v = wait_target(d)
                eng.wait_ge(s, v)

        with nc.Block() as block:
            @block.tensor
            def _(eng):
                run_engine("pe", eng)

            @block.scalar
            def _(eng):
                run_engine("act", eng)

            @block.vector
            def _(eng):
                run_engine("dve", eng)

            @block.gpsimd
            def _(eng):
                run_engine("pool", eng, extra_waits=final_waits)

            @block.sync
            def _(eng):
                run_engine("sp", eng)
        self.stack.close()


class Ring:
    def __init__(self, items):
        self.items = items
        self.i = 0
        for k, it in enumerate(items):
            if isinstance(it, tuple) and isinstance(it[1], T):
                it[1].slot = k

    def next(self):
        it = self.items[self.i % len(self.items)]
        self.i += 1
        return it


def seq_tiles(n_full, width):
    out = []
    t = 0
    while t < S:
        n = min(width, S - t)
        out.append((t, n))
        t += n
    return out


def build(nlayers=4, nsub=None):
    nc = bass.Bass("TRN2", target_bir_lowering=False)
    P = Prog(nc)
    ins = {}

    def din(name, shape, dt=F32):
        ins[name] = nc.dram_tensor(name, list(shape), dt, kind="ExternalInput")
        return ins[name]

    xT = din("xT", [D, TC])
    ng_d = din("ng", [128, 128])
    NLD = max(nlayers, 1)
    wup_d = [din(f"wup{l}", [128, 8, 2 * DFF]) for l in range(NLD)]
    wdn_d = [din(f"wdn{l}", [128, 22, D]) for l in range(NLD)]
    fcw_d = [din(f"fcw{l}", [128, 44, 4]) for l in range(NLD)]
    a_win_d = din("a_win", [128, 8, 2048]); a_lng_d = din("a_lng", [128, 1024]); a_lnb_d = din("a_lnb", [128, 1024])
    a_wsT_d = din("a_wsT", [128, 8, 128]); a_mask_d = din("a_mask", [128, 128]); a_bs_d = din("a_bs", [128, 8, 512])
    a_wout_d = din("a_wout", [128, 8, D])
    if nlayers > 1:
      b_win_d = din("b_win", [128, 8, 3072]); b_lam_d = din("b_lam", [1, 256]); b_subg_d = din("b_subg", [128, 1])
      b_wout_d = din("b_wout", [128, 8, D]); relb_d = din("relb", [32, 8]); ohrev_d = din("ohrev", [32, GVN]); jflip_d = din("jflip", [128, 128])
    if nlayers > 2:
      c_win_d = din("c_win", [128, 8, 3088]); c_bf_d = din("c_bf", [16, 1]); c_wout_d = din("c_wout", [128, 8, D]); tri_d = din("tri", [128, 128])
    if nlayers > 3:
      d_win_d = din("d_win", [128, 8, 2 * RW]); d_cw_d = din("d_cw", [128, 10, 5]); d_wr_d = din("d_wr", [128, 10, RW]); d_wi_d = din("d_wi", [128, 10, RW])
      d_br_d = din("d_br", [128, 10]); d_bi_d = din("d_bi", [128, 10]); d_lam_d = din("d_lam", [128, 10]); d_wout_d = din("d_wout", [128, 10, D])
    outT = nc.dram_tensor("outT", [D, TC], F32, kind="ExternalOutput")

    def dscr(name, shape, dt):
        return nc.dram_tensor(name, list(shape), dt)

    wup_b = [dscr(f"wupb{l}", [128, 8, 2 * DFF], BF16) for l in range(4)]
    wdn_b = [dscr(f"wdnb{l}", [128, 22, D], BF16) for l in range(4)]
    a_win_b = dscr("a_winb", [128, 8, 2048], BF16); a_wout_b = dscr("a_woutb", [128, 8, D], BF16)
    b_win_b = dscr("b_winb", [128, 8, 3072], BF16); b_wout_b = dscr("b_woutb", [128, 8, D], BF16)
    c_win_b = dscr("c_winb", [128, 8, 3088], BF16); c_wout_b = dscr("c_woutb", [128, 8, D], BF16)
    d_win_b = dscr("d_winb", [128, 8, 2 * RW], BF16); d_wout_b = dscr("d_woutb", [128, 10, D], BF16)
    d_wr_b = dscr("d_wrb", [128, 10, RW], BF16); d_wi_b = dscr("d_wib", [128, 10, RW], BF16)
    hbuf = [dscr(f"h{i}", [D, TC], F32) for i in range(7)]
    qk_s = {l: dscr(f"qk{l}", [2048, TC], BF16) for l in (1, 2)}
    v_s = {l: dscr(f"v{l}", [TC, 1024], BF16) for l in (1, 2)}
    ao_s = {l: dscr(f"ao{l}", [D, TC], BF16) for l in (1, 2)}
    cq_s = dscr("cq", [16, 6, TC], BF16); ck_s = dscr("ck", [16, 6, TC], BF16)
    gv_s = dscr("gv", [8, GVN], F32)

    sb = lambda name, shape, dt=F32: P.stack.enter_context(nc.sbuf_tensor(name, list(shape), dt))
    ARENA_N = 52400
    arena = sb("arena", [128, ARENA_N])
    ones_bf = sb("ones_bf", [128, 128], BF16); t_ones = T(const=True)
    ng = sb("ngs", [128, 128]); t_ng = T(const=True)
    PSALL = P.stack.enter_context(nc.psum_tensor("psall", [128, 4096], F32))
    PSB = [PSALL[:, i * 512:(i + 1) * 512] for i in range(8)]
    TPS = [T() for _ in range(8)]

    class Arena:
        def __init__(self):
            self.off = 0

        def reset(self):
            self.off = 0

        def alloc(self, n, dt=F32):
            n32 = n if dt == F32 else (n + 1) // 2
            a = arena[:, self.off:self.off + n32]
            self.off += n32
            assert self.off <= ARENA_N, self.off
            return a if dt == F32 else a.bitcast(BF16)[:, :n]

        def tile(self, shape, dt=F32):
            n = int(np.prod(shape))
            a = self.alloc(n, dt)
            if len(shape) == 2:
                a = a.rearrange("p (a b) -> p a b", a=shape[0])
            elif len(shape) == 3:
                a = a.rearrange("p (a b c) -> p a b c", a=shape[0], b=shape[1])
            return a, T()

    A = Arena()

    def act(out, in_, func, reads, writes=(), pw=(), **kw):
        return P.op("act", lambda e: e.activation(out, in_, func, **kw), reads, writes, pw)

    def mm(out, lhsT, rhs, start, stop, reads, writes):
        return P.op("pe", lambda e: e.matmul(out, lhsT, rhs, start=start, stop=stop), reads, writes)

    def tt(eng, out, in0, in1, op, reads, writes=(), pw=()):
        return P.op(eng, lambda e: e.tensor_tensor(out, in0, in1, op), reads, writes, pw)

    def stt(out, in0, scalar, in1, op0, op1, reads, writes=(), pw=()):
        return P.op("dve", lambda e: e.scalar_tensor_tensor(out, in0, scalar, in1, op0, op1), reads, writes, pw)

    def ts(eng, out, in0, s1, s2, op0, op1, reads, writes=(), pw=()):
        if s2 is None:
            return P.op(eng, lambda e: e.tensor_scalar(out, in0, s1, None, op0), reads, writes, pw)
        return P.op(eng, lambda e: e.tensor_scalar(out, in0, s1, s2, op0, op1), reads, writes, pw)

    def cp(eng, out, in_, reads, writes=(), pw=()):
        return P.op(eng, lambda e: e.tensor_copy(out, in_), reads, writes, pw)

    def recip(out, in_, reads, writes=(), pw=()):
        return P.op("dve", lambda e: e.reciprocal(out, in_), reads, writes, pw)

    def memset(eng, ap, val, writes=(), pw=()):
        return P.op(eng, lambda e: e.memset(ap, val), (), writes, pw)

    def dma(q, out, in_, key, reads=(), writes=(), pw=()):
        return P.op(q, lambda e: e.dma_start(out=out, in_=in_), reads, writes, pw, dma=key)

    dumps = []

    def dump(name, ap, reads):
        if not KDUMP:
            return
        d = nc.dram_tensor("dbg_" + name, list(ap.shape), ap.dtype, kind="ExternalOutput")
        dumps.append(dma("pool", d.ap(), ap, "dbg", reads=reads))

    ps_i = [0]
    ps_pool = [list(range(8))]

    def psnext(excl=None):
        pool = ps_pool[0]
        while True:
            i = pool[ps_i[0] % len(pool)]
            ps_i[0] += 1
            if excl is None or PSB[i] is not excl:
                return PSB[i], TPS[i]

    memset("pool", ones_bf[:], 1.0, writes=[t_ones])
    dma("sp", ng[:], ng_d.ap(), "c", writes=[t_ng])

    t_w = {}

    def cast_weight(src, dst, kc, n, name):
        t = T(const=True)
        t_w[name] = t
        total = kc * n
        sflat = src.ap().rearrange("p a b -> p (a b)")
        dflat = dst.ap().rearrange("p a b -> p (a b)")
        PIECE = 2048
        for o in range(0, total, PIECE):
            m = min(PIECE, total - o)
            (s32, ts32) = stg32.next()
            (s16, ts16) = stg16.next()
            dma("sp", s32[:, :m], sflat[:, o:o + m], f"cw{ts32.slot}", writes=[ts32])
            eng = cast_eng.next()
            if eng == "act":
                act(s16[:, :m], s32[:, :m], AF.Copy, reads=[ts32], writes=[ts16])
            else:
                cp(eng, s16[:, :m], s32[:, :m], reads=[ts32], writes=[ts16])
            dma("pool", dflat[:, o:o + m], s16[:, :m], f"cs{ts16.slot}", reads=[ts16], pw=[t])

    A.reset()
    stg32 = Ring([(A.alloc(2048), T()) for _ in range(4)])
    stg16 = Ring([(A.alloc(2048, BF16), T()) for _ in range(4)])
    cast_eng = Ring(["dve", "act", "pool"])
    cast_list = [(a_win_d, a_win_b, 8, 2048, "a_win"), (a_wout_d, a_wout_b, 8, D, "a_wout"), (wup_d[0], wup_b[0], 8, 2 * DFF, "wup0"), (wdn_d[0], wdn_b[0], 22, D, "wdn0")]
    if nlayers > 1:
        cast_list += [(b_win_d, b_win_b, 8, 3072, "b_win"), (b_wout_d, b_wout_b, 8, D, "b_wout"), (wup_d[1], wup_b[1], 8, 2 * DFF, "wup1"), (wdn_d[1], wdn_b[1], 22, D, "wdn1")]
    if nlayers > 2:
        cast_list += [(c_win_d, c_win_b, 8, 3088, "c_win"), (c_wout_d, c_wout_b, 8, D, "c_wout"), (wup_d[2], wup_b[2], 8, 2 * DFF, "wup2"), (wdn_d[2], wdn_b[2], 22, D, "wdn2")]
    if nlayers > 3:
        cast_list += [(d_win_d, d_win_b, 8, 2 * RW, "d_win"), (d_wout_d, d_wout_b, 10, D, "d_wout"), (d_wr_d, d_wr_b, 10, RW, "d_wr"), (d_wi_d, d_wi_b, 10, RW, "d_wi"),
                      (wup_d[3], wup_b[3], 8, 2 * DFF, "wup3"), (wdn_d[3], wdn_b[3], 22, D, "wdn3")]
    for c in cast_list:
        cast_weight(*c)

    def hview(buf, a, b):
        return buf.ap().rearrange("(c p) t -> p c t", p=128)[:, :, a:b]

    class Ctx:
        pass

    def common_alloc(wmax, kc_back, nh=2, ny=2, nyo=2, full=True):
        c = Ctx()
        if full:
            c.wring = Ring([A.tile([4096], BF16) for _ in range(4)])
            c.hring = Ring([A.tile([8, wmax]) for _ in range(nh)])
            c.yring = Ring([A.tile([8, wmax], BF16) for _ in range(ny)])
            c.yoring = Ring([A.tile([8, wmax]) for _ in range(nyo)])
        c.sqring = Ring([A.tile([wmax], BF16) for _ in range(3)])
        c.rsring = Ring([A.tile([wmax]) for _ in range(2)])
        return c

    def wload(c, src_ap, shape, tw):
        (wt, twt) = c.wring.next()
        a, b = shape
        view = wt[:, :a * b].rearrange("p (a b) -> p a b", a=a)
        dma("sp", view, src_ap, f"w{twt.slot}", reads=[tw], writes=[twt])
        return view, twt

    def load_h(c, src, base, t0, n, halo):
        (hb, th) = c.hring.next()
        w = n + halo
        if halo and t0 == 0:
            P.newgen(th)
            memset("pool", hb[:, :, 0:halo], 0.0, pw=[th])
            dma("sp", hb[:, :, halo:w], hview(src, base, base + n), f"h{th.slot}", reads=[t_h[src.name]], pw=[th])
        else:
            dma("sp", hb[:, :, 0:w], hview(src, base + t0 - halo, base + t0 + n), f"h{th.slot}", reads=[t_h[src.name]], writes=[th])
        return hb, th

    def rstd_from_ps(c, pb, tpb, w, scale):
        (rs, trs) = c.rsring.next()
        if getattr(c, "rstd_exp", False):
            act(rs[:, :w], pb[:, :w], AF.Ln, reads=[tpb], writes=[trs], scale=scale, bias=EPS)
            act(rs[:, :w], rs[:, :w], AF.Exp, reads=[trs], writes=[trs], scale=-0.5)
        else:
            act(rs[:, :w], pb[:, :w], AF.Sqrt, reads=[tpb], writes=[trs], scale=scale, bias=EPS)
            recip(rs[:, :w], rs[:, :w], reads=[trs], writes=[trs])
        return rs, trs

    def norm_front(c, hb, th, w, gcol):
        (y, ty) = c.yring.next()
        P.newgen(ty)
        pb, tpb = psnext()
        for ch in range(8):
            (sq, tsq) = c.sqring.next()
            act(sq[:, :w], hb[:, ch, :w], AF.Square, reads=[th], writes=[tsq])
            mm(pb[:, :w], ones_bf[:], sq[:, :w], ch == 0, ch == 7, reads=[tsq, t_ones], writes=[tpb])
        rs, trs = rstd_from_ps(c, pb, tpb, w, 1.0 / D)
        for ch in range(8):
            stt(y[:, ch, :w], hb[:, ch, :w], ng[:, gcol + ch:gcol + ch + 1], rs[:, :w], ALU.mult, ALU.mult, reads=[th, trs, t_ng], pw=[ty])
        return y, ty

    dbg_ob = [False]

    def out_back(c, src, tsrc, kc, n, wsrc, tw, gcol, hb, th, halo, dst, dbase, defer=False):
        (yo, tyo) = c.yoring.next()
        P.newgen(tyo)
        pn, tpn = psnext()
        prev_sq = None
        for oc in range(8):
            wv, twv = wload(c, wsrc.ap()[:, :, oc * 128:(oc + 1) * 128], (kc, 128), tw)
            pb, tpb = psnext(excl=pn)
            for k in range(kc):
                tk = tsrc[k] if isinstance(tsrc, list) else tsrc
                mm(pb[:, :n], wv[:, k, :], src[:, k, :n], k == 0, k == kc - 1, reads=[twv, tk], writes=[tpb])
            if prev_sq is not None:
                mm(pn[:, :n], ones_bf[:], prev_sq[0][:, :n], prev_sq[2] == 0, False, reads=[prev_sq[1], t_ones], writes=[tpn])
            act(yo[:, oc, :n], pb[:, :n], AF.Copy, reads=[tpb], pw=[tyo])
            (sq, tsq) = c.sqring.next()
            act(sq[:, :n], pb[:, :n], AF.Square, reads=[tpb], writes=[tsq])
            prev_sq = (sq, tsq, oc)
        mm(pn[:, :n], ones_bf[:], prev_sq[0][:, :n], False, True, reads=[prev_sq[1], t_ones], writes=[tpn])
        rs, trs = rstd_from_ps(c, pn, tpn, n, 1.0 / D)
        tyo2 = T()
        tyo3 = T()

        def piece(oc):
            tt("pool" if defer else "dve", yo[:, oc, :n], yo[:, oc, :n], rs[:, :n], ALU.mult, reads=[tyo, trs], pw=[tyo2])
            stt(yo[:, oc, :n], yo[:, oc, :n], ng[:, gcol + oc:gcol + oc + 1], hb[:, oc, halo:halo + n], ALU.mult, ALU.add, reads=[tyo2, th, t_ng], pw=[tyo3])

        def store():
            st = dma("pool", hview(dst, dbase, dbase + n), yo[:, :, :n], f"hs{tyo.slot}", reads=[tyo3], pw=[t_h[dst.name]])
            tyo.r.append(st)
            tyo.w.extend(tyo2.w + tyo3.w)
            final.append(st)
            return st

        if defer:
            return [(lambda oc=oc: piece(oc)) for oc in range(8)] + [store]
        for oc in range(8):
            piece(oc)
        return store()

    t_h = {b.name: T(const=True) for b in hbuf}
    t_h[xT.name] = T(const=True)
    t_h[outT.name] = T(const=True)
    final = []

    def ffn_layer(l, hin, hout):
        P.barrier()
        A.reset()
        c = common_alloc(458, 22, nh=3)
        fcw, tfcw = A.tile([44, 4])
        dma("sp", fcw, fcw_d[l].ap(), "c", writes=[tfcw])
        tfcw.const = True
        P.fence_dma("c")
        gu, tgu = A.tile([22, 456], BF16)
        aring = Ring([A.tile([456]) for _ in range(6)])
        glring = Ring([A.tile([456]) for _ in range(3)])
        tiles = [(s, t0, n) for s in range(NSEQ) for (t0, n) in seq_tiles(None, 456)]
        twu, twd = t_w[f"wup{l}"], t_w[f"wdn{l}"]
        nxt = load_h(c, hin, tiles[0][0] * S, tiles[0][1], tiles[0][2], 2)
        y_nxt = norm_front(c, nxt[0], nxt[1], tiles[0][2] + 2, (l * 4 + 2) * 8)
        tgu_l = [T() for _ in range(22)]
        tail = []
        for ti, (s, t0, n) in enumerate(tiles):
            hb, th = nxt
            y, ty = y_nxt
            if ti + 1 < len(tiles):
                s2, t02, n2 = tiles[ti + 1]
                nxt = load_h(c, hin, s2 * S, t02, n2, 2)
            w = n + 2
            for j in range(22):
                if tail and j >= 1:
                    tail.pop(0)()
                if j == 10 and ti + 1 < len(tiles):
                    assert not tail
                    y_nxt = norm_front(c, nxt[0], nxt[1], tiles[ti + 1][2] + 2, (l * 4 + 2) * 8)
                wv, twv = wload(c, wup_b[l].ap()[:, :, j * 256:(j + 1) * 256], (8, 256), twu)
                res = []
                for half in range(2):
                    cc = j + 22 * half
                    pb, tpb = psnext()
                    for k in range(8):
                        mm(pb[:, :w], wv[:, k, half * 128:(half + 1) * 128], y[:, k, :w], k == 0, k == 7, reads=[twv, ty], writes=[tpb])
                    (a, ta) = aring.next()
                    act(a[:, :n], pb[:, 2:w], AF.Identity, reads=[tpb, tfcw], writes=[ta], scale=fcw[:, cc, 2:3], bias=fcw[:, cc, 3:4])
                    stt(a[:, :n], pb[:, 1:w - 1], fcw[:, cc, 1:2], a[:, :n], ALU.mult, ALU.add, reads=[tpb, ta, tfcw], writes=[ta])
                    stt(a[:, :n], pb[:, 0:w - 2], fcw[:, cc, 0:1], a[:, :n], ALU.mult, ALU.add, reads=[tpb, ta, tfcw], writes=[ta])
                    res.append((a, ta))
                (gl, tgl) = glring.next()
                act(gl[:, :n], res[0][0][:, :n], AF.Gelu_apprx_tanh, reads=[res[0][1]], writes=[tgl])
                tt("pool", gu[:, j, :n], gl[:, :n], res[1][0][:, :n], ALU.mult, reads=[tgl, res[1][1]], writes=[tgu_l[j]])
            tail = out_back(c, gu, tgu_l, 22, n, wdn_b[l], twd, (l * 4 + 3) * 8, hb, th, 2, hout, s * S + t0, defer=True)
        while tail:
            tail.pop(0)()

    def gmlp_layer(hin, hout):
        P.barrier()
        A.reset()
        c = common_alloc(512, 8)
        lng, tlng = A.tile([1024]); lnb, tlnb = A.tile([1024])
        bs, tbs = A.tile([8, 512])
        wsT32, tws32 = A.tile([8, 128]); msk, tmsk = A.tile([128])
        wsT, tws = A.tile([8, 128], BF16)
        dma("sp", lng, a_lng_d.ap(), "c", writes=[tlng]); dma("sp", lnb, a_lnb_d.ap(), "c", writes=[tlnb])
        dma("sp", bs, a_bs_d.ap(), "c", writes=[tbs]); dma("sp", wsT32, a_wsT_d.ap(), "c", writes=[tws32]); dma("sp", msk, a_mask_d.ap(), "c", writes=[tmsk])
        for g in range(8):
            tt("dve", wsT[:, g, :], wsT32[:, g, :], msk, ALU.mult, reads=[tws32, tmsk], pw=[tws])
        for t in (tlng, tlnb, tbs, tws):
            t.const = True
        P.fence_dma("c")
        u_sb, tu = A.tile([8, 512])
        v_sb, tv = A.tile([4, 1024])
        tv_l = [T() for _ in range(4)]
        tguv_l = [T() for _ in range(8)]
        vn, tvn = A.tile([4, 1024], BF16)
        guv, tguv = A.tile([8, 512], BF16)
        st6, tst6 = A.tile([4, 12]); mv, tmv = A.tile([4, 4])
        tiles = [(s, t0) for s in range(NSEQ) for t0 in range(0, S, 512)]
        twin, twout = t_w["a_win"], t_w["a_wout"]
        nxt = load_h(c, hin, 0, 0, 512, 0)
        y_nxt = norm_front(c, nxt[0], nxt[1], 512, (0 * 4 + 0) * 8)
        for ti, (s, t0) in enumerate(tiles):
            hb, th = nxt
            def cut(extra):
                st = dma("pool", hview(hout, s * S + t0, s * S + t0 + 512), hb[:, :, :512], "hs", reads=[th] + extra, pw=[t_h[hout.name]])
                final.append(st)
            if KCUT == 0:
                cut([]); continue
            y, ty = y_nxt
            if ti == 0:
                dump("y", y, [ty])
            if KCUT == 1:
                cut([ty]); continue
            P.newgen(tu)
            for oc in range(8):
                wv, twv = wload(c, a_win_b.ap()[:, :, oc * 128:(oc + 1) * 128], (8, 128), twin)
                pb, tpb = psnext()
                for k in range(8):
                    mm(pb[:, :], wv[:, k, :], y[:, k, :], k == 0, k == 7, reads=[twv, ty], writes=[tpb])
                act(u_sb[:, oc, :], pb[:, :], AF.Gelu_apprx_tanh, reads=[tpb], pw=[tu])
            if KCUT == 2:
                cut([tu]); continue
            wvs = [wload(c, a_win_b.ap()[:, :, 1024 + half * 512:1024 + (half + 1) * 512], (8, 512), twin) for half in range(2)]
            if ti + 1 < len(tiles):
                s2, t02 = tiles[ti + 1]
                nxt_new = load_h(c, hin, s2 * S, t02, 512, 0)
            P.newgen(tvn)
            for tcn in range(4):
                tvt = tv_l[tcn]
                P.newgen(tvt)
                for half in range(2):
                    wv, twv = wvs[half]
                    pb, tpb = psnext()
                    for k in range(8):
                        mm(pb[:, :], y[:, k, tcn * 128:(tcn + 1) * 128], wv[:, k, :], k == 0, k == 7, reads=[twv, ty], writes=[tpb])
                    act(v_sb[:, tcn, half * 512:(half + 1) * 512], pb[:, :], AF.Gelu_apprx_tanh, reads=[tpb], pw=[tvt])
                tl = T()
                P.op("dve", lambda e, tcn=tcn: e.bn_stats(st6[:, tcn, 0:6], v_sb[:, tcn, 0:512]), reads=[tvt], writes=[tl])
                P.op("dve", lambda e, tcn=tcn: e.bn_stats(st6[:, tcn, 6:12], v_sb[:, tcn, 512:1024]), reads=[tvt], pw=[tl])
                tm = T()
                P.op("dve", lambda e, tcn=tcn: e.bn_aggr(mv[:, tcn, 0:2], st6[:, tcn, :]), reads=[tl], writes=[tm])
                act(mv[:, tcn, 2:3], mv[:, tcn, 1:2], AF.Sqrt, reads=[tm], writes=[tm], scale=1.0, bias=EPS)
                recip(mv[:, tcn, 3:4], mv[:, tcn, 2:3], reads=[tm], writes=[tm])
                tvv = T()
                ts("dve", v_sb[:, tcn, :], v_sb[:, tcn, :], mv[:, tcn, 0:1], mv[:, tcn, 3:4], ALU.subtract, ALU.mult, reads=[tvt, tm], writes=[tvv])
                tt("dve", v_sb[:, tcn, :], v_sb[:, tcn, :], lng, ALU.mult, reads=[tvv, tlng], writes=[tvv])
                tt("dve", vn[:, tcn, :], v_sb[:, tcn, :], lnb, ALU.add, reads=[tvv, tlnb], pw=[tvn])
                tvt.r.extend(tvv.w + tvv.r)
            if ti == 0:
                dump("vn", vn, [tvn])
            if KCUT == 4:
                cut([tu, tvn]); continue
            P.newgen(tguv)
            for g in range(8):
                pb, tpb = psnext()
                for tcn in range(4):
                    mm(pb[:, tcn * 128:(tcn + 1) * 128], vn[:, tcn, g * 128:(g + 1) * 128], wsT[:, g, :], True, True, reads=[tvn, tws], writes=[tpb])
                (sq, tsq) = c.rsring.next()
                tt("dve", sq[:, :512], pb[:, :], bs[:, g, :], ALU.add, reads=[tpb, tbs], writes=[tsq])
                tt("dve", guv[:, g, :], sq[:, :512], u_sb[:, g, :], ALU.mult, reads=[tsq, tu], writes=[tguv_l[g]])
            if ti + 1 < len(tiles):
                y_nxt = norm_front(c, nxt_new[0], nxt_new[1], 512, (0 * 4 + 0) * 8)
            st = out_back(c, guv, tguv_l, 8, 512, a_wout_b, twout, (0 * 4 + 1) * 8, hb, th, 0, hout, s * S + t0)
            if ti + 1 < len(tiles):
                nxt = nxt_new

    def attn_layer(l, hin, hout):
        fox = (l == 2)
        P.barrier()
        A.reset()
        c = common_alloc(512, 8)
        win_b = c_win_b if fox else b_win_b
        twin = t_w["c_win" if fox else "b_win"]
        wout_b = c_wout_b if fox else b_wout_b
        twout = t_w["c_wout" if fox else "b_wout"]
        qk, v, ao = qk_s[l], v_s[l], ao_s[l]
        tqk = [T(const=True) for _ in range(NSEQ)]; tvs = [T(const=True) for _ in range(NSEQ)]; tao = [T(const=True) for _ in range(NSEQ)]
        tcqk = [T(const=True) for _ in range(NSEQ)]
        mark = A.off
        qko, tqko = A.tile([16, 512], BF16)
        vt, tvt = A.tile([4, 1024], BF16)
        if fox:
            nbf, tnbf = A.tile([1]); dma("sp", nbf[0:16, :], c_bf_d.ap(), "c", writes=[tnbf])
            ts("dve", nbf[0:16, :], nbf[0:16, :], -1.0, None, ALU.mult, ALU.bypass, reads=[tnbf], writes=[tnbf]); tnbf.const = True
            on16, ton16 = A.tile([512]); memset("pool", on16[0:16, :], 1.0, writes=[ton16]); ton16.const = True
            spb = Ring([A.tile([512]) for _ in range(2)])
            csum = Ring([A.tile([512]) for _ in range(2)])
            r1, tr1 = A.tile([512]); hf, thf = A.tile([512])
            cqt = Ring([A.tile([6, 512], BF16) for _ in range(2)]); ckt = Ring([A.tile([6, 512], BF16) for _ in range(2)])
            prev_cs = None
        tiles = [(s, t0) for s in range(NSEQ) for t0 in range(0, S, 512)]
        P.fence_dma("c")
        nxt = load_h(c, hin, 0, 0, 512, 0)
        y_nxt = norm_front(c, nxt[0], nxt[1], 512, (l * 4 + 0) * 8)
        for ti, (s, t0) in enumerate(tiles):
            hb, th = nxt
            if ti + 1 < len(tiles):
                s2, t02 = tiles[ti + 1]
                nxt = load_h(c, hin, s2 * S, t02, 512, 0)
            g0 = s * S + t0
            y, ty = y_nxt
            P.newgen(tqko)
            for oc in range(16):
                wv, twv = wload(c, win_b.ap()[:, :, oc * 128:(oc + 1) * 128], (8, 128), twin)
                pb, tpb = psnext()
                for k in range(8):
                    mm(pb[:, :], wv[:, k, :], y[:, k, :], k == 0, k == 7, reads=[twv, ty], writes=[tpb])
                if oc % 2 == 0:
                    act(qko[:, oc, :], pb[:, :], AF.Copy, reads=[tpb], pw=[tqko])
                else:
                    cp("dve", qko[:, oc, :], pb[:, :], reads=[tpb], pw=[tqko])
            dma("pool", qk.ap().rearrange("(c p) t -> p c t", p=128)[:, :, g0:g0 + 512], qko, "qsq", reads=[tqko], pw=[tqk[s]])
            y_cur_keep = (y, ty)
            if ti + 1 < len(tiles):
                y_nxt = norm_front(c, nxt[0], nxt[1], 512, (l * 4 + 0) * 8)
            P.newgen(tvt)
            for half in range(2):
                wv, twv = wload(c, win_b.ap()[:, :, 2048 + half * 512:2048 + (half + 1) * 512], (8, 512), twin)
                for tcn in range(4):
                    pb, tpb = psnext()
                    for k in range(8):
                        mm(pb[:, :], y[:, k, tcn * 128:(tcn + 1) * 128], wv[:, k, :], k == 0, k == 7, reads=[twv, ty], writes=[tpb])
                    if tcn % 2 == 0:
                        act(vt[:, tcn, half * 512:(half + 1) * 512], pb[:, :], AF.Copy, reads=[tpb], pw=[tvt])
                    else:
                        cp("dve", vt[:, tcn, half * 512:(half + 1) * 512], pb[:, :], reads=[tpb], pw=[tvt])
            dma("pool", v.ap()[g0:g0 + 512, :].rearrange("(n p) c -> p n c", p=128), vt, "qsv", reads=[tvt], pw=[tvs[s]])
            if fox:
                wv, twv = wload(c, win_b.ap()[:, :, 3072:3088], (8, 16), twin)
                pb, tpb = psnext()
                for k in range(8):
                    mm(pb[0:16, :], wv[:, k, :], y[:, k, :], k == 0, k == 7, reads=[twv, ty], writes=[tpb])
                (sp_, tsp) = spb.next()
                act(sp_[0:16, :], pb[0:16, :], AF.Exp, reads=[tpb, tnbf], writes=[tsp], scale=-1.0, bias=nbf[0:16, 0:1])
                act(sp_[0:16, :], sp_[0:16, :], AF.Ln, reads=[tsp], writes=[tsp], scale=1.0, bias=1.0)
                ts("dve", sp_[0:16, :], sp_[0:16, :], 8.0, None, ALU.mult, ALU.bypass, reads=[tsp], writes=[tsp])
                (cs, tcs) = csum.next()
                init = 0.0 if t0 == 0 else prev_cs[0][0:16, 511:512]
                rd = [tsp, ton16] + ([] if t0 == 0 else [prev_cs[1]])
                P.op("dve", lambda e, cs=cs, sp_=sp_, init=init: e.tensor_tensor_scan(cs[0:16, :], on16[0:16, :], sp_[0:16, :], init, ALU.mult, ALU.add), reads=rd, writes=[tcs])
                prev_cs = (cs, tcs)
                (cq, tcq) = cqt.next(); (ck, tck) = ckt.next()
                P.newgen(tcq); P.newgen(tck)
                cur, tcur = cs, tcs
                for part in range(3):
                    cp("dve", ck[0:16, 3 + part, :], cur[0:16, :], reads=[tcur], pw=[tck])
                    ts("dve", cq[0:16, part, :], ck[0:16, 3 + part, :], -1.0, None, ALU.mult, ALU.bypass, reads=[tck], pw=[tcq])
                    if part < 2:
                        cp("dve", hf[0:16, :], ck[0:16, 3 + part, :], reads=[tck], writes=[thf])
                        tt("dve", r1[0:16, :], cur[0:16, :], hf[0:16, :], ALU.subtract, reads=[tcur, thf], writes=[tr1])
                        cur, tcur = r1, tr1
                    memset("pool", ck[0:16, part, :], 1.0, pw=[tck])
                    memset("pool", cq[0:16, 3 + part, :], 1.0, pw=[tcq])
                dma("pool", cq_s.ap()[:, :, g0:g0 + 512], cq[0:16, :, :], f"qcq{tcq.slot}", reads=[tcq], pw=[tcqk[s]])
                dma("pool", ck_s.ap()[:, :, g0:g0 + 512], ck[0:16, :, :], f"qck{tck.slot}", reads=[tck], pw=[tcqk[s]])
        P.barrier()
        A.reset()
        c = common_alloc(512, 8, full=False)
        ps_pool[0] = [2, 3, 4, 5, 6, 7] if fox else [4, 5, 6, 7]
        acc_i = [0]
        NH = 16 if fox else 8
        qt_r = Ring([A.tile([S], BF16) for _ in range(2)])
        kt_r = Ring([A.tile([S], BF16) for _ in range(2)])
        va_r = Ring([A.tile([32, 128], BF16) for _ in range(2)])
        e_r = Ring([A.tile([512], BF16) for _ in range(6)])
        sb_r = Ring([A.tile([512]) for _ in range(3)])
        o_r = Ring([A.tile([512]) for _ in range(4)])
        aot_r = Ring([A.tile([512], BF16) for _ in range(3)])
        if fox:
            tri, ttri = A.tile([128], BF16)
            tri32, ttri32 = A.tile([128])
            dma("sp", tri32, tri_d.ap(), "c", writes=[ttri32])
            cp("dve", tri, tri32, reads=[ttri32], writes=[ttri]); ttri.const = True
            for (va, tva) in va_r.items:
                memset("pool", va[:, :, 64:128], 1.0, writes=[tva])
        else:
            lam, tlam = A.tile([256]); dma("sp", lam[0:1, :], b_lam_d.ap(), "c", writes=[tlam])
            sc, tsc = A.tile([8])
            tt("dve", lam[0:1, 0:64], lam[0:1, 0:64], lam[0:1, 64:128], ALU.mult, reads=[tlam], writes=[tlam])
            tt("dve", lam[0:1, 128:192], lam[0:1, 128:192], lam[0:1, 192:256], ALU.mult, reads=[tlam], writes=[tlam])
            P.op("dve", lambda e: e.reduce_sum(sc[0:1, 0:1], lam[0:1, 0:64], mybir.AxisListType.X), reads=[tlam], writes=[tsc])
            P.op("dve", lambda e: e.reduce_sum(sc[0:1, 1:2], lam[0:1, 128:192], mybir.AxisListType.X), reads=[tlam, tsc], writes=[tsc])
            act(sc[0:1, 0:2], sc[0:1, 0:2], AF.Exp, reads=[tsc], writes=[tsc])
            tt("dve", sc[0:1, 2:3], sc[0:1, 1:2], sc[0:1, 0:1], ALU.subtract, reads=[tsc], writes=[tsc])
            ts("dve", sc[0:1, 3:4], sc[0:1, 2:3], -LAMBDA_INIT, None, ALU.add, ALU.bypass, reads=[tsc], writes=[tsc])
            on32, ton32 = A.tile([128]); memset("pool", on32[0:1, :], 1.0, writes=[ton32])
            pb, tpb = psnext()
            mm(pb[:, 0:1], on32[0:1, :], sc[0:1, 3:4], True, True, reads=[ton32, tsc], writes=[tpb])
            nlam, tnl = A.tile([1]); cp("dve", nlam, pb[:, 0:1], reads=[tpb], writes=[tnl]); tnl.const = True
            subg, tsg = A.tile([1]); dma("sp", subg, b_subg_d.ap(), "c", writes=[tsg])
            ts("dve", subg, subg, 1.0 - LAMBDA_INIT, None, ALU.mult, ALU.bypass, reads=[tsg], writes=[tsg]); tsg.const = True
            relb, trb = A.tile([8]); dma("sp", relb[0:32, :], relb_d.ap(), "c", writes=[trb])
            oh, toh = A.tile([GVN]); dma("sp", oh[0:32, :], ohrev_d.ap(), "c", writes=[toh])
            gvt, tgvt = A.tile([GVN])
            for o in range(0, GVN, 384):
                pb, tpb = psnext()
                mm(pb[0:8, 0:384], relb[0:32, :], oh[0:32, o:o + 384], True, True, reads=[trb, toh], writes=[tpb])
                cp("dve", gvt[0:8, o:o + 384], pb[0:8, 0:384], reads=[tpb], pw=[tgvt])
            tgv = T(const=True)
            dma("pool", gv_s.ap(), gvt[0:8, :], "gv", reads=[tgvt], pw=[tgv])
            jf, tjf = A.tile([128]); dma("sp", jf, jflip_d.ap(), "c", writes=[tjf])
            P.fence_dma("c"); P.fence_dma("gv")
            tz, ttz = A.tile([8, 1024]); P.newgen(ttz)
            cb, tcb = A.tile([8]); P.newgen(tcb)
            hk, thk = A.tile([1024])
            for h in range(8):
                dma("sp", hk, bass.AP(gv_s, h * GVN, [[1, 128], [1, 1024]]), "c", reads=[tgv], writes=[thk])
                for o in range(2):
                    pb, tpb = psnext()
                    mm(pb[:, :], jf, hk[:, o * 512:(o + 1) * 512], True, True, reads=[tjf, thk], writes=[tpb])
                    cp("dve", tz[:, h, o * 512:(o + 1) * 512], pb[:, :], reads=[tpb], pw=[ttz])
                cp("dve", cb[:, h:h + 1], tz[:, h, 1023:1024], reads=[ttz], pw=[tcb])
            ttz.const = True; tcb.const = True
        P.fence_dma("c")
        AOV = ao.ap()

        P.barrier()
        NM = 8
        steps = []
        for s in range(NSEQ):
            for h in range(NH):
                for m in range(NM):
                    nkt = 4 * m + 4
                    for kk in range(nkt):
                        steps.append((s, h, m, kk, nkt))
        NSTEP = len(steps)
        heads = [(s, h) for s in range(NSEQ) for h in range(NH)]
        hbufs = {}

        def load_head(idx):
            if idx >= len(heads) or idx in hbufs:
                return
            s, h = heads[idx]
            (qt, tqt) = qt_r.next(); (kt, tkt) = kt_r.next(); (va, tva) = va_r.next()
            base = s * S
            if fox:
                P.newgen(tqt); P.newgen(tkt)
                dma("sp", qt[0:64, :], qk.ap()[h * 64:(h + 1) * 64, base:base + S], f"hq{tqt.slot}", reads=[tqk[s]], pw=[tqt])
                dma("sp", kt[0:64, :], qk.ap()[1024 + h * 64:1024 + (h + 1) * 64, base:base + S], f"hk{tkt.slot}", reads=[tqk[s]], pw=[tkt])
                dma("sp", qt[64:70, :], cq_s.ap()[h, :, base:base + S], f"hq{tqt.slot}", reads=[tcqk[s]], pw=[tqt])
                dma("sp", kt[64:70, :], ck_s.ap()[h, :, base:base + S], f"hk{tkt.slot}", reads=[tcqk[s]], pw=[tkt])
                P.newgen(tva)
                dma("sp", va[:, :, 0:64], v.ap()[base:base + S, h * 64:(h + 1) * 64].rearrange("(n p) c -> p n c", p=128), f"hv{tva.slot}", reads=[tvs[s]], pw=[tva])
            else:
                dma("sp", qt, qk.ap()[h * 128:(h + 1) * 128, base:base + S], f"hq{tqt.slot}", reads=[tqk[s]], writes=[tqt])
                dma("sp", kt, qk.ap()[1024 + h * 128:1024 + (h + 1) * 128, base:base + S], f"hk{tkt.slot}", reads=[tqk[s]], writes=[tkt])
                dma("sp", va, v.ap()[base:base + S, h * 128:(h + 1) * 128].rearrange("(n p) c -> p n c", p=128), f"hv{tva.slot}", reads=[tvs[s]], writes=[tva])
            hbufs[idx] = (qt, tqt, kt, tkt, va, tva)

        KR = 70 if fox else 64
        if fox:
            s_ring = Ring([(PSB[i], TPS[i]) for i in (2, 3, 4, 5, 6, 7)])
            acc_ring = Ring([(PSB[0], TPS[0]), (PSB[1], TPS[1])])
        else:
            pairs = [(PSALL[:, (4 + 2 * i) * 512:(6 + 2 * i) * 512].rearrange("p (a b) -> p a b", a=2), T()) for i in range(2)]
            po_pair = (PSALL[:, 0:1024].rearrange("p (a b) -> p a b", a=2), T())
            pl_pair = (PSALL[:, 1024:2048].rearrange("p (a b) -> p a b", a=2), T())
            s_ring = Ring(pairs)
            ones32, ton32b = A.tile([128]); memset("pool", ones32, 1.0, writes=[ton32b]); ton32b.const = True
            e2_r = Ring([A.tile([2, 512], BF16) for _ in range(4)])
            sb2_r = Ring([A.tile([2, 512]) for _ in range(2)])
            osb_r = Ring([A.tile([2, 512]) for _ in range(2)])
            r12_r = Ring([A.tile([2, 512]) for _ in range(2)])
        st_S = {}
        st_E = {}
        cur = {}
        pending = []

        def stage_A(g):
            s, h, m, kk, nkt = steps[g]
            hi = heads.index((s, h))
            if m == 0 and kk == 0:
                load_head(hi)
            qt, tqt, kt, tkt, va, tva = hbufs[hi]
            c0 = max(0, kk - 4 * m) * 128
            ks = slice(kk * 128, (kk + 1) * 128)
            qs = slice(m * 512 + c0, m * 512 + 512)
            (sp_, tsp_) = s_ring.next()
            if fox:
                mm(sp_[:, c0:512], kt[0:KR, ks], qt[0:KR, qs], True, True, reads=[tkt, tqt], writes=[tsp_])
            else:
                for mp in range(2):
                    rr = slice(mp * 64, (mp + 1) * 64)
                    mm(sp_[:, mp, c0:512], kt[rr, ks], qt[rr, qs], True, True, reads=[tkt, tqt], writes=[tsp_])
            st_S[g] = (sp_, tsp_)

        def stage_B(g):
            s, h, m, kk, nkt = steps[g]
            c0 = max(0, kk - 4 * m) * 128
            diag = kk >= 4 * m
            sp_, tsp_ = st_S.pop(g)
            if fox:
                (e1, te1) = e_r.next()
                act(e1[:, c0:512], sp_[:, c0:512], AF.Exp, reads=[tsp_], writes=[te1], scale=0.125)
                if diag:
                    tt("dve", e1[:, c0:c0 + 128], e1[:, c0:c0 + 128], tri, ALU.mult, reads=[te1, ttri], writes=[te1])
                st_E[g] = (e1, te1)
            else:
                near = kk >= 4 * m - 1
                (e2, te2) = e2_r.next()
                if near:
                    dd = 128 * (kk - 4 * m)
                    j0 = TZC - dd + c0
                    (sbt, tsb) = sb2_r.next()
                    P.newgen(tsb)
                    for mp in range(2):
                        stt(sbt[:, mp, c0:512], sp_[:, mp, c0:512], 0.125, tz[:, h, j0:j0 + 512 - c0], ALU.mult, ALU.add, reads=[tsp_, ttz], pw=[tsb])
                    if c0 == 0:
                        act(e2[:, :, :], sbt[:, :, :], AF.Exp, reads=[tsb], writes=[te2])
                    else:
                        P.newgen(te2)
                        for mp in range(2):
                            act(e2[:, mp, c0:512], sbt[:, mp, c0:512], AF.Exp, reads=[tsb], pw=[te2])
                else:
                    act(e2[:, :, c0:512], sp_[:, :, c0:512], AF.Exp, reads=[tsp_, tcb], writes=[te2], scale=0.125, bias=cb[:, h:h + 1])
                if diag:
                    tm_ = T()
                    tm_.w = list(te2.w)
                    for mp in range(2):
                        memset("pool", e2[64:128, mp, c0:c0 + 64], 0.0, writes=[tm_])
                        te2.w.extend(tm_.w)
                st_E[g] = (e2, te2)

        def stage_C(g):
            s, h, m, kk, nkt = steps[g]
            hi = heads.index((s, h))
            qt, tqt, kt, tkt, va, tva = hbufs[hi]
            c0 = max(0, kk - 4 * m) * 128
            e_, te_ = st_E.pop(g)
            if m == 0 and kk == 0:
                load_head(hi + 1)
            base = s * S
            g0 = base + m * 512
            if fox:
                if kk == 0:
                    cur["po"] = acc_ring.next()
                po, tpo = cur["po"]
                mm(po[:, c0:512], va[:, kk, :], e_[:, c0:512], kk == 0, kk == nkt - 1, reads=[tva, te_], writes=[tpo])
                if kk == nkt - 1:
                    (rl, trl) = o_r.next()
                    recip(rl[64:128, :], po[64:128, :], reads=[tpo], writes=[trl])
                    (aot, taot) = aot_r.next()
                    tt("dve", aot[0:64, :], po[0:64, :], rl[64:128, :], ALU.mult, reads=[tpo, trl], writes=[taot])
                    dma("pool", AOV[h * 64:(h + 1) * 64, g0:g0 + 512], aot[0:64, :], f"ao{taot.slot}", reads=[taot], pw=[tao[s]])
            else:
                po, tpo = po_pair
                pl, tpl = pl_pair
                for mp in range(2):
                    mm(po[:, mp, c0:512], va[:, kk, :], e_[:, mp, c0:512], kk == 0, kk == nkt - 1, reads=[tva, te_], writes=[tpo])
                    mm(pl[:, mp, c0:512], ones_bf[:], e_[:, mp, c0:512], kk == 0, kk == nkt - 1, reads=[t_ones, te_], writes=[tpl])
                if kk == nkt - 1:
                    (osb, tosb) = osb_r.next()
                    cp("dve", osb[:, :, :], po[:, :, :], reads=[tpo], writes=[tosb])
                    (r12, tr12) = r12_r.next()
                    P.op("dve", lambda e, r12=r12, pl=pl: e.reciprocal(r12[:, :, :], pl[:, :, :]), reads=[tpl], writes=[tr12])

                    (sq, tsq) = c.sqring.next()

                    def epi1(osb=osb, tosb=tosb, r12=r12, tr12=tr12, sq=sq, tsq=tsq):
                        tt("dve", r12[:, :, :], osb[:, :, :], r12[:, :, :], ALU.mult, reads=[tosb, tr12], writes=[tr12])
                        stt(r12[:, 0, :], r12[:, 1, :], nlam[:, 0:1], r12[:, 0, :], ALU.mult, ALU.add, reads=[tr12, tnl], writes=[tr12])
                        tt("dve", sq[:, :512], r12[:, 0, :], r12[:, 0, :], ALU.mult, reads=[tr12], writes=[tsq])

                    def epi2(h=h, s=s, g0=g0, r12=r12, tr12=tr12, sq=sq, tsq=tsq):
                        (pn2, tpn2) = s_ring.items[s_ring.i % len(s_ring.items)]
                        mm(pn2[:, 0, :], ones_bf[:], sq[:, :512], True, True, reads=[tsq, t_ones], writes=[tpn2])
                        (rs, trs) = c.rsring.next()
                        act(rs[:, :512], pn2[:, 0, :], AF.Ln, reads=[tpn2], writes=[trs], scale=1.0 / 128, bias=EPS)
                        act(rs[:, :512], rs[:, :512], AF.Exp, reads=[trs], writes=[trs], scale=-0.5)
                        (aot, taot) = aot_r.next()
                        stt(aot, r12[:, 0, :], subg[:, 0:1], rs[:, :512], ALU.mult, ALU.mult, reads=[tr12, trs, tsg], writes=[taot])
                        dma("pool", AOV[h * 128:(h + 1) * 128, g0:g0 + 512], aot, f"ao{taot.slot}", reads=[taot], pw=[tao[s]])
                    pending.append((g + 1, epi1))
                    pending.append((g + 3, epi2))

        DEPTH = 2
        for g in range(min(DEPTH, NSTEP)):
            stage_A(g)
        for g in range(NSTEP):
            stage_B(g)
            while pending and pending[0][0] <= g:
                pending.pop(0)[1]()
            if g + DEPTH < NSTEP:
                stage_A(g + DEPTH)
            stage_C(g)
        while pending:
            pending.pop(0)[1]()

        P.barrier()
        A.reset()
        ps_pool[0] = list(range(8))
        c = common_alloc(512, 8)
        aor = Ring([A.tile([8, 512], BF16) for _ in range(2)])
        for ti, (s, t0) in enumerate(tiles):
            g0 = s * S + t0
            hb, th = load_h(c, hin, s * S, t0, 512, 0)
            (at, tat) = aor.next()
            dma("sp", at, AOV.rearrange("(c p) t -> p c t", p=128)[:, :, g0:g0 + 512], f"al{tat.slot}", reads=[tao[s]], writes=[tat])
            st = out_back(c, at, tat, 8, 512, wout_b, twout, (l * 4 + 1) * 8, hb, th, 0, hout, g0)

    def rglru_layer(hin, hout):
        l = 3
        P.barrier()
        A.reset()
        c = common_alloc(459, 10, ny=2)
        kcs = []
        for oc in range(10):
            lo = (oc * 128) // 80 * 80
            hi = -(-((oc + 1) * 128) // 80) * 80
            kcs.append(list(range(lo // 128, min(10, -(-hi // 128)))))
        wr, twr = A.tile([10, 3, 128], BF16); wi, twi = A.tile([10, 3, 128], BF16)
        P.newgen(twr); P.newgen(twi)
        for oc in range(10):
            for i, k in enumerate(kcs[oc]):
                dma("sp", wr[:, oc, i, :], d_wr_b.ap()[:, k, oc * 128:(oc + 1) * 128], "c", reads=[t_w["d_wr"]], pw=[twr])
                dma("sp", wi[:, oc, i, :], d_wi_b.ap()[:, k, oc * 128:(oc + 1) * 128], "c", reads=[t_w["d_wi"]], pw=[twi])
        twr.const = True; twi.const = True
        cw, tcw = A.tile([10, 5]); dma("sp", cw, d_cw_d.ap(), "c", writes=[tcw]); tcw.const = True
        br, tbr = A.tile([10]); dma("sp", br, d_br_d.ap(), "c", writes=[tbr])
        bi, tbi = A.tile([10]); dma("sp", bi, d_bi_d.ap(), "c", writes=[tbi])
        ts("dve", br, br, -1.0, None, ALU.mult, ALU.bypass, reads=[tbr], writes=[tbr]); tbr.const = True
        ts("dve", bi, bi, -1.0, None, ALU.mult, ALU.bypass, reads=[tbi], writes=[tbi]); tbi.const = True
        cs1, tcs1 = A.tile([10]); dma("sp", cs1, d_lam_d.ap(), "c", writes=[tcs1])
        cs2, tcs2 = A.tile([10])
        act(cs1, cs1, AF.Exp, reads=[tcs1], writes=[tcs1], scale=-1.0)
        act(cs1, cs1, AF.Ln, reads=[tcs1], writes=[tcs1], scale=1.0, bias=1.0)
        ts("dve", cs1, cs1, -8.0, None, ALU.mult, ALU.bypass, reads=[tcs1], writes=[tcs1])
        ts("dve", cs2, cs1, 2.0, None, ALU.mult, ALU.bypass, reads=[tcs1], writes=[tcs2])
        tcs1.const = True; tcs2.const = True
        P.fence_dma("c")
        xc, txc = A.tile([10, 456]); xcb, txcb = A.tile([10, 456], BF16)
        gg, tgg = A.tile([10, 456], BF16)
        hh, thh = A.tile([10, 456])
        hst, thst = A.tile([10])
        yv, tyv = A.tile([10, 456], BF16)
        tyv_l = [T() for _ in range(10)]
        tmp_r = Ring([A.tile([456]) for _ in range(8)])
        tiles = [(s, t0, n) for s in range(NSEQ) for (t0, n) in seq_tiles(None, 456)]
        twin, twout = t_w["d_win"], t_w["d_wout"]
        nxt = load_h(c, hin, 0, 0, tiles[0][2], 3)
        prev = None
        c.rstd_exp = True
        y_nxt = norm_front(c, nxt[0], nxt[1], tiles[0][2] + 3, (l * 4 + 0) * 8)
        for ti, (s, t0, n) in enumerate(tiles):
            hb, th = nxt
            y, ty = y_nxt
            w = n + 3
            P.newgen(txc); P.newgen(txcb); P.newgen(tgg)
            for ch in range(10):
                wv, twv = wload(c, d_win_b.ap()[:, :, RW + ch * 128:RW + (ch + 1) * 128], (8, 128), twin)
                pb, tpb = psnext()
                for k in range(8):
                    mm(pb[:, :w], wv[:, k, :], y[:, k, :w], k == 0, k == 7, reads=[twv, ty], writes=[tpb])
                ta = T()
                act(xc[:, ch, :n], pb[:, 3:w], AF.Identity, reads=[tpb, tcw], writes=[ta], pw=[txc], scale=cw[:, ch, 3:4], bias=cw[:, ch, 4:5])
                for tap in range(3):
                    stt(xc[:, ch, :n], pb[:, tap:tap + n], cw[:, ch, tap:tap + 1], xc[:, ch, :n], ALU.mult, ALU.add, reads=[tpb, ta, tcw], writes=[ta])
                txc.w.extend(ta.w)
                act(xcb[:, ch, :n], xc[:, ch, :n], AF.Copy, reads=[ta], pw=[txcb])
                txc.r.extend(ta.r)
                wv, twv = wload(c, d_win_b.ap()[:, :, ch * 128:(ch + 1) * 128], (8, 128), twin)
                pb, tpb = psnext()
                for k in range(8):
                    mm(pb[:, :w], wv[:, k, :], y[:, k, :w], k == 0, k == 7, reads=[twv, ty], writes=[tpb])
                act(gg[:, ch, :n], pb[:, 3:w], AF.Gelu_apprx_tanh, reads=[tpb], pw=[tgg])
            if ti + 1 < len(tiles):
                s2, t02, n2 = tiles[ti + 1]
                nxt_new = load_h(c, hin, s2 * S, t02, n2, 3)
            P.newgen(thh); P.newgen(tyv)
            for oc in range(10):
                pr, tpr = psnext()
                for i, k in enumerate(kcs[oc]):
                    mm(pr[:, :n], wr[:, oc, i, :], xcb[:, k, :n], i == 0, i == len(kcs[oc]) - 1, reads=[twr, txcb], writes=[tpr])
                pi, tpi = psnext()
                for i, k in enumerate(kcs[oc]):
                    mm(pi[:, :n], wi[:, oc, i, :], xcb[:, k, :n], i == 0, i == len(kcs[oc]) - 1, reads=[twi, txcb], writes=[tpi])
                (r_, tr_) = tmp_r.next(); (i_, ti_) = tmp_r.next(); (a_, ta_) = tmp_r.next(); (m_, tm_) = tmp_r.next()
                act(r_[:, :n], pr[:, :n], AF.Exp, reads=[tpr, tbr], writes=[tr_], scale=-1.0, bias=br[:, oc:oc + 1])
                act(i_[:, :n], pi[:, :n], AF.Exp, reads=[tpi, tbi], writes=[ti_], scale=-1.0, bias=bi[:, oc:oc + 1])
                act(r_[:, :n], r_[:, :n], AF.Ln, reads=[tr_], writes=[tr_], scale=1.0, bias=1.0)
                act(i_[:, :n], i_[:, :n], AF.Ln, reads=[ti_], writes=[ti_], scale=1.0, bias=1.0)
                act(r_[:, :n], r_[:, :n], AF.Exp, reads=[tr_], writes=[tr_], scale=-1.0)
                act(i_[:, :n], i_[:, :n], AF.Exp, reads=[ti_], writes=[ti_], scale=-1.0)
                act(a_[:, :n], r_[:, :n], AF.Exp, reads=[tr_, tcs1], writes=[ta_], scale=cs1[:, oc:oc + 1])
                act(m_[:, :n], r_[:, :n], AF.Exp, reads=[tr_, tcs2], writes=[tm_], scale=cs2[:, oc:oc + 1])
                act(m_[:, :n], m_[:, :n], AF.Ln, reads=[tm_], writes=[tm_], scale=-1.0, bias=1.0)
                act(m_[:, :n], m_[:, :n], AF.Exp, reads=[tm_], writes=[tm_], scale=0.5)
                tt("dve", i_[:, :n], i_[:, :n], xc[:, oc, :n], ALU.mult, reads=[ti_, txc], writes=[ti_])
                tt("dve", i_[:, :n], i_[:, :n], m_[:, :n], ALU.mult, reads=[ti_, tm_], writes=[ti_])
                if t0 == 0:
                    init, rd = 0.0, []
                else:
                    init, rd = hst[:, oc:oc + 1], [thst]
                tsc_ = T()
                P.op("dve", lambda e, oc=oc, a_=a_, i_=i_, init=init, n=n: e.tensor_tensor_scan(hh[:, oc, :n], a_[:, :n], i_[:, :n], init, ALU.mult, ALU.add),
                     reads=[ta_, ti_] + rd, writes=[tsc_])
                thh.w.extend(tsc_.w)
                cp("dve", hst[:, oc:oc + 1], hh[:, oc, n - 1:n], reads=[tsc_], pw=[thst])
                tt("dve", yv[:, oc, :n], hh[:, oc, :n], gg[:, oc, :n], ALU.mult, reads=[tsc_, tgg], writes=[tyv_l[oc]])
            if ti + 1 < len(tiles):
                y_nxt = norm_front(c, nxt_new[0], nxt_new[1], tiles[ti + 1][2] + 3, (l * 4 + 0) * 8)
            st = out_back(c, yv, tyv_l, 10, n, d_wout_b, twout, (l * 4 + 1) * 8, hb, th, 3, hout, s * S + t0)
            if ti + 1 < len(tiles):
                nxt = nxt_new

    seqn = []
    for l in range(nlayers):
        seqn += [("mix", l), ("ffn", l)]
    if nsub is not None:
        seqn = seqn[:nsub]
    bufs = [xT] + [hbuf[i] for i in range(len(seqn) - 1)] + [outT]
    for i, (kind, l) in enumerate(seqn):
        hin, hout = bufs[i], bufs[i + 1]
        final.clear()
        if kind == "ffn":
            ffn_layer(l, hin, hout)
        elif l == 0:
            gmlp_layer(hin, hout)
        elif l in (1, 2):
            attn_layer(l, hin, hout)
        else:
            rglru_layer(hin, hout)
    P.emit(final_waits=list(final) + dumps)
    nc._in_names = list(ins.keys())
    return nc


def _t5_bucket(rel):
    half, max_exact = 16, 8
    n = np.abs(rel)
    ret = np.where(rel > 0, half, 0)
    nf = np.maximum(n, 1).astype(np.float32)
    large = max_exact + (np.log(nf / np.float32(max_exact)) / np.float32(math.log(128 / max_exact)) * (half - max_exact)).astype(np.int32)
    large = np.minimum(large, half - 1)
    return ret + np.where(n < max_exact, n, large)


def _kc(w):
    K, N = w.shape
    return np.ascontiguousarray(w.reshape(K // 128, 128, N).transpose(1, 0, 2))


def _col(v, nch):
    return np.ascontiguousarray(v.reshape(nch, 128).T)


def prep_shared(inp):
    f = np.float32
    m = {}
    ngx = inp["norm_g"]
    m["ng"] = np.ascontiguousarray(ngx.reshape(4, 4, 8, 128).transpose(3, 0, 1, 2).reshape(128, 128)).astype(f)
    perm = np.concatenate([np.concatenate([np.arange(j * 128, (j + 1) * 128), DFF + np.arange(j * 128, (j + 1) * 128)]) for j in range(22)])
    for l in range(4):
        m[f"wup{l}"] = _kc(inp["ffn_w_up"][l][:, perm])
        m[f"wdn{l}"] = _kc(inp["ffn_w_down"][l])
        cwv = inp["ffn_conv_w"][l]
        cb = inp["ffn_conv_b"][l]
        arr = np.concatenate([cwv, cb[None]], 0)
        m[f"fcw{l}"] = np.ascontiguousarray(arr.reshape(4, 44, 128).transpose(2, 1, 0)).astype(f)
    m["a_win"] = _kc(inp["a_w_in"][0])
    m["a_lng"] = np.ascontiguousarray(np.broadcast_to(inp["a_ln_g"][0][None], (128, 1024))).astype(f)
    m["a_lnb"] = np.ascontiguousarray(np.broadcast_to(inp["a_ln_b"][0][None], (128, 1024))).astype(f)
    m["a_wsT"] = np.ascontiguousarray(inp["a_w_s"][0].transpose(2, 0, 1)).astype(f)
    p = np.arange(128)
    m["a_mask"] = ((p[:, None] // 64) <= (p[None, :] // 64)).astype(f)
    m["a_bs"] = np.ascontiguousarray(np.broadcast_to(np.tile(inp["a_b_s"][0], (1, 4))[None], (128, 8, 512))).astype(f)
    m["a_wout"] = _kc(inp["a_w_out"][0])
    m["b_win"] = _kc(inp["b_w_in"][0])
    m["b_lam"] = np.ascontiguousarray(inp["b_lam"][0].reshape(1, 256)).astype(f)
    m["b_subg"] = np.ascontiguousarray(inp["b_sub_g"][0].reshape(128, 1)).astype(f)
    m["b_wout"] = _kc(inp["b_w_out"][0])
    m["relb"] = np.ascontiguousarray(inp["rel_bias"]).astype(f)
    mmv = np.arange(GVN)
    bk = _t5_bucket(127 + TZC - mmv)
    oh = np.zeros((32, GVN), f)
    oh[bk, mmv] = 1.0
    m["ohrev"] = oh
    m["jflip"] = np.ascontiguousarray(np.eye(128, dtype=f)[::-1])
    m["c_win"] = _kc(inp["c_w_in"][0])
    m["c_bf"] = np.ascontiguousarray(inp["c_b_f"][0].reshape(16, 1)).astype(f)
    m["c_wout"] = _kc(inp["c_w_out"][0])
    m["tri"] = (p[:, None] <= p[None, :]).astype(f)
    m["d_win"] = _kc(inp["d_w_in"][0])
    cwd = np.concatenate([inp["d_conv_w"][0], inp["d_conv_b"][0][None]], 0)
    m["d_cw"] = np.ascontiguousarray(cwd.reshape(5, 10, 128).transpose(2, 1, 0)).astype(f)
    for nm, key in (("d_wr", "d_w_r"), ("d_wi", "d_w_i")):
        dense = np.zeros((RW, RW), f)
        for n in range(16):
            dense[n * 80:(n + 1) * 80, n * 80:(n + 1) * 80] = inp[key][0][n]
        m[nm] = _kc(dense)
    m["d_br"] = _col(inp["d_b_r"][0], 10).astype(f)
    m["d_bi"] = _col(inp["d_b_i"][0], 10).astype(f)
    m["d_lam"] = _col(inp["d_lam"][0], 10).astype(f)
    m["d_wout"] = _kc(inp["d_w_out"][0])
    return m


_NC_CACHE = {}


def run(inputs, nlayers=4, trace=False, nsub=None):
    inp = {k: np.asarray(v) for k, v in inputs.items()}
    if (nlayers, nsub) not in _NC_CACHE:
        _NC_CACHE[(nlayers, nsub)] = build(nlayers, nsub)
    nc = _NC_CACHE[(nlayers, nsub)]
    shared = prep_shared(inp)
    shared = {k: v for k, v in shared.items() if k in nc._in_names}
    x = inp["x"]
    in_maps = []
    for cidx in range(NCORES):
        xs = x[cidx * NSEQ:(cidx + 1) * NSEQ].reshape(TC, D)
        mcore = dict(shared)
        mcore["xT"] = np.ascontiguousarray(xs.T)
        in_maps.append(mcore)
    res = run_bass_kernel_spmd(nc, in_maps, core_ids=list(range(NCORES)), **({"trace": True} if trace else {}))
    out = np.empty((16, S, D), np.float32)
    for cidx in range(NCORES):
        out[cidx * NSEQ:(cidx + 1) * NSEQ] = res.results[cidx]["outT"].T.reshape(NSEQ, S, D)
    return out, res


def kernel(**inputs):
    out, _ = run(inputs, 4)
    return out
```

```python
import contextlib
import math
import os
KCUT = int(os.environ.get('KCUT', '99'))
KDUMP = int(os.environ.get('KDUMP', '0'))
import numpy as np
import concourse.bass as bass
import concourse.mybir as mybir
from concourse.bass_utils import run_bass_kernel_spmd

F32 = mybir.dt.float32
BF16 = mybir.dt.bfloat16
AF = mybir.ActivationFunctionType
ALU = mybir.AluOpType

NCORES = 8
D = 1024
S = 4096
NSEQ = 2
TC = S * NSEQ
EPS = 1e-6
DFF = 2816
RW = 1280
SEM_CAP = 6000
LAMBDA_INIT = 0.8 - 0.6 * math.exp(-0.3 * 1)
TZC = 384
GVN = 1152


class T:
    __slots__ = ("w", "r", "prev", "const", "slot")

    def __init__(self, const=False):
        self.slot = 0
        self.w = []
        self.r = []
        self.prev = []
        self.const = const


class Prog:
    ENGS = ("pe", "act", "dve", "pool", "sp")

    def __init__(self, nc):
        self.nc = nc
        self.ins = {e: [] for e in self.ENGS}
        self.dma_cnt = {}
        self.last_real = {}
        self.stack = contextlib.ExitStack()

    def newgen(self, t):
        t.prev = t.w + t.r
        t.w = []
        t.r = []

    def op(self, eng, fn, reads=(), writes=(), pw=(), dma=None):
        me_idx = len(self.ins[eng])
        if dma:
            prod = ("dma:" + dma, self.dma_cnt.get(dma, 0))
            self.dma_cnt[dma] = prod[1] + 1
        else:
            prod = (eng, me_idx)
        deps = set()
        raw = set()
        for t in reads:
            deps.update(t.w)
            raw.update(t.w)
        for t in writes:
            deps.update(t.w)
            deps.update(t.r)
            deps.update(t.prev)
        for t in pw:
            deps.update(t.prev)
        pruned = []
        for d in deps:
            if d[0] == eng and not dma:
                if eng == "pe" or d not in raw:
                    continue
            pruned.append(d)
        self.ins[eng].append([fn, pruned, False, dma])
        if not dma:
            self.last_real[eng] = me_idx
        for t in writes:
            t.w = [prod]
            t.r = []
            t.prev = []
        for t in pw:
            t.w.append(prod)
        for t in reads:
            if not t.const and prod not in t.w:
                t.r.append(prod)
        return prod

    def fence_dma(self, key):
        c = self.dma_cnt.get(key, 0)
        if c:
            for e in self.ENGS:
                self.ins[e].append([None, [("dma:" + key, c - 1)], False, None])

    def barrier(self):
        lasts = [(e, i) for e, i in self.last_real.items()]
        dmas = [("dma:" + k, c - 1) for k, c in self.dma_cnt.items() if c > 0]
        for e in self.ENGS:
            deps = [d for d in lasts if d[0] != e] + dmas
            self.ins[e].append([None, deps, False, None])

    def emit(self, final_waits=()):
        nc = self.nc
        for e in self.ENGS:
            for ins in self.ins[e]:
                for d in ins[1]:
                    if not d[0].startswith("dma:"):
                        self.ins[d[0]][d[1]][2] = True
        signum = {}
        for e in self.ENGS:
            c = 0
            for i, ins in enumerate(self.ins[e]):
                if ins[2]:
                    signum[(e, i)] = c
                    c += 1
        sems = {}

        def getsem(key):
            if key not in sems:
                sems[key] = self.stack.enter_context(nc.semaphore(key.replace(":", "_")))
            return sems[key]

        dcap = SEM_CAP // 16

        def wait_target(d):
            if d[0].startswith("dma:"):
                j = d[1]
                return (getsem(f"{d[0]}_{j // dcap}"), 16 * (j % dcap + 1))
            j = signum[d]
            return (getsem(f"e_{d[0]}_{j // SEM_CAP}"), j % SEM_CAP + 1)

        for e in self.ENGS:
            dcount = {}
            for i, ins in enumerate(self.ins[e]):
                for d in ins[1]:
                    wait_target(d)
                if ins[2]:
                    wait_target((e, i))
                if ins[3]:
                    k = ins[3]
                    wait_target(("dma:" + k, dcount.get(k, 0)))
                    dcount[k] = dcount.get(k, 0) + 1
        for d in final_waits:
            wait_target(d)
        prog = self

        def run_engine(ename, eng, extra_waits=()):
            seen = {}
            dcount = {}
            for i, (fn, deps, sig, dma) in enumerate(prog.ins[ename]):
                wl = {}
                for d in deps:
                    s, v = wait_target(d)
                    if seen.get(s.name, 0) >= v:
                        continue
                    if wl.get(s.name, (None, 0))[1] < v:
                        wl[s.name] = (s, v)
                for s, v in wl.values():
                    eng.wait_ge(s, v)
                    seen[s.name] = v
                if fn is None:
                    continue
                instr = fn(eng)
                if dma:
                    j = dcount.get(dma, 0)
                    dcount[dma] = j + 1
                    s, v = wait_target(("dma:" + dma, j))
                    instr.then_inc(s, 16)
                elif sig:
                    s, v = wait_target((ename, i))
                    instr.then_inc(s, 1)
            for d in extra_waits:
                s, v = wait_target(d)
                eng.wait_ge(s, v)

        with nc.Block() as block:
            @block.tensor
            def _(eng):
                run_engine("pe", eng)

            @block.scalar
            def _(eng):
                run_engine("act", eng)

            @block.vector
            def _(eng):
                run_engine("dve", eng)

            @block.gpsimd
            def _(eng):
                run_engine("pool", eng, extra_waits=final_waits)

            @block.sync
            def _(eng):
                run_engine("sp", eng)
        self.stack.close()


class Ring:
    def __init__(self, items):
        self.items = items
        self.i = 0
        for k, it in enumerate(items):
            if isinstance(it, tuple) and isinstance(it[1], T):
                it[1].slot = k

    def next(self):
        it = self.items[self.i % len(self.items)]
        self.i += 1
        return it


def seq_tiles(n_full, width):
    out = []
    t = 0
    while t < S:
        n = min(width, S - t)
        out.append((t, n))
        t += n
    return out


def build(nlayers=4, nsub=None):
    nc = bass.Bass("TRN2", target_bir_lowering=False)
    P = Prog(nc)
    ins = {}

    def din(name, shape, dt=F32):
        ins[name] = nc.dram_tensor(name, list(shape), dt, kind="ExternalInput")
        return ins[name]

    xT = din("xT", [D, TC])
    ng_d = din("ng", [128, 128])
    NLD = max(nlayers, 1)
    wup_d = [din(f"wup{l}", [128, 8, 2 * DFF]) for l in range(NLD)]
    wdn_d = [din(f"wdn{l}", [128, 22, D]) for l in range(NLD)]
    fcw_d = [din(f"fcw{l}", [128, 44, 4]) for l in range(NLD)]
    a_win_d = din("a_win", [128, 8, 2048]); a_lng_d = din("a_lng", [128, 1024]); a_lnb_d = din("a_lnb", [128, 1024])
    a_wsT_d = din("a_wsT", [128, 8, 128]); a_mask_d = din("a_mask", [128, 128]); a_bs_d = din("a_bs", [128, 8, 512])
    a_wout_d = din("a_wout", [128, 8, D])
    if nlayers > 1:
      b_win_d = din("b_win", [128, 8, 3072]); b_lam_d = din("b_lam", [1, 256]); b_subg_d = din("b_subg", [128, 1])
      b_wout_d = din("b_wout", [128, 8, D]); relb_d = din("relb", [32, 8]); ohrev_d = din("ohrev", [32, GVN]); jflip_d = din("jflip", [128, 128])
    if nlayers > 2:
      c_win_d = din("c_win", [128, 8, 3088]); c_bf_d = din("c_bf", [16, 1]); c_wout_d = din("c_wout", [128, 8, D]); tri_d = din("tri", [128, 128])
    if nlayers > 3:
      d_win_d = din("d_win", [128, 8, 2 * RW]); d_cw_d = din("d_cw", [128, 10, 5]); d_wr_d = din("d_wr", [128, 10, RW]); d_wi_d = din("d_wi", [128, 10, RW])
      d_br_d = din("d_br", [128, 10]); d_bi_d = din("d_bi", [128, 10]); d_lam_d = din("d_lam", [128, 10]); d_wout_d = din("d_wout", [128, 10, D])
    outT = nc.dram_tensor("outT", [D, TC], F32, kind="ExternalOutput")

    def dscr(name, shape, dt):
        return nc.dram_tensor(name, list(shape), dt)

    wup_b = [dscr(f"wupb{l}", [128, 8, 2 * DFF], BF16) for l in range(4)]
    wdn_b = [dscr(f"wdnb{l}", [128, 22, D], BF16) for l in range(4)]
    a_win_b = dscr("a_winb", [128, 8, 2048], BF16); a_wout_b = dscr("a_woutb", [128, 8, D], BF16)
    b_win_b = dscr("b_winb", [128, 8, 3072], BF16); b_wout_b = dscr("b_woutb", [128, 8, D], BF16)
    c_win_b = dscr("c_winb", [128, 8, 3088], BF16); c_wout_b = dscr("c_woutb", [128, 8, D], BF16)
    d_win_b = dscr("d_winb", [128, 8, 2 * RW], BF16); d_wout_b = dscr("d_woutb", [128, 10, D], BF16)
    d_wr_b = dscr("d_wrb", [128, 10, RW], BF16); d_wi_b = dscr("d_wib", [128, 10, RW], BF16)
    hbuf = [dscr(f"h{i}", [D, TC], F32) for i in range(7)]
    qk_s = {l: dscr(f"qk{l}", [2048, TC], BF16) for l in (1, 2)}
    v_s = {l: dscr(f"v{l}", [TC, 1024], BF16) for l in (1, 2)}
    ao_s = {l: dscr(f"ao{l}", [D, TC], BF16) for l in (1, 2)}
    cq_s = dscr("cq", [16, 6, TC], BF16); ck_s = dscr("ck", [16, 6, TC], BF16)
    gv_s = dscr("gv", [8, GVN], F32)

    sb = lambda name, shape, dt=F32: P.stack.enter_context(nc.sbuf_tensor(name, list(shape), dt))
    ARENA_N = 52400
    arena = sb("arena", [128, ARENA_N])
    ones_bf = sb("ones_bf", [128, 128], BF16); t_ones = T(const=True)
    ng = sb("ngs", [128, 128]); t_ng = T(const=True)
    PSALL = P.stack.enter_context(nc.psum_tensor("psall", [128, 4096], F32))
    PSB = [PSALL[:, i * 512:(i + 1) * 512] for i in range(8)]
    TPS = [T() for _ in range(8)]

    class Arena:
        def __init__(self):
            self.off = 0

        def reset(self):
            self.off = 0

        def alloc(self, n, dt=F32):
            n32 = n if dt == F32 else (n + 1) // 2
            a = arena[:, self.off:self.off + n32]
            self.off += n32
            assert self.off <= ARENA_N, self.off
            return a if dt == F32 else a.bitcast(BF16)[:, :n]

        def tile(self, shape, dt=F32):
            n = int(np.prod(shape))
            a = self.alloc(n, dt)
            if len(shape) == 2:
                a = a.rearrange("p (a b) -> p a b", a=shape[0])
            elif len(shape) == 3:
                a = a.rearrange("p (a b c) -> p a b c", a=shape[0], b=shape[1])
            return a, T()

    A = Arena()

    def act(out, in_, func, reads, writes=(), pw=(), **kw):
        return P.op("act", lambda e: e.activation(out, in_, func, **kw), reads, writes, pw)

    def mm(out, lhsT, rhs, start, stop, reads, writes):
        return P.op("pe", lambda e: e.matmul(out, lhsT, rhs, start=start, stop=stop), reads, writes)

    def tt(eng, out, in0, in1, op, reads, writes=(), pw=()):
        return P.op(eng, lambda e: e.tensor_tensor(out, in0, in1, op), reads, writes, pw)

    def stt(out, in0, scalar, in1, op0, op1, reads, writes=(), pw=()):
        return P.op("dve", lambda e: e.scalar_tensor_tensor(out, in0, scalar, in1, op0, op1), reads, writes, pw)

    def ts(eng, out, in0, s1, s2, op0, op1, reads, writes=(), pw=()):
        if s2 is None:
            return P.op(eng, lambda e: e.tensor_scalar(out, in0, s1, None, op0), reads, writes, pw)
        return P.op(eng, lambda e: e.tensor_scalar(out, in0, s1, s2, op0, op1), reads, writes, pw)

    def cp(eng, out, in_, reads, writes=(), pw=()):
        return P.op(eng, lambda e: e.tensor_copy(out, in_), reads, writes, pw)

    def recip(out, in_, reads, writes=(), pw=()):
        return P.op("dve", lambda e: e.reciprocal(out, in_), reads, writes, pw)

    def memset(eng, ap, val, writes=(), pw=()):
        return P.op(eng, lambda e: e.memset(ap, val), (), writes, pw)

    def dma(q, out, in_, key, reads=(), writes=(), pw=()):
        return P.op(q, lambda e: e.dma_start(out=out, in_=in_), reads, writes, pw, dma=key)

    dumps = []

    def dump(name, ap, reads):
        if not KDUMP:
            return
        d = nc.dram_tensor("dbg_" + name, list(ap.shape), ap.dtype, kind="ExternalOutput")
        dumps.append(dma("pool", d.ap(), ap, "dbg", reads=reads))

    ps_i = [0]
    ps_pool = [list(range(8))]

    def psnext(excl=None):
        pool = ps_pool[0]
        while True:
            i = pool[ps_i[0] % len(pool)]
            ps_i[0] += 1
            if excl is None or PSB[i] is not excl:
                return PSB[i], TPS[i]

    memset("pool", ones_bf[:], 1.0, writes=[t_ones])
    dma("sp", ng[:], ng_d.ap(), "c", writes=[t_ng])

    t_w = {}

    def cast_weight(src, dst, kc, n, name):
        t = T(const=True)
        t_w[name] = t
        total = kc * n
        sflat = src.ap().rearrange("p a b -> p (a b)")
        dflat = dst.ap().rearrange("p a b -> p (a b)")
        PIECE = 2048
        for o in range(0, total, PIECE):
            m = min(PIECE, total - o)
            (s32, ts32) = stg32.next()
            (s16, ts16) = stg16.next()
            dma("sp", s32[:, :m], sflat[:, o:o + m], f"cw{ts32.slot}", writes=[ts32])
            eng = cast_eng.next()
            if eng == "act":
                act(s16[:, :m], s32[:, :m], AF.Copy, reads=[ts32], writes=[ts16])
            else:
                cp(eng, s16[:, :m], s32[:, :m], reads=[ts32], writes=[ts16])
            dma("pool", dflat[:, o:o + m], s16[:, :m], f"cs{ts16.slot}", reads=[ts16], pw=[t])

    A.reset()
    stg32 = Ring([(A.alloc(2048), T()) for _ in range(4)])
    stg16 = Ring([(A.alloc(2048, BF16), T()) for _ in range(4)])
    cast_eng = Ring(["dve", "act", "pool"])
    cast_list = [(a_win_d, a_win_b, 8, 2048, "a_win"), (a_wout_d, a_wout_b, 8, D, "a_wout"), (wup_d[0], wup_b[0], 8, 2 * DFF, "wup0"), (wdn_d[0], wdn_b[0], 22, D, "wdn0")]
    if nlayers > 1:
        cast_list += [(b_win_d, b_win_b, 8, 3072, "b_win"), (b_wout_d, b_wout_b, 8, D, "b_wout"), (wup_d[1], wup_b[1], 8, 2 * DFF, "wup1"), (wdn_d[1], wdn_b[1], 22, D, "wdn1")]
    if nlayers > 2:
        cast_list += [(c_win_d, c_win_b, 8, 3088, "c_win"), (c_wout_d, c_wout_b, 8, D, "c_wout"), (wup_d[2], wup_b[2], 8, 2 * DFF, "wup2"), (wdn_d[2], wdn_b[2], 22, D, "wdn2")]
    if nlayers > 3:
        cast_list += [(d_win_d, d_win_b, 8, 2 * RW, "d_win"), (d_wout_d, d_wout_b, 10, D, "d_wout"), (d_wr_d, d_wr_b, 10, RW, "d_wr"), (d_wi_d, d_wi_b, 10, RW, "d_wi"),
                      (wup_d[3], wup_b[3], 8, 2 * DFF, "wup3"), (wdn_d[3], wdn_b[3], 22, D, "wdn3")]
    for c in cast_list:
        cast_weight(*c)

    def hview(buf, a, b):
        return buf.ap().rearrange("(c p) t -> p c t", p=128)[:, :, a:b]

    class Ctx:
        pass

    def common_alloc(wmax, kc_back, nh=2, ny=2, nyo=2, full=True):
        c = Ctx()
        if full:
            c.wring = Ring([A.tile([4096], BF16) for _ in range(4)])
            c.hring = Ring([A.tile([8, wmax]) for _ in range(nh)])
            c.yring = Ring([A.tile([8, wmax], BF16) for _ in range(ny)])
            c.yoring = Ring([A.tile([8, wmax]) for _ in range(nyo)])
        c.sqring = Ring([A.tile([wmax], BF16) for _ in range(3)])
        c.rsring = Ring([A.tile([wmax]) for _ in range(2)])
        return c

    def wload(c, src_ap, shape, tw):
        (wt, twt) = c.wring.next()
        a, b = shape
        view = wt[:, :a * b].rearrange("p (a b) -> p a b", a=a)
        dma("sp", view, src_ap, f"w{twt.slot}", reads=[tw], writes=[twt])
        return view, twt

    def load_h(c, src, base, t0, n, halo):
        (hb, th) = c.hring.next()
        w = n + halo
        if halo and t0 == 0:
            P.newgen(th)
            memset("pool", hb[:, :, 0:halo], 0.0, pw=[th])
            dma("sp", hb[:, :, halo:w], hview(src, base, base + n), f"h{th.slot}", reads=[t_h[src.name]], pw=[th])
        else:
            dma("sp", hb[:, :, 0:w], hview(src, base + t0 - halo, base + t0 + n), f"h{th.slot}", reads=[t_h[src.name]], writes=[th])
        return hb, th

    def rstd_from_ps(c, pb, tpb, w, scale):
        (rs, trs) = c.rsring.next()
        if getattr(c, "rstd_exp", True):
            act(rs[:, :w], pb[:, :w], AF.Ln, reads=[tpb], writes=[trs], scale=scale, bias=EPS)
            act(rs[:, :w], rs[:, :w], AF.Exp, reads=[trs], writes=[trs], scale=-0.5)
        else:
            act(rs[:, :w], pb[:, :w], AF.Sqrt, reads=[tpb], writes=[trs], scale=scale, bias=EPS)
            recip(rs[:, :w], rs[:, :w], reads=[trs], writes=[trs])
        return rs, trs

    def norm_front(c, hb, th, w, gcol):
        (y, ty) = c.yring.next()
        P.newgen(ty)
        pb, tpb = psnext()
        for ch in range(8):
            (sq, tsq) = c.sqring.next()
            act(sq[:, :w], hb[:, ch, :w], AF.Square, reads=[th], writes=[tsq])
            mm(pb[:, :w], ones_bf[:], sq[:, :w], ch == 0, ch == 7, reads=[tsq, t_ones], writes=[tpb])
        rs, trs = rstd_from_ps(c, pb, tpb, w, 1.0 / D)
        for ch in range(8):
            stt(y[:, ch, :w], hb[:, ch, :w], ng[:, gcol + ch:gcol + ch + 1], rs[:, :w], ALU.mult, ALU.mult, reads=[th, trs, t_ng], pw=[ty])
        return y, ty

    dbg_ob = [False]

    def out_back(c, src, tsrc, kc, n, wsrc, tw, gcol, hb, th, halo, dst, dbase, defer=False):
        (yo, tyo) = c.yoring.next()
        P.newgen(tyo)
        pn, tpn = psnext()
        prev_sq = None
        for oc in range(8):
            wv, twv = wload(c, wsrc.ap()[:, :, oc * 128:(oc + 1) * 128], (kc, 128), tw)
            pb, tpb = psnext(excl=pn)
            for k in range(kc):
                tk = tsrc[k] if isinstance(tsrc, list) else tsrc
                mm(pb[:, :n], wv[:, k, :], src[:, k, :n], k == 0, k == kc - 1, reads=[twv, tk], writes=[tpb])
            if prev_sq is not None:
                mm(pn[:, :n], ones_bf[:], prev_sq[0][:, :n], prev_sq[2] == 0, False, reads=[prev_sq[1], t_ones], writes=[tpn])
            act(yo[:, oc, :n], pb[:, :n], AF.Copy, reads=[tpb], pw=[tyo])
            (sq, tsq) = c.sqring.next()
            act(sq[:, :n], pb[:, :n], AF.Square, reads=[tpb], writes=[tsq])
            prev_sq = (sq, tsq, oc)
        mm(pn[:, :n], ones_bf[:], prev_sq[0][:, :n], False, True, reads=[prev_sq[1], t_ones], writes=[tpn])
        rs, trs = rstd_from_ps(c, pn, tpn, n, 1.0 / D)
        tyo2 = T()
        tyo3 = T()

        def piece(oc):
            tt("pool" if defer else "dve", yo[:, oc, :n], yo[:, oc, :n], rs[:, :n], ALU.mult, reads=[tyo, trs], pw=[tyo2])
            stt(yo[:, oc, :n], yo[:, oc, :n], ng[:, gcol + oc:gcol + oc + 1], hb[:, oc, halo:halo + n], ALU.mult, ALU.add, reads=[tyo2, th, t_ng], pw=[tyo3])

        def store():
            st = dma("pool", hview(dst, dbase, dbase + n), yo[:, :, :n], f"hs{tyo.slot}", reads=[tyo3], pw=[t_h[dst.name]])
            tyo.r.append(st)
            tyo.w.extend(tyo2.w + tyo3.w)
            final.append(st)
            return st

        if defer:
            return [(lambda oc=oc: piece(oc)) for oc in range(8)] + [store]
        for oc in range(8):
            piece(oc)
        return store()

    t_h = {b.name: T(const=True) for b in hbuf}
    t_h[xT.name] = T(const=True)
    t_h[outT.name] = T(const=True)
    final = []

    def ffn_layer(l, hin, hout):
        P.barrier()
        A.reset()
        c = common_alloc(458, 22, nh=3)
        fcw, tfcw = A.tile([44, 4])
        dma("sp", fcw, fcw_d[l].ap(), "c", writes=[tfcw])
        tfcw.const = True
        P.fence_dma("c")
        gu, tgu = A.tile([22, 456], BF16)
        aring = Ring([A.tile([456]) for _ in range(6)])
        glring = Ring([A.tile([456]) for _ in range(3)])
        tiles = [(s, t0, n) for s in range(NSEQ) for (t0, n) in seq_tiles(None, 456)]
        twu, twd = t_w[f"wup{l}"], t_w[f"wdn{l}"]
        nxt = load_h(c, hin, tiles[0][0] * S, tiles[0][1], tiles[0][2], 2)
        y_nxt = norm_front(c, nxt[0], nxt[1], tiles[0][2] + 2, (l * 4 + 2) * 8)
        tgu_l = [T() for _ in range(22)]
        tail = []
        for ti, (s, t0, n) in enumerate(tiles):
            hb, th = nxt
            y, ty = y_nxt
            if ti + 1 < len(tiles):
                s2, t02, n2 = tiles[ti + 1]
                nxt = load_h(c, hin, s2 * S, t02, n2, 2)
            w = n + 2
            for j in range(22):
                if tail and j >= 1:
                    tail.pop(0)()
                if j == 10 and ti + 1 < len(tiles):
                    assert not tail
                    y_nxt = norm_front(c, nxt[0], nxt[1], tiles[ti + 1][2] + 2, (l * 4 + 2) * 8)
                wv, twv = wload(c, wup_b[l].ap()[:, :, j * 256:(j + 1) * 256], (8, 256), twu)
                res = []
                for half in range(2):
                    cc = j + 22 * half
                    pb, tpb = psnext()
                    for k in range(8):
                        mm(pb[:, :w], wv[:, k, half * 128:(half + 1) * 128], y[:, k, :w], k == 0, k == 7, reads=[twv, ty], writes=[tpb])
                    (a, ta) = aring.next()
                    act(a[:, :n], pb[:, 2:w], AF.Identity, reads=[tpb, tfcw], writes=[ta], scale=fcw[:, cc, 2:3], bias=fcw[:, cc, 3:4])
                    stt(a[:, :n], pb[:, 1:w - 1], fcw[:, cc, 1:2], a[:, :n], ALU.mult, ALU.add, reads=[tpb, ta, tfcw], writes=[ta])
                    stt(a[:, :n], pb[:, 0:w - 2], fcw[:, cc, 0:1], a[:, :n], ALU.mult, ALU.add, reads=[tpb, ta, tfcw], writes=[ta])
                    res.append((a, ta))
                (gl, tgl) = glring.next()
                act(gl[:, :n], res[0][0][:, :n], AF.Gelu_apprx_tanh, reads=[res[0][1]], writes=[tgl])
                tt("pool", gu[:, j, :n], gl[:, :n], res[1][0][:, :n], ALU.mult, reads=[tgl, res[1][1]], writes=[tgu_l[j]])
            tail = out_back(c, gu, tgu_l, 22, n, wdn_b[l], twd, (l * 4 + 3) * 8, hb, th, 2, hout, s * S + t0, defer=True)
        while tail:
            tail.pop(0)()

    def gmlp_layer(hin, hout):
        P.barrier()
        A.reset()
        c = common_alloc(512, 8)
        lng, tlng = A.tile([1024]); lnb, tlnb = A.tile([1024])
        bs, tbs = A.tile([8, 512])
        wsT32, tws32 = A.tile([8, 128]); msk, tmsk = A.tile([128])
        wsT, tws = A.tile([8, 128], BF16)
        dma("sp", lng, a_lng_d.ap(), "c", writes=[tlng]); dma("sp", lnb, a_lnb_d.ap(), "c", writes=[tlnb])
        dma("sp", bs, a_bs_d.ap(), "c", writes=[tbs]); dma("sp", wsT32, a_wsT_d.ap(), "c", writes=[tws32]); dma("sp", msk, a_mask_d.ap(), "c", writes=[tmsk])
        for g in range(8):
            tt("dve", wsT[:, g, :], wsT32[:, g, :], msk, ALU.mult, reads=[tws32, tmsk], pw=[tws])
        for t in (tlng, tlnb, tbs, tws):
            t.const = True
        P.fence_dma("c")
        u_sb, tu = A.tile([8, 512])
        v_sb, tv = A.tile([4, 1024])
        tv_l = [T() for _ in range(4)]
        tguv_l = [T() for _ in range(8)]
        vn, tvn = A.tile([4, 1024], BF16)
        guv, tguv = A.tile([8, 512], BF16)
        st6, tst6 = A.tile([4, 12]); mv, tmv = A.tile([4, 4])
        tiles = [(s, t0) for s in range(NSEQ) for t0 in range(0, S, 512)]
        twin, twout = t_w["a_win"], t_w["a_wout"]
        nxt = load_h(c, hin, 0, 0, 512, 0)
        y_nxt = norm_front(c, nxt[0], nxt[1], 512, (0 * 4 + 0) * 8)
        for ti, (s, t0) in enumerate(tiles):
            hb, th = nxt
            def cut(extra):
                st = dma("pool", hview(hout, s * S + t0, s * S + t0 + 512), hb[:, :, :512], "hs", reads=[th] + extra, pw=[t_h[hout.name]])
                final.append(st)
            if KCUT == 0:
                cut([]); continue
            y, ty = y_nxt
            if ti == 0:
                dump("y", y, [ty])
            if KCUT == 1:
                cut([ty]); continue
            P.newgen(tu)
            for oc in range(8):
                wv, twv = wload(c, a_win_b.ap()[:, :, oc * 128:(oc + 1) * 128], (8, 128), twin)
                pb, tpb = psnext()
                for k in range(8):
                    mm(pb[:, :], wv[:, k, :], y[:, k, :], k == 0, k == 7, reads=[twv, ty], writes=[tpb])
                act(u_sb[:, oc, :], pb[:, :], AF.Gelu_apprx_tanh, reads=[tpb], pw=[tu])
            if KCUT == 2:
                cut([tu]); continue
            wvs = [wload(c, a_win_b.ap()[:, :, 1024 + half * 512:1024 + (half + 1) * 512], (8, 512), twin) for half in range(2)]
            if ti + 1 < len(tiles):
                s2, t02 = tiles[ti + 1]
                nxt_new = load_h(c, hin, s2 * S, t02, 512, 0)
            P.newgen(tvn)
            for tcn in range(4):
                tvt = tv_l[tcn]
                P.newgen(tvt)
                for half in range(2):
                    wv, twv = wvs[half]
                    pb, tpb = psnext()
                    for k in range(8):
                        mm(pb[:, :], y[:, k, tcn * 128:(tcn + 1) * 128], wv[:, k, :], k == 0, k == 7, reads=[twv, ty], writes=[tpb])
                    act(v_sb[:, tcn, half * 512:(half + 1) * 512], pb[:, :], AF.Gelu_apprx_tanh, reads=[tpb], pw=[tvt])
                tl = T()
                P.op("dve", lambda e, tcn=tcn: e.bn_stats(st6[:, tcn, 0:6], v_sb[:, tcn, 0:512]), reads=[tvt], writes=[tl])
                P.op("dve", lambda e, tcn=tcn: e.bn_stats(st6[:, tcn, 6:12], v_sb[:, tcn, 512:1024]), reads=[tvt], pw=[tl])
                tm = T()
                P.op("dve", lambda e, tcn=tcn: e.bn_aggr(mv[:, tcn, 0:2], st6[:, tcn, :]), reads=[tl], writes=[tm])
                act(mv[:, tcn, 2:3], mv[:, tcn, 1:2], AF.Sqrt, reads=[tm], writes=[tm], scale=1.0, bias=EPS)
                recip(mv[:, tcn, 3:4], mv[:, tcn, 2:3], reads=[tm], writes=[tm])
                tvv = T()
                ts("dve", v_sb[:, tcn, :], v_sb[:, tcn, :], mv[:, tcn, 0:1], mv[:, tcn, 3:4], ALU.subtract, ALU.mult, reads=[tvt, tm], writes=[tvv])
                tt("dve", v_sb[:, tcn, :], v_sb[:, tcn, :], lng, ALU.mult, reads=[tvv, tlng], writes=[tvv])
                tt("dve", vn[:, tcn, :], v_sb[:, tcn, :], lnb, ALU.add, reads=[tvv, tlnb], pw=[tvn])
                tvt.r.extend(tvv.w + tvv.r)
            if ti == 0:
                dump("vn", vn, [tvn])
            if KCUT == 4:
                cut([tu, tvn]); continue
            P.newgen(tguv)
            for g in range(8):
                pb, tpb = psnext()
                for tcn in range(4):
                    mm(pb[:, tcn * 128:(tcn + 1) * 128], vn[:, tcn, g * 128:(g + 1) * 128], wsT[:, g, :], True, True, reads=[tvn, tws], writes=[tpb])
                (sq, tsq) = c.rsring.next()
                tt("dve", sq[:, :512], pb[:, :], bs[:, g, :], ALU.add, reads=[tpb, tbs], writes=[tsq])
                tt("dve", guv[:, g, :], sq[:, :512], u_sb[:, g, :], ALU.mult, reads=[tsq, tu], writes=[tguv_l[g]])
            if ti + 1 < len(tiles):
                y_nxt = norm_front(c, nxt_new[0], nxt_new[1], 512, (0 * 4 + 0) * 8)
            st = out_back(c, guv, tguv_l, 8, 512, a_wout_b, twout, (0 * 4 + 1) * 8, hb, th, 0, hout, s * S + t0)
            if ti + 1 < len(tiles):
                nxt = nxt_new

    def attn_layer(l, hin, hout):
        fox = (l == 2)
        P.barrier()
        A.reset()
        c = common_alloc(512, 8)
        win_b = c_win_b if fox else b_win_b
        twin = t_w["c_win" if fox else "b_win"]
        wout_b = c_wout_b if fox else b_wout_b
        twout = t_w["c_wout" if fox else "b_wout"]
        qk, v, ao = qk_s[l], v_s[l], ao_s[l]
        tqk = [T(const=True) for _ in range(NSEQ)]; tvs = [T(const=True) for _ in range(NSEQ)]; tao = [T(const=True) for _ in range(NSEQ)]
        tcqk = [T(const=True) for _ in range(NSEQ)]
        mark = A.off
        qko, tqko = A.tile([16, 512], BF16)
        vt, tvt = A.tile([4, 1024], BF16)
        if fox:
            nbf, tnbf = A.tile([1]); dma("sp", nbf[0:16, :], c_bf_d.ap(), "c", writes=[tnbf])
            ts("dve", nbf[0:16, :], nbf[0:16, :], -1.0, None, ALU.mult, ALU.bypass, reads=[tnbf], writes=[tnbf]); tnbf.const = True
            on16, ton16 = A.tile([512]); memset("pool", on16[0:16, :], 1.0, writes=[ton16]); ton16.const = True
            spb = Ring([A.tile([512]) for _ in range(2)])
            csum = Ring([A.tile([512]) for _ in range(2)])
            r1, tr1 = A.tile([512]); hf, thf = A.tile([512])
            cqt = Ring([A.tile([6, 512], BF16) for _ in range(2)]); ckt = Ring([A.tile([6, 512], BF16) for _ in range(2)])
            prev_cs = None
        tiles = [(s, t0) for s in range(NSEQ) for t0 in range(0, S, 512)]
        P.fence_dma("c")
        nxt = load_h(c, hin, 0, 0, 512, 0)
        y_nxt = norm_front(c, nxt[0], nxt[1], 512, (l * 4 + 0) * 8)
        for ti, (s, t0) in enumerate(tiles):
            hb, th = nxt
            if ti + 1 < len(tiles):
                s2, t02 = tiles[ti + 1]
                nxt = load_h(c, hin, s2 * S, t02, 512, 0)
            g0 = s * S + t0
            y, ty = y_nxt
            P.newgen(tqko)
            for oc in range(16):
                wv, twv = wload(c, win_b.ap()[:, :, oc * 128:(oc + 1) * 128], (8, 128), twin)
                pb, tpb = psnext()
                for k in range(8):
                    mm(pb[:, :], wv[:, k, :], y[:, k, :], k == 0, k == 7, reads=[twv, ty], writes=[tpb])
                if oc % 2 == 0:
                    act(qko[:, oc, :], pb[:, :], AF.Copy, reads=[tpb], pw=[tqko])
                else:
                    cp("dve", qko[:, oc, :], pb[:, :], reads=[tpb], pw=[tqko])
            dma("pool", qk.ap().rearrange("(c p) t -> p c t", p=128)[:, :, g0:g0 + 512], qko, "qsq", reads=[tqko], pw=[tqk[s]])
            y_cur_keep = (y, ty)
            if ti + 1 < len(tiles):
                y_nxt = norm_front(c, nxt[0], nxt[1], 512, (l * 4 + 0) * 8)
            P.newgen(tvt)
            for half in range(2):
                wv, twv = wload(c, win_b.ap()[:, :, 2048 + half * 512:2048 + (half + 1) * 512], (8, 512), twin)
                for tcn in range(4):
                    pb, tpb = psnext()
                    for k in range(8):
                        mm(pb[:, :], y[:, k, tcn * 128:(tcn + 1) * 128], wv[:, k, :], k == 0, k == 7, reads=[twv, ty], writes=[tpb])
                    if tcn % 2 == 0:
                        act(vt[:, tcn, half * 512:(half + 1) * 512], pb[:, :], AF.Copy, reads=[tpb], pw=[tvt])
                    else:
                        cp("dve", vt[:, tcn, half * 512:(half + 1) * 512], pb[:, :], reads=[tpb], pw=[tvt])
            dma("pool", v.ap()[g0:g0 + 512, :].rearrange("(n p) c -> p n c", p=128), vt, "qsv", reads=[tvt], pw=[tvs[s]])
            if fox:
                wv, twv = wload(c, win_b.ap()[:, :, 3072:3088], (8, 16), twin)
                pb, tpb = psnext()
                for k in range(8):
                    mm(pb[0:16, :], wv[:, k, :], y[:, k, :], k == 0, k == 7, reads=[twv, ty], writes=[tpb])
                (sp_, tsp) = spb.next()
                act(sp_[0:16, :], pb[0:16, :], AF.Exp, reads=[tpb, tnbf], writes=[tsp], scale=-1.0, bias=nbf[0:16, 0:1])
                act(sp_[0:16, :], sp_[0:16, :], AF.Ln, reads=[tsp], writes=[tsp], scale=1.0, bias=1.0)
                ts("dve", sp_[0:16, :], sp_[0:16, :], 8.0, None, ALU.mult, ALU.bypass, reads=[tsp], writes=[tsp])
                (cs, tcs) = csum.next()
                init = 0.0 if t0 == 0 else prev_cs[0][0:16, 511:512]
                rd = [tsp, ton16] + ([] if t0 == 0 else [prev_cs[1]])
                P.op("dve", lambda e, cs=cs, sp_=sp_, init=init: e.tensor_tensor_scan(cs[0:16, :], on16[0:16, :], sp_[0:16, :], init, ALU.mult, ALU.add), reads=rd, writes=[tcs])
                prev_cs = (cs, tcs)
                (cq, tcq) = cqt.next(); (ck, tck) = ckt.next()
                P.newgen(tcq); P.newgen(tck)
                cur, tcur = cs, tcs
                for part in range(3):
                    cp("dve", ck[0:16, 3 + part, :], cur[0:16, :], reads=[tcur], pw=[tck])
                    ts("dve", cq[0:16, part, :], ck[0:16, 3 + part, :], -1.0, None, ALU.mult, ALU.bypass, reads=[tck], pw=[tcq])
                    if part < 2:
                        cp("dve", hf[0:16, :], ck[0:16, 3 + part, :], reads=[tck], writes=[thf])
                        tt("dve", r1[0:16, :], cur[0:16, :], hf[0:16, :], ALU.subtract, reads=[tcur, thf], writes=[tr1])
                        cur, tcur = r1, tr1
                    memset("pool", ck[0:16, part, :], 1.0, pw=[tck])
                    memset("pool", cq[0:16, 3 + part, :], 1.0, pw=[tcq])
                dma("pool", cq_s.ap()[:, :, g0:g0 + 512], cq[0:16, :, :], f"qcq{tcq.slot}", reads=[tcq], pw=[tcqk[s]])
                dma("pool", ck_s.ap()[:, :, g0:g0 + 512], ck[0:16, :, :], f"qck{tck.slot}", reads=[tck], pw=[tcqk[s]])
        P.barrier()
        A.reset()
        c = common_alloc(512, 8, full=False)
        ps_pool[0] = [2, 3, 4, 5, 6, 7] if fox else [4, 5, 6, 7]
        acc_i = [0]
        NH = 16 if fox else 8
        qt_r = Ring([A.tile([S], BF16) for _ in range(2)])
        kt_r = Ring([A.tile([S], BF16) for _ in range(2)])
        va_r = Ring([A.tile([32, 128], BF16) for _ in range(2)])
        e_r = Ring([A.tile([512], BF16) for _ in range(6)])
        sb_r = Ring([A.tile([512]) for _ in range(3)])
        o_r = Ring([A.tile([512]) for _ in range(4)])
        aot_r = Ring([A.tile([512], BF16) for _ in range(3)])
        if fox:
            tri, ttri = A.tile([128], BF16)
            tri32, ttri32 = A.tile([128])
            dma("sp", tri32, tri_d.ap(), "c", writes=[ttri32])
            cp("dve", tri, tri32, reads=[ttri32], writes=[ttri]); ttri.const = True
            for (va, tva) in va_r.items:
                memset("pool", va[:, :, 64:128], 1.0, writes=[tva])
        else:
            lam, tlam = A.tile([256]); dma("sp", lam[0:1, :], b_lam_d.ap(), "c", writes=[tlam])
            sc, tsc = A.tile([8])
            tt("dve", lam[0:1, 0:64], lam[0:1, 0:64], lam[0:1, 64:128], ALU.mult, reads=[tlam], writes=[tlam])
            tt("dve", lam[0:1, 128:192], lam[0:1, 128:192], lam[0:1, 192:256], ALU.mult, reads=[tlam], writes=[tlam])
            P.op("dve", lambda e: e.reduce_sum(sc[0:1, 0:1], lam[0:1, 0:64], mybir.AxisListType.X), reads=[tlam], writes=[tsc])
            P.op("dve", lambda e: e.reduce_sum(sc[0:1, 1:2], lam[0:1, 128:192], mybir.AxisListType.X), reads=[tlam, tsc], writes=[tsc])
            act(sc[0:1, 0:2], sc[0:1, 0:2], AF.Exp, reads=[tsc], writes=[tsc])
            tt("dve", sc[0:1, 2:3], sc[0:1, 1:2], sc[0:1, 0:1], ALU.subtract, reads=[tsc], writes=[tsc])
            ts("dve", sc[0:1, 3:4], sc[0:1, 2:3], -LAMBDA_INIT, None, ALU.add, ALU.bypass, reads=[tsc], writes=[tsc])
            on32, ton32 = A.tile([128]); memset("pool", on32[0:1, :], 1.0, writes=[ton32])
            pb, tpb = psnext()
            mm(pb[:, 0:1], on32[0:1, :], sc[0:1, 3:4], True, True, reads=[ton32, tsc], writes=[tpb])
            nlam, tnl = A.tile([1]); cp("dve", nlam, pb[:, 0:1], reads=[tpb], writes=[tnl]); tnl.const = True
            subg, tsg = A.tile([1]); dma("sp", subg, b_subg_d.ap(), "c", writes=[tsg])
            ts("dve", subg, subg, 1.0 - LAMBDA_INIT, None, ALU.mult, ALU.bypass, reads=[tsg], writes=[tsg]); tsg.const = True
            relb, trb = A.tile([8]); dma("sp", relb[0:32, :], relb_d.ap(), "c", writes=[trb])
            oh, toh = A.tile([GVN]); dma("sp", oh[0:32, :], ohrev_d.ap(), "c", writes=[toh])
            gvt, tgvt = A.tile([GVN])
            for o in range(0, GVN, 384):
                pb, tpb = psnext()
                mm(pb[0:8, 0:384], relb[0:32, :], oh[0:32, o:o + 384], True, True, reads=[trb, toh], writes=[tpb])
                cp("dve", gvt[0:8, o:o + 384], pb[0:8, 0:384], reads=[tpb], pw=[tgvt])
            tgv = T(const=True)
            dma("pool", gv_s.ap(), gvt[0:8, :], "gv", reads=[tgvt], pw=[tgv])
            jf, tjf = A.tile([128]); dma("sp", jf, jflip_d.ap(), "c", writes=[tjf])
            P.fence_dma("c"); P.fence_dma("gv")
            tz, ttz = A.tile([8, 1024]); P.newgen(ttz)
            cb, tcb = A.tile([8]); P.newgen(tcb)
            hk, thk = A.tile([1024])
            for h in range(8):
                dma("sp", hk, bass.AP(gv_s, h * GVN, [[1, 128], [1, 1024]]), "c", reads=[tgv], writes=[thk])
                for o in range(2):
                    pb, tpb = psnext()
                    mm(pb[:, :], jf, hk[:, o * 512:(o + 1) * 512], True, True, reads=[tjf, thk], writes=[tpb])
                    cp("dve", tz[:, h, o * 512:(o + 1) * 512], pb[:, :], reads=[tpb], pw=[ttz])
                cp("dve", cb[:, h:h + 1], tz[:, h, 1023:1024], reads=[ttz], pw=[tcb])
            ttz.const = True; tcb.const = True
        P.fence_dma("c")
        AOV = ao.ap()

        P.barrier()
        NM = 8
        steps = []
        for s in range(NSEQ):
            for h in range(NH):
                for m in range(NM):
                    nkt = 4 * m + 4
                    for kk in range(nkt):
                        steps.append((s, h, m, kk, nkt))
        NSTEP = len(steps)
        heads = [(s, h) for s in range(NSEQ) for h in range(NH)]
        hbufs = {}

        def load_head(idx):
            if idx >= len(heads) or idx in hbufs:
                return
            s, h = heads[idx]
            (qt, tqt) = qt_r.next(); (kt, tkt) = kt_r.next(); (va, tva) = va_r.next()
            base = s * S
            if fox:
                P.newgen(tqt); P.newgen(tkt)
                dma("sp", qt[0:64, :], qk.ap()[h * 64:(h + 1) * 64, base:base + S], f"hq{tqt.slot}", reads=[tqk[s]], pw=[tqt])
                dma("sp", kt[0:64, :], qk.ap()[1024 + h * 64:1024 + (h + 1) * 64, base:base + S], f"hk{tkt.slot}", reads=[tqk[s]], pw=[tkt])
                dma("sp", qt[64:70, :], cq_s.ap()[h, :, base:base + S], f"hq{tqt.slot}", reads=[tcqk[s]], pw=[tqt])
                dma("sp", kt[64:70, :], ck_s.ap()[h, :, base:base + S], f"hk{tkt.slot}", reads=[tcqk[s]], pw=[tkt])
                P.newgen(tva)
                dma("sp", va[:, :, 0:64], v.ap()[base:base + S, h * 64:(h + 1) * 64].rearrange("(n p) c -> p n c", p=128), f"hv{tva.slot}", reads=[tvs[s]], pw=[tva])
            else:
                dma("sp", qt, qk.ap()[h * 128:(h + 1) * 128, base:base + S], f"hq{tqt.slot}", reads=[tqk[s]], writes=[tqt])
                dma("sp", kt, qk.ap()[1024 + h * 128:1024 + (h + 1) * 128, base:base + S], f"hk{tkt.slot}", reads=[tqk[s]], writes=[tkt])
                dma("sp", va, v.ap()[base:base + S, h * 128:(h + 1) * 128].rearrange("(n p) c -> p n c", p=128), f"hv{tva.slot}", reads=[tvs[s]], writes=[tva])
            hbufs[idx] = (qt, tqt, kt, tkt, va, tva)

        KR = 70 if fox else 64
        if fox:
            s_ring = Ring([(PSB[i], TPS[i]) for i in (2, 3, 4, 5, 6, 7)])
            acc_ring = Ring([(PSB[0], TPS[0]), (PSB[1], TPS[1])])
        else:
            pairs = [(PSALL[:, (4 + 2 * i) * 512:(6 + 2 * i) * 512].rearrange("p (a b) -> p a b", a=2), T()) for i in range(2)]
            po_pair = (PSALL[:, 0:1024].rearrange("p (a b) -> p a b", a=2), T())
            pl_pair = (PSALL[:, 1024:2048].rearrange("p (a b) -> p a b", a=2), T())
            s_ring = Ring(pairs)
            ones32, ton32b = A.tile([128]); memset("pool", ones32, 1.0, writes=[ton32b]); ton32b.const = True
            e2_r = Ring([A.tile([2, 512], BF16) for _ in range(4)])
            sb2_r = Ring([A.tile([2, 512]) for _ in range(2)])
            osb_r = Ring([A.tile([2, 512]) for _ in range(2)])
            r12_r = Ring([A.tile([2, 512]) for _ in range(2)])
            lsb_r = Ring([A.tile([2, 512]) for _ in range(2)])
        st_S = {}
        st_E = {}
        cur = {}
        pending = []

        def stage_A(g):
            s, h, m, kk, nkt = steps[g]
            hi = heads.index((s, h))
            if m == 0 and kk == 0:
                load_head(hi)
            qt, tqt, kt, tkt, va, tva = hbufs[hi]
            c0 = max(0, kk - 4 * m) * 128
            ks = slice(kk * 128, (kk + 1) * 128)
            qs = slice(m * 512 + c0, m * 512 + 512)
            (sp_, tsp_) = s_ring.next()
            if fox:
                mm(sp_[:, c0:512], kt[0:KR, ks], qt[0:KR, qs], True, True, reads=[tkt, tqt], writes=[tsp_])
            else:
                for mp in range(2):
                    rr = slice(mp * 64, (mp + 1) * 64)
                    mm(sp_[:, mp, c0:512], kt[rr, ks], qt[rr, qs], True, True, reads=[tkt, tqt], writes=[tsp_])
            st_S[g] = (sp_, tsp_)

        def stage_B(g):
            s, h, m, kk, nkt = steps[g]
            c0 = max(0, kk - 4 * m) * 128
            diag = kk >= 4 * m
            sp_, tsp_ = st_S.pop(g)
            if fox:
                (e1, te1) = e_r.next()
                act(e1[:, c0:512], sp_[:, c0:512], AF.Exp, reads=[tsp_], writes=[te1], scale=0.125)
                if diag:
                    tt("dve", e1[:, c0:c0 + 128], e1[:, c0:c0 + 128], tri, ALU.mult, reads=[te1, ttri], writes=[te1])
                st_E[g] = (e1, te1)
            else:
                near = kk >= 4 * m - 1
                (e2, te2) = e2_r.next()
                if near:
                    dd = 128 * (kk - 4 * m)
                    j0 = TZC - dd + c0
                    (sbt, tsb) = sb2_r.next()
                    P.newgen(tsb)
                    for mp in range(2):
                        stt(sbt[:, mp, c0:512], sp_[:, mp, c0:512], 0.125, tz[:, h, j0:j0 + 512 - c0], ALU.mult, ALU.add, reads=[tsp_, ttz], pw=[tsb])
                    if c0 == 0:
                        act(e2[:, :, :], sbt[:, :, :], AF.Exp, reads=[tsb], writes=[te2])
                    else:
                        P.newgen(te2)
                        for mp in range(2):
                            act(e2[:, mp, c0:512], sbt[:, mp, c0:512], AF.Exp, reads=[tsb], pw=[te2])
                else:
                    act(e2[:, :, c0:512], sp_[:, :, c0:512], AF.Exp, reads=[tsp_, tcb], writes=[te2], scale=0.125, bias=cb[:, h:h + 1])
                if diag:
                    tm_ = T()
                    tm_.w = list(te2.w)
                    for mp in range(2):
                        memset("pool", e2[64:128, mp, c0:c0 + 64], 0.0, writes=[tm_])
                        te2.w.extend(tm_.w)
                st_E[g] = (e2, te2)

        def stage_C(g):
            s, h, m, kk, nkt = steps[g]
            hi = heads.index((s, h))
            qt, tqt, kt, tkt, va, tva = hbufs[hi]
            c0 = max(0, kk - 4 * m) * 128
            e_, te_ = st_E.pop(g)
            if m == 0 and kk == 0:
                load_head(hi + 1)
            base = s * S
            g0 = base + m * 512
            if fox:
                if kk == 0:
                    cur["po"] = acc_ring.next()
                po, tpo = cur["po"]
                mm(po[:, c0:512], va[:, kk, :], e_[:, c0:512], kk == 0, kk == nkt - 1, reads=[tva, te_], writes=[tpo])
                if kk == nkt - 1:
                    (rl, trl) = o_r.next()
                    recip(rl[64:128, :], po[64:128, :], reads=[tpo], writes=[trl])
                    (aot, taot) = aot_r.next()
                    tt("dve", aot[0:64, :], po[0:64, :], rl[64:128, :], ALU.mult, reads=[tpo, trl], writes=[taot])
                    dma("pool", AOV[h * 64:(h + 1) * 64, g0:g0 + 512], aot[0:64, :], f"ao{taot.slot}", reads=[taot], pw=[tao[s]])
            else:
                po, tpo = po_pair
                pl, tpl = pl_pair
                for mp in range(2):
                    mm(po[:, mp, c0:512], va[:, kk, :], e_[:, mp, c0:512], kk == 0, kk == nkt - 1, reads=[tva, te_], writes=[tpo])
                    mm(pl[:, mp, c0:512], ones_bf[:], e_[:, mp, c0:512], kk == 0, kk == nkt - 1, reads=[t_ones, te_], writes=[tpl])
                if kk == nkt - 1:
                    (osb, tosb) = osb_r.next()
                    cp("dve", osb[:, :, :], po[:, :, :], reads=[tpo], writes=[tosb])
                    (lsb, tlsb) = lsb_r.next()
                    act(lsb[:, :, :], pl[:, :, :], AF.Copy, reads=[tpl], writes=[tlsb])
                    (r12, tr12) = r12_r.next()

                    (sq, tsq) = c.sqring.next()

                    def epi1(osb=osb, tosb=tosb, r12=r12, tr12=tr12, sq=sq, tsq=tsq, lsb=lsb, tlsb=tlsb):
                        P.op("dve", lambda e: e.reciprocal(r12[:, :, :], lsb[:, :, :]), reads=[tlsb], writes=[tr12])
                        tt("dve", r12[:, :, :], osb[:, :, :], r12[:, :, :], ALU.mult, reads=[tosb, tr12], writes=[tr12])
                        stt(r12[:, 0, :], r12[:, 1, :], nlam[:, 0:1], r12[:, 0, :], ALU.mult, ALU.add, reads=[tr12, tnl], writes=[tr12])
                        tt("dve", sq[:, :512], r12[:, 0, :], r12[:, 0, :], ALU.mult, reads=[tr12], writes=[tsq])

                    def epi2(h=h, s=s, g0=g0, r12=r12, tr12=tr12, sq=sq, tsq=tsq):
                        (pn2, tpn2) = s_ring.items[s_ring.i % len(s_ring.items)]
                        mm(pn2[:, 0, :], ones_bf[:], sq[:, :512], True, True, reads=[tsq, t_ones], writes=[tpn2])
                        (rs, trs) = c.rsring.next()
                        act(rs[:, :512], pn2[:, 0, :], AF.Ln, reads=[tpn2], writes=[trs], scale=1.0 / 128, bias=EPS)
                        act(rs[:, :512], rs[:, :512], AF.Exp, reads=[trs], writes=[trs], scale=-0.5)
                        (aot, taot) = aot_r.next()
                        stt(aot, r12[:, 0, :], subg[:, 0:1], rs[:, :512], ALU.mult, ALU.mult, reads=[tr12, trs, tsg], writes=[taot])
                        dma("pool", AOV[h * 128:(h + 1) * 128, g0:g0 + 512], aot, f"ao{taot.slot}", reads=[taot], pw=[tao[s]])
                    pending.append((g + 1, epi1))
                    pending.append((g + 3, epi2))

        DEPTH = 2
        for g in range(min(DEPTH, NSTEP)):
            stage_A(g)
        for g in range(NSTEP):
            stage_B(g)
            while pending and pending[0][0] <= g:
                pending.pop(0)[1]()
            if g + DEPTH < NSTEP:
                stage_A(g + DEPTH)
            stage_C(g)
        while pending:
            pending.pop(0)[1]()

        P.barrier()
        A.reset()
        ps_pool[0] = list(range(8))
        c = common_alloc(512, 8)
        aor = Ring([A.tile([8, 512], BF16) for _ in range(2)])
        for ti, (s, t0) in enumerate(tiles):
            g0 = s * S + t0
            hb, th = load_h(c, hin, s * S, t0, 512, 0)
            (at, tat) = aor.next()
            dma("sp", at, AOV.rearrange("(c p) t -> p c t", p=128)[:, :, g0:g0 + 512], f"al{tat.slot}", reads=[tao[s]], writes=[tat])
            st = out_back(c, at, tat, 8, 512, wout_b, twout, (l * 4 + 1) * 8, hb, th, 0, hout, g0)

    def rglru_layer(hin, hout):
        l = 3
        P.barrier()
        A.reset()
        c = common_alloc(459, 10, ny=2)
        kcs = []
        for oc in range(10):
            lo = (oc * 128) // 80 * 80
            hi = -(-((oc + 1) * 128) // 80) * 80
            kcs.append(list(range(lo // 128, min(10, -(-hi // 128)))))
        wr, twr = A.tile([10, 3, 128], BF16); wi, twi = A.tile([10, 3, 128], BF16)
        P.newgen(twr); P.newgen(twi)
        for oc in range(10):
            for i, k in enumerate(kcs[oc]):
                dma("sp", wr[:, oc, i, :], d_wr_b.ap()[:, k, oc * 128:(oc + 1) * 128], "c", reads=[t_w["d_wr"]], pw=[twr])
                dma("sp", wi[:, oc, i, :], d_wi_b.ap()[:, k, oc * 128:(oc + 1) * 128], "c", reads=[t_w["d_wi"]], pw=[twi])
        twr.const = True; twi.const = True
        cw, tcw = A.tile([10, 5]); dma("sp", cw, d_cw_d.ap(), "c", writes=[tcw]); tcw.const = True
        br, tbr = A.tile([10]); dma("sp", br, d_br_d.ap(), "c", writes=[tbr])
        bi, tbi = A.tile([10]); dma("sp", bi, d_bi_d.ap(), "c", writes=[tbi])
        ts("dve", br, br, -1.0, None, ALU.mult, ALU.bypass, reads=[tbr], writes=[tbr]); tbr.const = True
        ts("dve", bi, bi, -1.0, None, ALU.mult, ALU.bypass, reads=[tbi], writes=[tbi]); tbi.const = True
        cs1, tcs1 = A.tile([10]); dma("sp", cs1, d_lam_d.ap(), "c", writes=[tcs1])
        cs2, tcs2 = A.tile([10])
        act(cs1, cs1, AF.Exp, reads=[tcs1], writes=[tcs1], scale=-1.0)
        act(cs1, cs1, AF.Ln, reads=[tcs1], writes=[tcs1], scale=1.0, bias=1.0)
        ts("dve", cs1, cs1, -8.0, None, ALU.mult, ALU.bypass, reads=[tcs1], writes=[tcs1])
        ts("dve", cs2, cs1, 2.0, None, ALU.mult, ALU.bypass, reads=[tcs1], writes=[tcs2])
        tcs1.const = True; tcs2.const = True
        P.fence_dma("c")
        xc, txc = A.tile([10, 456]); xcb, txcb = A.tile([10, 456], BF16)
        gg, tgg = A.tile([10, 456], BF16)
        hh, thh = A.tile([10, 456])
        hst, thst = A.tile([10])
        yv, tyv = A.tile([10, 456], BF16)
        tyv_l = [T() for _ in range(10)]
        tmp_r = Ring([A.tile([456]) for _ in range(8)])
        tiles = [(s, t0, n) for s in range(NSEQ) for (t0, n) in seq_tiles(None, 456)]
        twin, twout = t_w["d_win"], t_w["d_wout"]
        nxt = load_h(c, hin, 0, 0, tiles[0][2], 3)
        prev = None
        c.rstd_exp = True
        y_nxt = norm_front(c, nxt[0], nxt[1], tiles[0][2] + 3, (l * 4 + 0) * 8)
        for ti, (s, t0, n) in enumerate(tiles):
            hb, th = nxt
            y, ty = y_nxt
            w = n + 3
            P.newgen(txc); P.newgen(txcb); P.newgen(tgg)
            for ch in range(10):
                wv, twv = wload(c, d_win_b.ap()[:, :, RW + ch * 128:RW + (ch + 1) * 128], (8, 128), twin)
                pb, tpb = psnext()
                for k in range(8):
                    mm(pb[:, :w], wv[:, k, :], y[:, k, :w], k == 0, k == 7, reads=[twv, ty], writes=[tpb])
                ta = T()
                act(xc[:, ch, :n], pb[:, 3:w], AF.Identity, reads=[tpb, tcw], writes=[ta], pw=[txc], scale=cw[:, ch, 3:4], bias=cw[:, ch, 4:5])
                for tap in range(3):
                    stt(xc[:, ch, :n], pb[:, tap:tap + n], cw[:, ch, tap:tap + 1], xc[:, ch, :n], ALU.mult, ALU.add, reads=[tpb, ta, tcw], writes=[ta])
                txc.w.extend(ta.w)
                act(xcb[:, ch, :n], xc[:, ch, :n], AF.Copy, reads=[ta], pw=[txcb])
                txc.r.extend(ta.r)
                wv, twv = wload(c, d_win_b.ap()[:, :, ch * 128:(ch + 1) * 128], (8, 128), twin)
                pb, tpb = psnext()
                for k in range(8):
                    mm(pb[:, :w], wv[:, k, :], y[:, k, :w], k == 0, k == 7, reads=[twv, ty], writes=[tpb])
                act(gg[:, ch, :n], pb[:, 3:w], AF.Gelu_apprx_tanh, reads=[tpb], pw=[tgg])
            if ti + 1 < len(tiles):
                s2, t02, n2 = tiles[ti + 1]
                nxt_new = load_h(c, hin, s2 * S, t02, n2, 3)
            P.newgen(thh); P.newgen(tyv)
            for oc in range(10):
                pr, tpr = psnext()
                for i, k in enumerate(kcs[oc]):
                    mm(pr[:, :n], wr[:, oc, i, :], xcb[:, k, :n], i == 0, i == len(kcs[oc]) - 1, reads=[twr, txcb], writes=[tpr])
                pi, tpi = psnext()
                for i, k in enumerate(kcs[oc]):
                    mm(pi[:, :n], wi[:, oc, i, :], xcb[:, k, :n], i == 0, i == len(kcs[oc]) - 1, reads=[twi, txcb], writes=[tpi])
                (r_, tr_) = tmp_r.next(); (i_, ti_) = tmp_r.next(); (a_, ta_) = tmp_r.next(); (m_, tm_) = tmp_r.next()
                act(r_[:, :n], pr[:, :n], AF.Exp, reads=[tpr, tbr], writes=[tr_], scale=-1.0, bias=br[:, oc:oc + 1])
                act(i_[:, :n], pi[:, :n], AF.Exp, reads=[tpi, tbi], writes=[ti_], scale=-1.0, bias=bi[:, oc:oc + 1])
                act(r_[:, :n], r_[:, :n], AF.Ln, reads=[tr_], writes=[tr_], scale=1.0, bias=1.0)
                act(i_[:, :n], i_[:, :n], AF.Ln, reads=[ti_], writes=[ti_], scale=1.0, bias=1.0)
                act(r_[:, :n], r_[:, :n], AF.Exp, reads=[tr_], writes=[tr_], scale=-1.0)
                act(i_[:, :n], i_[:, :n], AF.Exp, reads=[ti_], writes=[ti_], scale=-1.0)
                act(a_[:, :n], r_[:, :n], AF.Exp, reads=[tr_, tcs1], writes=[ta_], scale=cs1[:, oc:oc + 1])
                act(m_[:, :n], r_[:, :n], AF.Exp, reads=[tr_, tcs2], writes=[tm_], scale=cs2[:, oc:oc + 1])
                act(m_[:, :n], m_[:, :n], AF.Ln, reads=[tm_], writes=[tm_], scale=-1.0, bias=1.0)
                act(m_[:, :n], m_[:, :n], AF.Exp, reads=[tm_], writes=[tm_], scale=0.5)
                tt("dve", i_[:, :n], i_[:, :n], xc[:, oc, :n], ALU.mult, reads=[ti_, txc], writes=[ti_])
                tt("dve", i_[:, :n], i_[:, :n], m_[:, :n], ALU.mult, reads=[ti_, tm_], writes=[ti_])
                if t0 == 0:
                    init, rd = 0.0, []
                else:
                    init, rd = hst[:, oc:oc + 1], [thst]
                tsc_ = T()
                P.op("dve", lambda e, oc=oc, a_=a_, i_=i_, init=init, n=n: e.tensor_tensor_scan(hh[:, oc, :n], a_[:, :n], i_[:, :n], init, ALU.mult, ALU.add),
                     reads=[ta_, ti_] + rd, writes=[tsc_])
                thh.w.extend(tsc_.w)
                cp("dve", hst[:, oc:oc + 1], hh[:, oc, n - 1:n], reads=[tsc_], pw=[thst])
                tt("dve", yv[:, oc, :n], hh[:, oc, :n], gg[:, oc, :n], ALU.mult, reads=[tsc_, tgg], writes=[tyv_l[oc]])
            if ti + 1 < len(tiles):
                y_nxt = norm_front(c, nxt_new[0], nxt_new[1], tiles[ti + 1][2] + 3, (l * 4 + 0) * 8)
            st = out_back(c, yv, tyv_l, 10, n, d_wout_b, twout, (l * 4 + 1) * 8, hb, th, 3, hout, s * S + t0)
            if ti + 1 < len(tiles):
                nxt = nxt_new

    seqn = []
    for l in range(nlayers):
        seqn += [("mix", l), ("ffn", l)]
    if nsub is not None:
        seqn = seqn[:nsub]
    bufs = [xT] + [hbuf[i] for i in range(len(seqn) - 1)] + [outT]
    for i, (kind, l) in enumerate(seqn):
        hin, hout = bufs[i], bufs[i + 1]
        final.clear()
        if kind == "ffn":
            ffn_layer(l, hin, hout)
        elif l == 0:
            gmlp_layer(hin, hout)
        elif l in (1, 2):
            attn_layer(l, hin, hout)
        else:
            rglru_layer(hin, hout)
    P.emit(final_waits=list(final) + dumps)
    nc._in_names = list(ins.keys())
    return nc


def _t5_bucket(rel):
    half, max_exact = 16, 8
    n = np.abs(rel)
    ret = np.where(rel > 0, half, 0)
    nf = np.maximum(n, 1).astype(np.float32)
    large = max_exact + (np.log(nf / np.float32(max_exact)) / np.float32(math.log(128 / max_exact)) * (half - max_exact)).astype(np.int32)
    large = np.minimum(large, half - 1)
    return ret + np.where(n < max_exact, n, large)


def _kc(w):
    K, N = w.shape
    return np.ascontiguousarray(w.reshape(K // 128, 128, N).transpose(1, 0, 2))


def _col(v, nch):
    return np.ascontiguousarray(v.reshape(nch, 128).T)


def prep_shared(inp):
    f = np.float32
    m = {}
    ngx = inp["norm_g"]
    m["ng"] = np.ascontiguousarray(ngx.reshape(4, 4, 8, 128).transpose(3, 0, 1, 2).reshape(128, 128)).astype(f)
    perm = np.concatenate([np.concatenate([np.arange(j * 128, (j + 1) * 128), DFF + np.arange(j * 128, (j + 1) * 128)]) for j in range(22)])
    for l in range(4):
        m[f"wup{l}"] = _kc(inp["ffn_w_up"][l][:, perm])
        m[f"wdn{l}"] = _kc(inp["ffn_w_down"][l])
        cwv = inp["ffn_conv_w"][l]
        cb = inp["ffn_conv_b"][l]
        arr = np.concatenate([cwv, cb[None]], 0)
        m[f"fcw{l}"] = np.ascontiguousarray(arr.reshape(4, 44, 128).transpose(2, 1, 0)).astype(f)
    m["a_win"] = _kc(inp["a_w_in"][0])
    m["a_lng"] = np.ascontiguousarray(np.broadcast_to(inp["a_ln_g"][0][None], (128, 1024))).astype(f)
    m["a_lnb"] = np.ascontiguousarray(np.broadcast_to(inp["a_ln_b"][0][None], (128, 1024))).astype(f)
    m["a_wsT"] = np.ascontiguousarray(inp["a_w_s"][0].transpose(2, 0, 1)).astype(f)
    p = np.arange(128)
    m["a_mask"] = ((p[:, None] // 64) <= (p[None, :] // 64)).astype(f)
    m["a_bs"] = np.ascontiguousarray(np.broadcast_to(np.tile(inp["a_b_s"][0], (1, 4))[None], (128, 8, 512))).astype(f)
    m["a_wout"] = _kc(inp["a_w_out"][0])
    m["b_win"] = _kc(inp["b_w_in"][0])
    m["b_lam"] = np.ascontiguousarray(inp["b_lam"][0].reshape(1, 256)).astype(f)
    m["b_subg"] = np.ascontiguousarray(inp["b_sub_g"][0].reshape(128, 1)).astype(f)
    m["b_wout"] = _kc(inp["b_w_out"][0])
    m["relb"] = np.ascontiguousarray(inp["rel_bias"]).astype(f)
    mmv = np.arange(GVN)
    bk = _t5_bucket(127 + TZC - mmv)
    oh = np.zeros((32, GVN), f)
    oh[bk, mmv] = 1.0
    m["ohrev"] = oh
    m["jflip"] = np.ascontiguousarray(np.eye(128, dtype=f)[::-1])
    m["c_win"] = _kc(inp["c_w_in"][0])
    m["c_bf"] = np.ascontiguousarray(inp["c_b_f"][0].reshape(16, 1)).astype(f)
    m["c_wout"] = _kc(inp["c_w_out"][0])
    m["tri"] = (p[:, None] <= p[None, :]).astype(f)
    m["d_win"] = _kc(inp["d_w_in"][0])
    cwd = np.concatenate([inp["d_conv_w"][0], inp["d_conv_b"][0][None]], 0)
    m["d_cw"] = np.ascontiguousarray(cwd.reshape(5, 10, 128).transpose(2, 1, 0)).astype(f)
    for nm, key in (("d_wr", "d_w_r"), ("d_wi", "d_w_i")):
        dense = np.zeros((RW, RW), f)
        for n in range(16):
            dense[n * 80:(n + 1) * 80, n * 80:(n + 1) * 80] = inp[key][0][n]
        m[nm] = _kc(dense)
    m["d_br"] = _col(inp["d_b_r"][0], 10).astype(f)
    m["d_bi"] = _col(inp["d_b_i"][0], 10).astype(f)
    m["d_lam"] = _col(inp["d_lam"][0], 10).astype(f)
    m["d_wout"] = _kc(inp["d_w_out"][0])
    return m


_NC_CACHE = {}


def run(inputs, nlayers=4, trace=False, nsub=None):
    inp = {k: np.asarray(v) for k, v in inputs.items()}
    if (nlayers, nsub) not in _NC_CACHE:
        _NC_CACHE[(nlayers, nsub)] = build(nlayers, nsub)
    nc = _NC_CACHE[(nlayers, nsub)]
    shared = prep_shared(inp)
    shared = {k: v for k, v in shared.items() if k in nc._in_names}
    x = inp["x"]
    in_maps = []
    for cidx in range(NCORES):
        xs = x[cidx * NSEQ:(cidx + 1) * NSEQ].reshape(TC, D)
        mcore = dict(shared)
        mcore["xT"] = np.ascontiguousarray(xs.T)
        in_maps.append(mcore)
    res = run_bass_kernel_spmd(nc, in_maps, core_ids=list(range(NCORES)), **({"trace": True} if trace else {}))
    out = np.empty((16, S, D), np.float32)
    for cidx in range(NCORES):
        out[cidx * NSEQ:(cidx + 1) * NSEQ] = res.results[cidx]["outT"].T.reshape(NSEQ, S, D)
    return out, res


def kernel(**inputs):
    out, _ = run(inputs, 4)
    return out
```
